# Optimizing a Trainium2 kernel written in Bass

```python
import jax, jax.numpy as jnp
from jax import lax
import numpy as np

D_MODEL = 1024
BATCH = 1
SEQ = 16384
DEPTH = 1
DEC_BATCH = 16
DEC_SEQ = 2048
PAST_LEN = 128

HEAD_DIM = 64
RWKV_WIDTH = D_MODEL // 2
RWKV_HEADS = RWKV_WIDTH // HEAD_DIM
NA_WIDTH = D_MODEL // 2
NA_HEADS = NA_WIDTH // HEAD_DIM
DECAY_LORA = 64
AAA_LORA = 64
GATE_LORA = 128
GRID_W = 64
NA_ROWS = 8
NA_COLS = 16
D_FF = -(-8 * D_MODEL // (3 * 256)) * 256
PLE_DIM = 256
NORM_EPS = 1e-6
GN_EPS = 64e-5
RWKV_SIZES = (RWKV_WIDTH, RWKV_WIDTH, RWKV_WIDTH, DECAY_LORA, DECAY_LORA, AAA_LORA, AAA_LORA, GATE_LORA)
RWKV_SPLITS = tuple(int(s) for s in np.cumsum(RWKV_SIZES)[:-1])
RWKV_IN = sum(RWKV_SIZES)
NA_IN = 3 * NA_WIDTH
GATE_IN = 2 * D_MODEL
D_IN = RWKV_IN + NA_IN + GATE_IN

kernel_name = 'hybrid_rwkv7_natten_encoder'


def rmsnorm(x, g):
    xf = x.astype(jnp.float32)
    y = xf * lax.rsqrt(jnp.mean(xf * xf, axis=-1, keepdims=True) + NORM_EPS)
    return (y * g.astype(jnp.float32)).astype(x.dtype)


def centred_shift(z, mu_prev, mu_next):
    zp = jnp.pad(z[:, :-1], ((0, 0), (1, 0), (0, 0)))
    zn = jnp.pad(z[:, 1:], ((0, 0), (0, 1), (0, 0)))
    return z + mu_prev * (zp - z) + mu_next * (zn - z)


def wkv7_scan(r, w, k, v, a, b, reverse):
    B, T, H, N = r.shape
    xs = tuple(jnp.moveaxis(t, 1, 0) for t in (r, w, k, v, a, b))

    def step(S, inp):
        r_t, w_t, k_t, v_t, a_t, b_t = inp
        S_new = (S * w_t[:, :, None, :]
                 + jnp.einsum('bhvk,bhk->bhv', S, a_t)[..., None] * b_t[:, :, None, :]
                 + v_t[..., None] * k_t[:, :, None, :])
        y = jnp.einsum('bhvk,bhk->bhv', S if reverse else S_new, r_t)
        return S_new, y

    S0 = jnp.zeros((B, H, N, N), jnp.float32)
    _, ys = lax.scan(step, S0, xs, reverse=reverse)
    return jnp.moveaxis(ys, 0, 1)


def rwkv7_branch(z, L):
    B, T, _ = z.shape
    f32 = jnp.float32
    z = centred_shift(z, L['mu_prev'], L['mu_next'])
    r, k, v, wd_f, wd_b, ad_f, ad_b, gd = jnp.split(z, RWKV_SPLITS, axis=-1)
    r, k, v = r.astype(f32), k.astype(f32), v.astype(f32)

    def decay(wd, w0, w2):
        w = -jax.nn.softplus(-(w0.astype(f32) + (jnp.tanh(wd) @ w2).astype(f32))) - 0.5
        return jnp.exp(-jnp.exp(w))

    def rate(ad, a0, a2):
        return jax.nn.sigmoid(a0.astype(f32) + (ad @ a2).astype(f32))

    w_f = decay(wd_f, L['w0_f'], L['w2_f'])
    w_b = decay(wd_b, L['w0_b'], L['w2_b'])
    a_f = rate(ad_f, L['a0_f'], L['a2_f'])
    a_b = rate(ad_b, L['a0_b'], L['a2_b'])
    g = (jax.nn.sigmoid(gd) @ L['g2']).astype(f32)
    k_a = L['k_a'].astype(f32)
    k_f = k * (1.0 + (a_f - 1.0) * k_a)
    k_b = k * (1.0 + (a_b - 1.0) * k_a)

    hd = lambda t: t.reshape(B, T, RWKV_HEADS, HEAD_DIM)
    kk = hd(k * L['k_k'].astype(f32))
    kk = kk / jnp.maximum(jnp.sqrt(jnp.sum(kk * kk, axis=-1, keepdims=True)), 1e-12)
    rh, vh = hd(r), hd(v)
    o = (wkv7_scan(rh, hd(w_f), hd(k_f), vh, -kk, kk * hd(a_f), False)
         + wkv7_scan(rh, hd(w_b), hd(k_b), vh, -kk, kk * hd(a_b), True))
    mu = jnp.mean(o, axis=-1, keepdims=True)
    var = jnp.mean(jnp.square(o - mu), axis=-1, keepdims=True)
    o = (o - mu) * lax.rsqrt(var + GN_EPS)
    o = o.reshape(B, T, RWKV_WIDTH) * L['lnx_w'].astype(f32) + L['lnx_b'].astype(f32)
    bonus = jnp.sum(rh * hd(0.5 * (k_f + k_b)) * L['r_k'].astype(f32), axis=-1, keepdims=True) * vh
    o = o + bonus.reshape(B, T, RWKV_WIDTH)
    return (o * g).astype(z.dtype)


def neighbourhood_attention(q, k, v, rpb):
    B, T, _ = q.shape
    rows = T // GRID_W
    kh = min(NA_ROWS, rows)
    shp = (B, rows, GRID_W, NA_HEADS, HEAD_DIM)
    q, k, v = q.reshape(shp), k.reshape(shp), v.reshape(shp)
    row_start = jnp.clip(jnp.arange(rows) - kh // 2, 0, rows - kh)
    cols = jnp.arange(GRID_W)
    col_idx = (jnp.clip(cols - NA_COLS // 2, 0, GRID_W - NA_COLS)[:, None]
               + jnp.arange(NA_COLS)[None, :])
    dc_idx = col_idx - cols[:, None] + (NA_COLS - 1)
    col_bias = rpb.astype(jnp.float32)[:, :, dc_idx]
    scale = HEAD_DIM ** -0.5

    def row_block(r):
        rs = row_start[r]
        kr = lax.dynamic_slice_in_dim(k, rs, kh, axis=1)[:, :, col_idx]
        vr = lax.dynamic_slice_in_dim(v, rs, kh, axis=1)[:, :, col_idx]
        qr = lax.dynamic_index_in_dim(q, r, axis=1, keepdims=False)
        s = jnp.einsum('bwhd,biwjhd->bhwij', qr, kr).astype(jnp.float32) * scale
        dr = rs + jnp.arange(kh) - r + (NA_ROWS - 1)
        bias = jnp.take(col_bias, dr, axis=1)
        s = s + jnp.transpose(bias, (0, 2, 1, 3))[None]
        p = jax.nn.softmax(s.reshape(B, NA_HEADS, GRID_W, kh * NA_COLS), axis=-1)
        p = p.reshape(s.shape).astype(v.dtype)
        return jnp.einsum('bhwij,biwjhd->bwhd', p, vr)

    out = lax.map(row_block, jnp.arange(rows))
    return jnp.moveaxis(out, 0, 1).reshape(B, T, NA_WIDTH)


def encoder_layer(x, p, L):
    h = rmsnorm(x, L['g_mix'])
    z = h @ L['w_in']
    z_rwkv = z[..., :RWKV_IN]
    z_na = z[..., RWKV_IN:RWKV_IN + NA_IN]
    z_gate = z[..., RWKV_IN + NA_IN:]
    u_a = rwkv7_branch(z_rwkv, L)
    q, k, v = jnp.split(z_na, 3, axis=-1)
    u_n = neighbourhood_attention(q, k, v, L['rpb'])
    gate_a, gate_n = jnp.split(jax.nn.sigmoid(z_gate), 2, axis=-1)
    m = gate_a * (u_a @ L['w_br_a']) + gate_n * (u_n @ L['w_br_n'])
    x = x + m @ L['w_out']
    h = rmsnorm(x, L['g_ffn'])
    x = x + (jax.nn.silu(h @ L['w_gate']) * (h @ L['w_up'])) @ L['w_down']
    x = x + (p @ L['w_ple']) * jax.nn.sigmoid(rmsnorm(x, L['g_ple']) @ L['w_pg'])
    return x


def trunk(x, p, layers, g_final):
    for i in range(DEPTH):
        L = {name: arr[i] for name, arr in layers.items()}
        x = encoder_layer(x, p[i], L)
    return rmsnorm(x, g_final)


def setup_inputs(seed: int = 0) -> dict:
    key = jax.random.key(seed)
    ks = iter(jax.random.split(key, 40))

    def nrm(shape, scale):
        return scale * jax.random.normal(next(ks), shape, jnp.float32)

    def gain(shape):
        return 1.0 + nrm(shape, 0.02)

    def unif(shape, lo, hi):
        return jax.random.uniform(next(ks), shape, jnp.float32, lo, hi)

    return {
        'x_prompt': nrm((BATCH, SEQ, D_MODEL), 1.0),
        'x_sample': nrm((DEC_BATCH, DEC_SEQ, D_MODEL), 1.0),
        'p_prompt': nrm((DEPTH, BATCH, SEQ, PLE_DIM), 1.0),
        'p_sample': nrm((DEPTH, DEC_BATCH, DEC_SEQ, PLE_DIM), 1.0),
        'g_mix': gain((DEPTH, D_MODEL)),
        'w_in': nrm((DEPTH, D_MODEL, D_IN), D_MODEL ** -0.5),
        'mu_prev': unif((DEPTH, RWKV_IN), 0.0, 0.5),
        'mu_next': unif((DEPTH, RWKV_IN), 0.0, 0.5),
        'w0_f': unif((DEPTH, RWKV_WIDTH), -5.0, -1.0),
        'w2_f': nrm((DEPTH, DECAY_LORA, RWKV_WIDTH), 0.3 * DECAY_LORA ** -0.5),
        'w0_b': unif((DEPTH, RWKV_WIDTH), -5.0, -1.0),
        'w2_b': nrm((DEPTH, DECAY_LORA, RWKV_WIDTH), 0.3 * DECAY_LORA ** -0.5),
        'a0_f': nrm((DEPTH, RWKV_WIDTH), 0.1),
        'a2_f': nrm((DEPTH, AAA_LORA, RWKV_WIDTH), 0.5 * AAA_LORA ** -0.5),
        'a0_b': nrm((DEPTH, RWKV_WIDTH), 0.1),
        'a2_b': nrm((DEPTH, AAA_LORA, RWKV_WIDTH), 0.5 * AAA_LORA ** -0.5),
        'g2': nrm((DEPTH, GATE_LORA, RWKV_WIDTH), GATE_LORA ** -0.5),
        'k_k': 0.85 + nrm((DEPTH, RWKV_WIDTH), 0.02),
        'k_a': gain((DEPTH, RWKV_WIDTH)),
        'r_k': nrm((DEPTH, RWKV_HEADS, HEAD_DIM), 0.1),
        'lnx_w': gain((DEPTH, RWKV_WIDTH)),
        'lnx_b': nrm((DEPTH, RWKV_WIDTH), 0.01),
        'rpb': nrm((DEPTH, NA_HEADS, 2 * NA_ROWS - 1, 2 * NA_COLS - 1), 0.1),
        'w_br_a': nrm((DEPTH, RWKV_WIDTH, D_MODEL), RWKV_WIDTH ** -0.5),
        'w_br_n': nrm((DEPTH, NA_WIDTH, D_MODEL), NA_WIDTH ** -0.5),
        'w_out': nrm((DEPTH, D_MODEL, D_MODEL), D_MODEL ** -0.5),
        'g_ffn': gain((DEPTH, D_MODEL)),
        'w_gate': nrm((DEPTH, D_MODEL, D_FF), D_MODEL ** -0.5),
        'w_up': nrm((DEPTH, D_MODEL, D_FF), D_MODEL ** -0.5),
        'w_down': nrm((DEPTH, D_FF, D_MODEL), D_FF ** -0.5),
        'g_ple': gain((DEPTH, D_MODEL)),
        'w_ple': nrm((DEPTH, PLE_DIM, D_MODEL), PLE_DIM ** -0.5),
        'w_pg': nrm((DEPTH, D_MODEL, D_MODEL), D_MODEL ** -0.5),
        'g_final': gain((D_MODEL,)),
    }


def reference(x_prompt, x_sample, p_prompt, p_sample, g_mix, w_in, mu_prev, mu_next,
              w0_f, w2_f, w0_b, w2_b, a0_f, a2_f, a0_b, a2_b, g2, k_k, k_a, r_k,
              lnx_w, lnx_b, rpb, w_br_a, w_br_n, w_out, g_ffn, w_gate, w_up, w_down,
              g_ple, w_ple, w_pg, g_final):
    layers = {
        'g_mix': g_mix, 'w_in': w_in, 'mu_prev': mu_prev, 'mu_next': mu_next,
        'w0_f': w0_f, 'w2_f': w2_f, 'w0_b': w0_b, 'w2_b': w2_b,
        'a0_f': a0_f, 'a2_f': a2_f, 'a0_b': a0_b, 'a2_b': a2_b,
        'g2': g2, 'k_k': k_k, 'k_a': k_a, 'r_k': r_k, 'lnx_w': lnx_w, 'lnx_b': lnx_b,
        'rpb': rpb, 'w_br_a': w_br_a, 'w_br_n': w_br_n, 'w_out': w_out,
        'g_ffn': g_ffn, 'w_gate': w_gate, 'w_up': w_up, 'w_down': w_down,
        'g_ple': g_ple, 'w_ple': w_ple, 'w_pg': w_pg,
    }
    y_prompt = trunk(x_prompt, p_prompt, layers, g_final)
    y_sample = trunk(x_sample, p_sample, layers, g_final)
    return (y_prompt, y_sample)
```

```python
from contextlib import ExitStack
import numpy as np
import concourse.bass as bass
import concourse.mybir as mybir
from concourse.bass_utils import run_bass_kernel_spmd

F32 = mybir.dt.float32
BF = mybir.dt.bfloat16
AF = mybir.ActivationFunctionType
ALU = mybir.AluOpType
AX = mybir.AxisListType

NCORE = 8
D = 1024
SEQT = 2048
HALO = 256
EXT = SEQT + 2 * HALO
TT = 512
NTILE = SEQT // TT
DIN = 5504
DFF = 2816
KAPPA = float(np.exp(-0.5))
NEG = -30000.0
WITH_XCORE = True
SLOT_EXT = SEQT + 256

CST_SPEC = [
    ("ident", 128), ("bones", 128),
    ("colmask", 64), ("narm", 576), ("ones", 128),
    ("gmix", 8), ("gffn", 8), ("gple", 8), ("gfin", 8), ("mp", 15), ("mn", 15),
    ("w0f", 4), ("w0b", 4), ("a0f", 4), ("a0b", 4), ("kk", 4), ("ka", 4), ("rk", 4),
    ("lnw", 256), ("lnb", 256), ("first", 1),
]
CST_OFF = {}
_o = 0
for _n, _w in CST_SPEC:
    CST_OFF[_n] = (_o, _w)
    _o += _w
NCST = _o


def _pp(v, nch):
    return np.ascontiguousarray(np.asarray(v, np.float32).reshape(nch, 128).T)


def _masks():
    p = np.arange(128)[:, None]
    f = np.arange(128)[None, :]
    same = (p // 64) == (f // 64)
    a = p % 64
    b = f % 64
    out = {}
    up = same & (a < b)
    lo = same & (a > b)
    upi = same & (a <= b)
    out["mXf"] = np.concatenate([up, lo, up], 1).astype(np.float32)
    out["mYf"] = np.concatenate([upi, upi], 1).astype(np.float32)
    out["mXb"] = np.concatenate([lo, up, lo], 1).astype(np.float32)
    out["mYb"] = np.concatenate([lo, lo], 1).astype(np.float32)
    return out


def _narm(core):
    out = np.zeros((128, 3, 32, 6), np.float32)
    for s in range(3):
        fc = (s > 0) or (core == 0)
        lc = (s > 0) or (core == NCORE - 1)
        for r in range(32):
            if r < 4 and fc:
                lo, hi = 4, 11
            elif r >= 28 and lc:
                lo, hi = 28, 35
            else:
                lo, hi = r, r + 7
            m0 = min(max(r // 2, 0), 14)
            for j in range(6):
                m = m0 + j
                for half in range(2):
                    e = 2 * m + half
                    if not (lo <= e <= hi):
                        out[64 * half:64 * half + 64, s, r, j] = NEG
    return out.reshape(128, 576)


def _build_cst(inp, core):
    c = np.zeros((128, NCST), np.float32)

    def put(name, arr):
        o, w = CST_OFF[name]
        c[:, o:o + w] = np.asarray(arr, np.float32).reshape(128, w)

    put("ident", np.eye(128))
    p = np.arange(128)
    put("bones", (p[:, None] // 64 == p[None, :] // 64))
    kc = np.arange(64)[:, None]
    cc = np.arange(64)[None, :]
    cs = np.clip(cc - 8, 0, 48)
    cm = ((kc >= cs) & (kc <= cs + 15)).astype(np.float32)
    put("colmask", np.concatenate([cm, cm], 0))
    put("narm", _narm(core))
    put("ones", np.ones((128, 128)))
    put("gmix", _pp(inp["g_mix"][0], 8))
    put("gffn", _pp(inp["g_ffn"][0], 8))
    put("gple", _pp(inp["g_ple"][0], 8))
    put("gfin", _pp(inp["g_final"], 8))
    put("mp", _pp(inp["mu_prev"][0], 15))
    put("mn", _pp(inp["mu_next"][0], 15))
    for nm, key in (("w0f", "w0_f"), ("w0b", "w0_b"), ("a0f", "a0_f"), ("a0b", "a0_b"),
                    ("kk", "k_k"), ("ka", "k_a")):
        put(nm, _pp(inp[key][0], 4))
    put("rk", _pp(inp["r_k"][0].reshape(-1), 4))
    for nm, key in (("lnw", "lnx_w"), ("lnb", "lnx_b")):
        v = np.asarray(inp[key][0], np.float32).reshape(4, 2, 64)
        a = np.transpose(v, (1, 0, 2))
        a = np.repeat(a[:, None], 64, axis=1).reshape(128, 256)
        put(nm, a)
    put("first", np.full((128, 1), 1.0 if core == 0 else 0.0))
    return c


def _build_nab(rpb):
    rpb = np.asarray(rpb, np.float32)
    kc = np.arange(64)[:, None]
    cc = np.arange(64)[None, :]
    idx = np.clip(kc - cc + 15, 0, 30)
    g = rpb[:, :, idx]
    out = np.zeros((2, 64, 8, 14, 64), np.float32)
    for half in range(2):
        out[half] = np.transpose(g[:, half:half + 14], (2, 0, 1, 3))
    return np.ascontiguousarray(out.reshape(128, 8 * 14 * 64))


class Ctx:
    NDS = 24

    def __init__(self, nc):
        self.nc = nc
        self.eng = {"pe": nc.tensor, "dve": nc.vector, "act": nc.scalar, "pool": nc.gpsimd,
                    "sp": nc.sync}
        self.sem = {e: nc.alloc_semaphore(name=f"pg_{e}_0") for e in self.eng}
        self.epoch = 0
        self.cnt = {e: 0 for e in self.eng}
        self.seen = {e: {} for e in self.eng}
        self.lastw = {}
        self.readers = {}
        self.dsems = [nc.alloc_semaphore(name=f"dq{i}") for i in range(self.NDS)]
        self.dcnt = [0] * self.NDS
        self.dnext = 0
        self.nins = 0
        self.trace = {e: [] for e in self.eng}

    def _semof(self, src):
        return self.sem[src[1]] if src[0] == "e" else self.dsems[src[1]]

    def _wait(self, e, src, val):
        if val <= 0:
            return
        if self.seen[e].get(src, 0) >= val:
            return
        self.eng[e].wait_ge(self._semof(src), val)
        self.trace[e].append(("w", (src, self.epoch if src[0] == "e" else 0), val))
        self.seen[e][src] = val

    def _deps(self, R, W):
        deps = {}

        def add(st):
            if st is None:
                return
            s, v = st
            if deps.get(s, 0) < v:
                deps[s] = v
        for k in R:
            add(self.lastw.get(k))
        for k in W:
            add(self.lastw.get(k))
            for s, v in self.readers.get(k, {}).items():
                add((s, v))
        return deps

    def _upd(self, R, W, stamp):
        for k in W:
            self.lastw[k] = stamp
            self.readers[k] = {}
        for k in R:
            d = self.readers.setdefault(k, {})
            if d.get(stamp[0], 0) < stamp[1]:
                d[stamp[0]] = stamp[1]

    def op(self, e, fn, R=(), W=(), inc=True):
        for s, v in self._deps(R, W).items():
            if e == "pe" and s == ("e", "pe"):
                continue
            self._wait(e, s, v)
        ins = fn(self.eng[e])
        self.nins += 1
        if inc:
            self.cnt[e] += 1
            ins.then_inc(self.sem[e], 1)
            self.trace[e].append(("i", ((("e", e)), self.epoch), 1))
            stamp = (("e", e), self.cnt[e])
        else:
            stamp = (("e", e), self.cnt[e] + 1)
        self._upd(R, W, stamp)
        return ins

    def dma(self, q, out, in_, R=(), W=()):
        i = self.dnext
        self.dnext = (self.dnext + 1) % self.NDS
        self._wait(q, ("d", i), self.dcnt[i])
        for s, v in self._deps(R, W).items():
            self._wait(q, s, v)
        self.eng[q].dma_start(out=out, in_=in_).then_inc(self.dsems[i], 16)
        self.trace[q].append(("i", (("d", i), 0), 16))
        self.nins += 1
        self.dcnt[i] += 16
        self._upd(R, W, (("d", i), self.dcnt[i]))

    def barrier(self):
        for e in self.eng:
            for e2 in self.eng:
                if e2 != e:
                    self._wait(e, ("e", e2), self.cnt[e2])
            for i in range(self.NDS):
                self._wait(e, ("d", i), self.dcnt[i])
        self.lastw = {}
        self.readers = {}

    def maybe_rotate(self, limit=24000):
        if max(self.cnt.values()) < limit:
            return
        self.barrier()
        self.epoch += 1
        for e in self.eng:
            self.sem[e] = self.nc.alloc_semaphore(name=f"pg_{e}_{self.epoch}")
            self.cnt[e] = 0
        for e in self.eng:
            for e2 in self.eng:
                self.seen[e].pop(("e", e2), None)


class _Stop(Exception):
    pass


def build_program(debug=None):
    nc = bass.Bass("TRN2", target_bir_lowering=False)
    dt = lambda n, s: nc.dram_tensor(n, s, F32, kind="ExternalInput").ap()
    xs = dt("xs", [3 * EXT, D])
    pp = dt("pp", [3 * SEQT, 256])
    cst_d = dt("cst", [128, NCST])
    nab_d = dt("nab", [128, 8 * 14 * 64])
    msk_d = dt("msk", [128, 1280])
    w_in = dt("w_in", [D, DIN])
    w_lora = dt("w_lora", [128, 2 * 512])
    g2_d = dt("g2", [128, 512])
    w_bra = dt("w_br_a", [512, D])
    w_brn = dt("w_br_n", [512, D])
    w_out = dt("w_out", [D, D])
    w_gate = dt("w_gate", [D, DFF])
    w_up = dt("w_up", [D, DFF])
    w_down = dt("w_down", [DFF, D])
    w_ple = dt("w_ple", [256, D])
    w_pg = dt("w_pg", [D, D])
    xo = dt("xo", [7 * SLOT_EXT, D])
    slp_d = dt("slp", [128, 7 * 40 + 8])
    y_d = nc.dram_tensor("y", [3 * SEQT, D], F32, kind="ExternalOutput").ap()

    C = Ctx(nc)
    ES = ExitStack()
    dbg_d = None
    if debug:
        dbg_d = nc.dram_tensor("dbg", [128, debug.get("n", 8 * EXT)], BF if debug.get("bf", True) else F32, kind="ExternalOutput").ap()

    def dbg_dump(tag, ap2d):
        if debug and debug.get("stop") == tag:
            C.barrier()
            C.dma("sp", dbg_d[:, 0:ap2d.shape[1]], ap2d, R=[], W=[])
            C.barrier()
            ex = _Stop()
            ex.nc = nc
            raise ex

    uid = {"n": 0}

    def sb(name, shape, dtype=F32, es=ES):
        uid["n"] += 1
        return es.enter_context(nc.sbuf_tensor(f"{name}_u{uid['n']}", shape, dtype))

    class PSV:
        def __init__(self, name, ap):
            self.name = name
            self.ap = ap

        def __getitem__(self, idx):
            return self.ap[idx]

    class PSD:
        def __init__(self, name):
            self.t = nc.alloc_psum_tensor(name, [128, 1024], F32)
            self.b = [PSV(f"{name}_b{h}", self.t[:, h * 512:(h + 1) * 512]) for h in range(2)]

    QA, QB, QX = PSD("qa"), PSD("qb"), PSD("qx")
    PA = QA.b + QB.b
    PX = QX.b
    PS6 = PSV("ps6", nc.alloc_psum_tensor("ps6t", [128, 512], F32)[:, :])
    PT = [PSV("pt0", nc.alloc_psum_tensor("pt0t", [128, 1024], BF)[:, :])]
    for t in PA + PX + [PS6]:
        C.op("dve", lambda e, t=t: e.memset(t[:], 0.0), W=[("ps", t.name)])

    def dk(Q):
        return [("ps", Q.b[0].name), ("ps", Q.b[1].name)]

    def pk(t, sub=None):
        return ("ps", t.name) if sub is None else ("ps", t.name, sub)

    cs = sb("cs", [128, NCST])
    C.dma("sp", cs[:], cst_d, W=["cs"])

    def cv(name, a=0, b=None):
        o, w = CST_OFF[name]
        return cs[:, o + a:o + (w if b is None else b)]

    identb = sb("identb", [128, 128], BF)
    bonesb = sb("bonesb", [128, 128], BF)
    onesb = sb("onesb", [128, 128], BF)
    C.op("dve", lambda e: e.tensor_copy(out=identb[:], in_=cv("ident")), R=["cs"], W=["identb"])
    C.op("dve", lambda e: e.tensor_copy(out=bonesb[:], in_=cv("bones")), R=["cs"], W=["bonesb"])
    C.op("dve", lambda e: e.tensor_copy(out=onesb[:], in_=cv("ones")), R=["cs"], W=["onesb"])
    dv = sb("dv", [128, 64])
    c0 = dv[:, 0:15]
    omka = dv[:, 16:20]
    hrk = dv[:, 20:24]
    C.op("dve", lambda e: e.tensor_tensor(out=c0, in0=cv("mp"), in1=cv("mn"), op=ALU.add), R=["cs"], W=["dv0"])
    C.op("dve", lambda e: e.tensor_scalar(out=c0, in0=c0, scalar1=-1.0, scalar2=1.0, op0=ALU.mult, op1=ALU.add), R=["dv0"], W=["dv0"])
    C.op("dve", lambda e: e.tensor_scalar(out=omka, in0=cv("ka"), scalar1=-1.0, scalar2=1.0, op0=ALU.mult, op1=ALU.add), R=["cs"], W=["dv1"])
    C.op("dve", lambda e: e.tensor_scalar(out=hrk, in0=cv("rk"), scalar1=0.5, scalar2=None, op0=ALU.mult), R=["cs"], W=["dv2"])
    DVK = ["dv0", "dv1", "dv2"]
    epsc = sb("epsc", [128, 4])
    C.op("dve", lambda e: e.memset(epsc[:, 0:1], 1e-6), W=["epsc"])
    C.op("dve", lambda e: e.memset(epsc[:, 1:2], 1e-24), W=["epsc"])
    C.op("dve", lambda e: e.memset(epsc[:, 2:3], 64e-5), W=["epsc"])
    C.op("dve", lambda e: e.memset(epsc[:, 3:4], 0.0), W=["epsc"])

    mskb = sb("mskb", [128, 1280], BF)
    C.dma("pool", mskb[:], msk_d, W=["mskb"])
    MSK = {"mXf": mskb[:, 0:384], "mYf": mskb[:, 384:640], "mXb": mskb[:, 640:1024], "mYb": mskb[:, 1024:1280]}
    lorab = sb("lorab", [128, 1024], BF)
    g2b = sb("g2b", [128, 512], BF)
    C.dma("pool", lorab[:], w_lora, W=["lorab"])
    C.dma("pool", g2b[:], g2_d, W=["g2b"])

    def build_epb(es0):
        epb = sb("epb", [128, 8 * 14 * 64], BF, es0)
        stg = sb("nabstg", [128, 1792], F32, es0)
        for q in range(4):
            C.dma("sp", stg[:], nab_d[:, q * 1792:(q + 1) * 1792], W=["stg"])
            C.op("act", lambda e: e.activation(out=stg[:], in_=stg[:], func=AF.Exp), R=["stg"], W=["stg"])
            C.op("dve", lambda e, q=q: e.tensor_tensor(
                out=epb[:, q * 1792:(q + 1) * 1792].rearrange("p (g c) -> p g c", c=64),
                in0=stg[:].rearrange("p (g c) -> p g c", c=64),
                in1=cv("colmask").unsqueeze(1).to_broadcast([128, 28, 64]), op=ALU.mult),
                R=["stg", "cs"], W=["epb"])
        return epb[:].rearrange("p (h d c) -> p h d c", h=8, d=14)

    dbg_dump("p0", lorab[:])
    hT = sb("hT", [128, 8, EXT], BF)
    UnT = sb("UnT", [128, 4, SEQT], BF)
    UaT = sb("UaT", [128, 4, SEQT], BF)
    wbuf = [None, None]

    def alloc_wbuf(es, tag, width):
        for i in range(2):
            wbuf[i] = sb(f"wbuf_{tag}_{i}", [128, 8, width], BF, es)
    wstate = {"i": 0}

    def load_w(src, r0, c0_, ncols, nk=8):
        i = wstate["i"]
        wstate["i"] = 1 - i
        t = wbuf[i]
        v = src[r0:r0 + nk * 128, c0_:c0_ + ncols].rearrange("(k p) c -> p k c", p=128)
        C.dma("pool", t[:, 0:nk, 0:ncols], v, W=[f"wbuf{i}"])
        return t, f"wbuf{i}"

    evac_rr = {"i": 0}

    def evac(out, in_, R, W, scale=None):
        evac_rr["i"] ^= 1
        if evac_rr["i"]:
            C.op("act", lambda e: e.activation(out=out, in_=in_, func=AF.Copy), R=R, W=W)
        else:
            C.op("dve", lambda e: e.tensor_copy(out=out, in_=in_), R=R, W=W)

    def mm(out, lhsT, rhs, start, stop, R, W, inc=False):
        return C.op("pe", lambda e: e.matmul(out, lhsT=lhsT, rhs=rhs, start=start, stop=stop,
                                             skip_group_check=True), R=R, W=W, inc=inc)

    def tp(out, in_, R, W, inc=False):
        b0 = in_.base_partition()
        n0 = in_.shape[0]
        return C.op("pe", lambda e: e.transpose(out=out, in_=in_, identity=identb[b0:b0 + n0, b0:b0 + n0]),
                    R=list(R) + ["identb"], W=W, inc=inc)

    st_ = sb("p1st", [128, 4], F32)
    Ssave = sb("Ssave", [128, 2, 4, 64], F32)
    slp = sb("slp", [128, 7 * 40 + 8], F32)
    C.dma("sp", slp[:], slp_d, W=["slp"])

    def fill_hT(src, row0, blks, xt_t, xb_t, kx, kb):
        for n_, blk in enumerate(blks):
            i = n_ % len(xt_t)
            xt_a, xb_a = xt_t[i], xb_t[i]
            kxi, kbi = kx[i], kb[i]
            C.dma("sp", xt_a, src[row0 + n_ * 128: row0 + (n_ + 1) * 128, :], W=[kxi])
            C.op("act", lambda e, xt_a=xt_a, xb_a=xb_a, i=i: e.activation(out=xb_a, in_=xt_a, func=AF.Square,
                                                                  accum_out=st_[:, 2 * i:2 * i + 1]),
                 R=[kxi], W=[kbi, f"p1st{i}"])
            C.op("act", lambda e, i=i: e.activation(out=st_[:, 2 * i + 1:2 * i + 2], in_=st_[:, 2 * i:2 * i + 1], func=AF.Sqrt,
                                                    bias=epsc[:, 0:1], scale=1.0 / D),
                 R=[f"p1st{i}", "epsc"], W=[f"p1st{i}b"])
            C.op("dve", lambda e, i=i: e.reciprocal(out=st_[:, 2 * i + 1:2 * i + 2], in_=st_[:, 2 * i + 1:2 * i + 2]),
                 R=[f"p1st{i}b"], W=[f"p1st{i}b"])
            C.op("dve", lambda e, xt_a=xt_a, xb_a=xb_a, i=i: e.tensor_scalar(out=xb_a, in0=xt_a, scalar1=st_[:, 2 * i + 1:2 * i + 2],
                                                                     scalar2=None, op0=ALU.mult),
                 R=[kxi, f"p1st{i}b"], W=[kbi])
            pt = PT[0]
            for kc in range(8):
                tp(pt[:, kc * 128:(kc + 1) * 128], xb_a[:, kc * 128:(kc + 1) * 128],
                   R=[kbi], W=[pk(pt)], inc=(kc == 7))
            C.op("dve", lambda e, pt=pt, blk=blk: e.tensor_tensor(
                out=hT[:, :, blk * 128:(blk + 1) * 128],
                in0=pt[:].rearrange("p (k t) -> p k t", k=8),
                in1=cv("gmix").unsqueeze(2).to_broadcast([128, 8, 128]), op=ALU.mult),
                R=[pk(pt), "cs"], W=[("hT", blk // 4)])

    try:
      for s in (debug["seqs"] if debug and "seqs" in debug else range(3)):
          xoff = s * EXT
          with ExitStack() as es1:
              xt = [sb(f"p1x{i}", [128, D], F32, es1) for i in range(2)]
              xb = [sb(f"p1xb{i}", [128, D], BF, es1) for i in range(2)]
              fill_hT(xs, xoff, list(range(EXT // 128)), [t[:] for t in xt], [t[:] for t in xb],
                      ["p1x0", "p1x1"], ["p1xb0", "p1xb1"])
              dbg_dump("p1", hT[:].rearrange("p k t -> p (k t)"))
              C.barrier()

          with ExitStack() as es2:
              alloc_wbuf(es2, f"p2s{s}", 512)
              epb4 = build_epb(es2)
              qT = sb("qT", [128, 4, SEQT], BF, es2)
              kT = sb("kT", [128, 4, EXT], BF, es2)
              Vt = sb("Vt", [128, EXT // 128, 512], BF, es2)
              Eb = [sb(f"Eb{j}", [128, 512], BF, es2) for j in range(6)]
              Pb = [sb(f"Pb{j}", [128, 512], BF, es2) for j in range(6)]
              rinv = sb("rinv", [128, 256], F32, es2)
              wt, wk = load_w(w_in, 0, 1920, 512)
              ai = 0
              for t5 in range(NTILE):
                  for cc in range(4):
                      pa = PA[ai % 4]; ai += 1
                      for kc in range(8):
                          mm(pa[:], wt[:, kc, cc * 128:(cc + 1) * 128], hT[:, kc, HALO + t5 * TT: HALO + (t5 + 1) * TT],
                             kc == 0, kc == 7, R=[wk, ("hT", (HALO + t5 * TT) // 512), ("hT", (HALO + t5 * TT) // 512 + 1)],
                             W=[pk(pa)], inc=(kc == 7))
                      evac(qT[:, cc, t5 * TT:(t5 + 1) * TT], pa[:], R=[pk(pa)], W=[("qT", t5)])
              wt, wk = load_w(w_in, 0, 2432, 512)
              for t5 in range(EXT // TT):
                  for cc in range(4):
                      pa = PA[ai % 4]; ai += 1
                      for kc in range(8):
                          mm(pa[:], wt[:, kc, cc * 128:(cc + 1) * 128], hT[:, kc, t5 * TT:(t5 + 1) * TT],
                             kc == 0, kc == 7, R=[wk, ("hT", t5)], W=[pk(pa)], inc=(kc == 7))
                      evac(kT[:, cc, t5 * TT:(t5 + 1) * TT], pa[:], R=[pk(pa)], W=[("kT", t5)])
              wt, wk = load_w(w_in, 0, 2944, 512)
              for blk in range(EXT // 128):
                  pa = PA[ai % 4]; ai += 1
                  for kc in range(8):
                      mm(pa[:], hT[:, kc, blk * 128:(blk + 1) * 128], wt[:, kc, :], kc == 0, kc == 7,
                         R=[wk, ("hT", blk // 4)], W=[pk(pa)], inc=(kc == 7))
                  evac(Vt[:, blk, :], pa[:], R=[pk(pa)], W=[("Vt", blk)])
              for r in range(32):
                  m0 = min(max(r // 2, 0), 14)
                  q0 = r * 64
                  for j in range(6):
                      m = m0 + j
                      d = 2 * m - r + 3
                      Q = (QA, QB)[j % 2]
                      for h in range(8):
                          b = 64 * (h % 2)
                          o = (h % 2) * 512 + (h // 2) * 64
                          mm(Q.t[:, o:o + 64], kT[b:b + 64, h // 2, m * 128:(m + 1) * 128],
                             qT[b:b + 64, h // 2, q0:q0 + 64], True, True,
                             R=[("kT", m // 4), ("qT", r // 8)], W=dk(Q), inc=(h == 7))
                      o, _ = CST_OFF["narm"]
                      col = o + (s * 32 + r) * 6 + j
                      C.op("act", lambda e, Q=Q, j=j, col=col: e.activation(
                          out=Eb[j][:].rearrange("p (b c) -> p b c", b=2),
                          in_=Q.t[:].rearrange("p (b c) -> p b c", b=2)[:, :, 0:256],
                          func=AF.Exp, bias=cs[:, col:col + 1], scale=0.125),
                          R=dk(Q) + ["cs"], W=[f"Eb{j}"])
                      C.op("pool", lambda e, j=j, d=d: e.tensor_tensor(
                          out=Pb[j][:].rearrange("p (b hh c) -> p b hh c", b=2, hh=4),
                          in0=Eb[j][:].rearrange("p (b hh c) -> p b hh c", b=2, hh=4),
                          in1=epb4[:, :, d, :].rearrange("p (hh b) c -> p b hh c", b=2), op=ALU.mult),
                          R=[f"Eb{j}", "epb"], W=[f"Pb{j}"])
                  pv = PX[0]
                  sm = PX[1]
                  for h in range(8):
                      b = 64 * (h % 2)
                      for j in range(6):
                          mm(pv[b:b + 64, (h // 2) * 64:(h // 2 + 1) * 64], Vt[:, m0 + j, h * 64:(h + 1) * 64],
                             Pb[j][:, (h % 2) * 256 + (h // 2) * 64:(h % 2) * 256 + (h // 2) * 64 + 64], j == 0, j == 5,
                             R=[("Vt", m0 + j), f"Pb{j}"], W=[pk(pv)], inc=False)
                  for h in range(8):
                      b = 64 * (h % 2)
                      for j in range(6):
                          mm(sm[b:b + 64, (h // 2) * 64:(h // 2 + 1) * 64], onesb[:, 0:64],
                             Pb[j][:, (h % 2) * 256 + (h // 2) * 64:(h % 2) * 256 + (h // 2) * 64 + 64], j == 0, j == 5,
                             R=["onesb", f"Pb{j}"], W=[pk(sm)], inc=(h == 7 and j == 5))
                  C.op("act", lambda e: e.activation(out=rinv[:], in_=sm[:, 0:256], func=AF.Ln),
                       R=[pk(sm)], W=["rinv"])
                  C.op("act", lambda e: e.activation(out=rinv[:], in_=rinv[:], func=AF.Exp, scale=-1.0),
                       R=["rinv"], W=["rinv"])
                  C.op("dve", lambda e, q0=q0: e.tensor_tensor(
                      out=UnT[:, :, q0:q0 + 64], in0=pv[:, 0:256].rearrange("p (g c) -> p g c", g=4),
                      in1=rinv[:].rearrange("p (g c) -> p g c", g=4), op=ALU.mult),
                      R=[pk(pv), "rinv"], W=[("UnT", r // 8)])
              dbg_dump("p2", UnT[:].rearrange("p k t -> p (k t)"))
              C.barrier()
          C.maybe_rotate()

          with ExitStack() as es3:
              alloc_wbuf(es3, f"p3s{s}", 384)
              Ofb = sb("Ofb", [128, 32, 4, 64], BF, es3)
              sbon = sb("sbon", [128, 32, 4], F32, es3)
              Sst = sb("Sst", [128, 4, 64], F32, es3)
              Sbf = sb("Sbf", [128, 4, 64], BF, es3)
              Stmp = sb("Stmp", [128, 4, 64], F32, es3)
              zr = [sb("zr0", [128, TT + 2], F32, es3)]
              zsf = {n: sb(f"zs_{n}", [128, TT], F32, es3) for n in ("r", "k", "v")}
              lzs = sb("lzs", [128, TT], F32, es3)
              thb = sb("thb", [128, TT], BF, es3)
              adb = sb("adb", [128, TT], BF, es3)
              sgb = sb("sgb", [128, TT], BF, es3)
              cumx = sb("cumx", [128, TT + 1], F32, es3)
              sgw = sb("sgw", [128, TT], F32, es3)
              ar = sb("ar", [128, TT], F32, es3)
              dd = [sb(f"dd{i}", [128, TT], F32, es3) for i in range(3)]
              g0t = sb("g0t", [128, TT], F32, es3)
              ksq = sb("ksq", [128, TT], BF, es3)
              kkn = sb("kkn", [128, TT], F32, es3)
              prodb = sb("prodb", [128, TT], BF, es3)
              vb = sb("vb", [128, TT], BF, es3)
              ARt = sb("ARt", [128, 4, 8, 128], BF, es3)
              Btt = sb("Btt", [128, 4, TT], BF, es3)
              Ktt = sb("Ktt", [128, 4, TT], BF, es3)
              Bht = sb("Bht", [128, 4, TT], BF, es3)
              Kht = sb("Kht", [128, 4, TT], BF, es3)
              GCt = sb("GCt", [128, 4, 8], F32, es3)
              Vst = sb("Vst", [128, 4, 8, 64], BF, es3)
              BhT = sb("BhT", [128, 4, 8, 64], BF, es3)
              KhT = sb("KhT", [128, 4, 8, 64], BF, es3)
              Lp = [sb(f"Lp{p}", [128, 512], BF, es3) for p in range(4)]
              Ygb = sb("Ygb", [128, 4, 256], BF, es3)
              Upb = sb("Upb", [128, 4, 64], BF, es3)
              Ubz = sb("Ubz", [128, 2, 4, 64], BF, es3)
              fina = sb("fina", [128, 4, 4, 64], F32, es3)
              finb = sb("finb", [128, 4, 4, 64], F32, es3)
              fst = sb("fst", [128, 2, 16], F32, es3)
              uab = sb("uab", [128, 4, 4, 64], BF, es3)

              ones_bc = cv("ones")[:, 0:1].to_broadcast([128, TT])
              C.op("pool", lambda e: e.memset(cumx[:, 0:1], 0.0), W=["cumx0"])
              C.op("pool", lambda e: e.memset(sbon[:], 0.0), W=["sbon"])
              for p_ in range(4):
                  C.op("pool", lambda e, p_=p_: e.memset(Lp[p_][:], 0.0), W=[("LT", p_), ("LP", p_), ("LM", p_)])
              C.op("pool", lambda e: e.memset(Ubz[:], 0.0), W=["Ub"])
              C.op("pool", lambda e: e.memset(Ygb[:], 0.0), W=[("Ygb", p) for p in range(4)])

              def zproj(wt, wk, wc, e0, ci, dst, after=None, mus=None):
                  pa = PA[zproj.i % 2]
                  zproj.i += 1
                  px = PX[0]
                  z = zr[0]
                  zk = "zr0"
                  hk = [("hT", (e0 - 1) // 512), ("hT", e0 // 512), ("hT", min((e0 + 512) // 512, 4))]
                  for kc in range(8):
                      mm(pa[:], wt[:, kc, wc:wc + 128], hT[:, kc, e0:e0 + TT], kc == 0, kc == 7,
                         R=[wk] + hk, W=[pk(pa)], inc=False)
                  hbv = hT[:, :, e0 - 1:e0 + TT + 1]
                  for kc in range(8):
                      mm(px[:, 0:2], wt[:, kc, wc:wc + 128], hbv[:, kc, 0:TT + 2:TT + 1], kc == 0, kc == 7,
                         R=[wk] + hk, W=[pk(px)], inc=(kc == 7))
                  C.op("act", lambda e: e.activation(out=z[:, 1:TT + 1], in_=pa[:], func=AF.Copy),
                       R=[pk(pa)], W=[zk])
                  C.op("dve", lambda e: e.tensor_copy(out=z[:, 0:TT + 2:TT + 1], in_=px[:, 0:2]),
                       R=[pk(px)], W=[zk])
                  mpc = cv("mp", ci, ci + 1) if mus is None else mus[0][:, ci:ci + 1]
                  mnc = cv("mn", ci, ci + 1) if mus is None else mus[1][:, ci:ci + 1]
                  C.op("dve", lambda e: e.tensor_scalar(out=dst, in0=z[:, 1:TT + 1], scalar1=c0[:, ci:ci + 1],
                                                        scalar2=None, op0=ALU.mult), R=[zk] + DVK, W=[after])
                  C.op("dve", lambda e: e.scalar_tensor_tensor(out=dst, in0=z[:, 0:TT], scalar=mpc, in1=dst,
                                                               op0=ALU.mult, op1=ALU.add), R=[zk, "cs", "slp", after], W=[after])
                  C.op("dve", lambda e: e.scalar_tensor_tensor(out=dst, in0=z[:, 2:TT + 2], scalar=mnc, in1=dst,
                                                               op0=ALU.mult, op1=ALU.add), R=[zk, "cs", "slp", after], W=[after])
              zproj.i = 0
              c3 = lambda a: a.rearrange("p (c t) -> p c t", t=64)

              def finalize(t5, hf):
                  cb = t5 * 8 + hf * 4
                  A_ = fina[:].rearrange("p c a v -> p (c a) v")
                  B_ = finb[:].rearrange("p c a v -> p (c a) v")
                  mean = fst[:, 0, :]
                  var = fst[:, 1, :]
                  C.op("dve", lambda e: e.tensor_reduce(out=mean, in_=A_, axis=AX.X, op=ALU.add), R=["fina"], W=["fst0"])
                  C.op("dve", lambda e: e.tensor_scalar(out=mean, in0=mean, scalar1=1.0 / 64, scalar2=None, op0=ALU.mult), R=["fst0"], W=["fst0"])
                  C.op("dve", lambda e: e.tensor_tensor(out=A_, in0=A_, in1=mean.unsqueeze(2).to_broadcast([128, 16, 64]), op=ALU.subtract),
                       R=["fina", "fst0"], W=["fina"])
                  C.op("pool", lambda e: e.tensor_tensor(out=B_, in0=A_, in1=A_, op=ALU.mult), R=["fina"], W=["finb"])
                  C.op("dve", lambda e: e.tensor_reduce(out=var, in_=B_, axis=AX.X, op=ALU.add), R=["finb"], W=["fst1"])
                  C.op("act", lambda e: e.activation(out=var, in_=var, func=AF.Sqrt, bias=epsc[:, 2:3], scale=1.0 / 64), R=["fst1", "epsc"], W=["fst1"])
                  C.op("dve", lambda e: e.reciprocal(out=var, in_=var), R=["fst1"], W=["fst1"])
                  C.op("dve", lambda e: e.tensor_tensor(out=A_, in0=A_, in1=var.unsqueeze(2).to_broadcast([128, 16, 64]), op=ALU.mult),
                       R=["fina", "fst1"], W=["fina"])
                  lnw = cv("lnw").rearrange("p (a v) -> p a v", a=4).unsqueeze(1).to_broadcast([128, 4, 4, 64])
                  lnb = cv("lnb").rearrange("p (a v) -> p a v", a=4).unsqueeze(1).to_broadcast([128, 4, 4, 64])
                  C.op("pool", lambda e: e.tensor_tensor(out=fina[:], in0=fina[:], in1=lnw, op=ALU.mult), R=["fina", "cs"], W=["fina"])
                  C.op("pool", lambda e: e.tensor_tensor(out=fina[:], in0=fina[:], in1=lnb, op=ALU.add), R=["fina", "cs"], W=["fina"])
                  sb_ = sbon[:, cb:cb + 4, :].unsqueeze(3).to_broadcast([128, 4, 4, 64])
                  VK = [("Vst", p) for p in range(4)]
                  C.op("dve", lambda e: e.tensor_tensor(out=finb[:], in0=Vst[:, :, hf * 4:hf * 4 + 4, :].rearrange("p a c v -> p c a v"), in1=sb_, op=ALU.mult),
                       R=VK + ["sbon"], W=["finb"])
                  C.op("pool", lambda e: e.tensor_tensor(out=fina[:], in0=fina[:], in1=finb[:], op=ALU.add), R=["fina", "finb"], W=["fina"])
                  for c2 in range(2):
                      pg = PA[c2 % 2]
                      for cc in range(2):
                          c = hf * 4 + c2 * 2 + cc
                          for p in range(4):
                              for h in range(2):
                                  o = (cc * 4 + p) * 64
                                  mm(pg[64 * h:64 * h + 64, o:o + 64], sgb[:, c * 64:(c + 1) * 64],
                                     g2b[:, (2 * p + h) * 64:(2 * p + h + 1) * 64], True, True,
                                     R=["sgb", "g2b"], W=[pk(pg)], inc=(cc == 1 and p == 3 and h == 1))
                      C.op("dve", lambda e, pg=pg, c2=c2: e.tensor_tensor(
                          out=uab[:, c2 * 2:c2 * 2 + 2].rearrange("p c a v -> p (c a v)"),
                          in0=fina[:, c2 * 2:c2 * 2 + 2].rearrange("p c a v -> p (c a v)"),
                          in1=pg[:], op=ALU.mult), R=["fina", pk(pg)], W=["uab"])
                  for p in range(4):
                      for c in range(4):
                          for h in range(2):
                              mm(PS6[64 * h:64 * h + 64, c * 64:(c + 1) * 64], uab[:, c, p, :], identb[:, 64 * h:64 * h + 64], True, True,
                                 R=["uab", "identb"], W=[pk(PS6)], inc=(c == 3 and h == 1))
                      evac(UaT[:, p, t5 * TT + hf * 256:t5 * TT + hf * 256 + 256], PS6[:, 0:256], R=[pk(PS6)], W=[("UaT", t5)])

              def rwkv_pass(dr, state_only=False, init="zero", slot=None):
                  fwd = (dr == 0)
                  mus = None
                  if slot is not None:
                      sp_ = slp[:, slot * 40:(slot + 1) * 40]
                      mus = (sp_[:, 8:23], sp_[:, 23:38])
                      dmask = sp_[:, 38:39]
                  if init == "zero":
                      C.op("dve", lambda e: e.memset(Sst[:], 0.0), W=["Sst"])
                      C.op("dve", lambda e: e.memset(Sbf[:], 0.0), W=["Sbf"])
                  elif init == "saved":
                      C.op("dve", lambda e: e.tensor_copy(out=Sst[:], in_=Ssave[:, dr]), R=[("Ssave", dr)], W=["Sst"])
                      C.op("act", lambda e: e.activation(out=Sbf[:], in_=Ssave[:, dr], func=AF.Copy), R=[("Ssave", dr)], W=["Sbf"])
                  tiles = range(NTILE) if fwd else range(NTILE - 1, -1, -1)
                  mX = MSK["mXf" if fwd else "mXb"]
                  mY = MSK["mYf" if fwd else "mYb"]
                  pb = 0 if fwd else 64
                  for t5 in tiles:
                      e0 = HALO + t5 * TT
                      wt, wk = load_w(w_in, 0, 1536, 384)
                      zproj(wt, wk, 0, e0, 12, lzs[:], "lzs", mus=mus)
                      C.op("act", lambda e: e.activation(out=thb[:], in_=lzs[:], func=AF.Tanh), R=["lzs"], W=["thb"])
                      if slot is not None:
                          C.op("dve", lambda e: e.tensor_scalar(out=thb[:], in0=thb[:], scalar1=dmask, scalar2=None, op0=ALU.mult),
                               R=["thb", "slp"], W=["thb"])
                      zproj(wt, wk, 128, e0, 13, lzs[:], "lzs", mus=mus)
                      if slot is not None:
                          C.op("dve", lambda e: e.tensor_scalar(out=adb[:], in0=lzs[:], scalar1=dmask, scalar2=None, op0=ALU.mult),
                               R=["lzs", "slp"], W=["adb"])
                      else:
                          C.op("pool", lambda e: e.tensor_copy(out=adb[:], in_=lzs[:]), R=["lzs"], W=["adb"])
                      dbg_dump("s1", thb[:])
                      if (not fwd) and not state_only:
                          zproj(wt, wk, 256, e0, 14, lzs[:], "lzs")
                          C.op("act", lambda e: e.activation(out=sgb[:], in_=lzs[:], func=AF.Sigmoid), R=["lzs"], W=["sgb"])
                      for p in range(4):
                          for n, base in (("r", 0), ("k", 512), ("v", 1024)):
                              if state_only and n == "r":
                                  continue
                              wt, wk = load_w(w_in, 0, base + p * 128, 128)
                              zproj(wt, wk, 0, e0, (base // 128) + p, zsf[n][:], f"zs_{n}", mus=mus)
                          pa = PA[2]
                          if slot is None:
                              mm(pa[:], lorab[pb:pb + 64, p * 128:(p + 1) * 128], thb[pb:pb + 64, :], True, True,
                                 R=["lorab", "thb"], W=[pk(pa)], inc=True)
                              w0ap = cv("w0f" if fwd else "w0b", p, p + 1)
                              a0ap = cv("a0f" if fwd else "a0b", p, p + 1)
                          else:
                              mm(pa[:], lorab[:, p * 128:(p + 1) * 128], thb[:, :], True, True,
                                 R=["lorab", "thb"], W=[pk(pa)], inc=True)
                              w0ap = sp_[:, p:p + 1]
                              a0ap = sp_[:, 4 + p:5 + p]
                          C.op("act", lambda e, pa=pa, p=p, w0ap=w0ap: e.activation(
                              out=sgw[:], in_=pa[:], func=AF.Sigmoid,
                              bias=w0ap, scale=1.0), R=[pk(pa), "cs", "slp"], W=["sgw"])
                          pa2 = PA[3]
                          if slot is None:
                              mm(pa2[:], lorab[pb:pb + 64, 512 + p * 128:512 + (p + 1) * 128], adb[pb:pb + 64, :], True, True,
                                 R=["lorab", "adb"], W=[pk(pa2)], inc=True)
                          else:
                              mm(pa2[:], lorab[:, 512 + p * 128:512 + (p + 1) * 128], adb[:, :], True, True,
                                 R=["lorab", "adb"], W=[pk(pa2)], inc=True)
                          C.op("act", lambda e, pa2=pa2, p=p, a0ap=a0ap: e.activation(
                              out=ar[:], in_=pa2[:], func=AF.Sigmoid,
                              bias=a0ap, scale=1.0), R=[pk(pa2), "cs", "slp"], W=["ar"])
                          C.op("dve", lambda e: e.tensor_tensor_scan(out=cumx[:, 1:TT + 1], data0=ones_bc, data1=sgw[:],
                                                                     initial=0.0, op0=ALU.mult, op1=ALU.add),
                               R=["cs", "sgw"], W=["cumx"])
                          cst_ = c3(cumx[:, 0:TT])[:, :, 0:1].to_broadcast([128, 8, 64])
                          cen_ = c3(cumx[:, 1:TT + 1])[:, :, 63:64].to_broadcast([128, 8, 64])
                          cK = ["cumx", "cumx0"]
                          if fwd:
                              C.op("pool", lambda e: e.tensor_tensor(out=c3(dd[0][:]), in0=c3(cumx[:, 1:TT + 1]), in1=cst_, op=ALU.subtract), R=cK, W=["dd0"])
                              C.op("pool", lambda e: e.tensor_tensor(out=c3(dd[1][:]), in0=c3(cumx[:, 0:TT]), in1=cst_, op=ALU.subtract), R=cK, W=["dd1"])
                              C.op("pool", lambda e: e.tensor_tensor(out=c3(dd[2][:]), in0=cen_, in1=c3(cumx[:, 1:TT + 1]), op=ALU.subtract), R=cK, W=["dd2"])
                          else:
                              C.op("pool", lambda e: e.tensor_tensor(out=c3(dd[0][:]), in0=cen_, in1=c3(cumx[:, 0:TT]), op=ALU.subtract), R=cK, W=["dd0"])
                              C.op("pool", lambda e: e.tensor_tensor(out=c3(dd[1][:]), in0=cen_, in1=c3(cumx[:, 1:TT + 1]), op=ALU.subtract), R=cK, W=["dd1"])
                              C.op("pool", lambda e: e.tensor_tensor(out=c3(dd[2][:]), in0=c3(cumx[:, 0:TT]), in1=cst_, op=ALU.subtract), R=cK, W=["dd2"])
                          C.op("act", lambda e: e.activation(out=g0t[:], in_=dd[0][:], func=AF.Exp, scale=-KAPPA), R=["dd0"], W=["g0t"])
                          C.op("act", lambda e: e.activation(out=dd[0][:], in_=dd[0][:], func=AF.Exp, scale=KAPPA), R=["dd0"], W=["dd0"])
                          C.op("act", lambda e: e.activation(out=dd[1][:], in_=dd[1][:], func=AF.Exp, scale=-KAPPA), R=["dd1"], W=["dd1"])
                          C.op("act", lambda e: e.activation(out=dd[2][:], in_=dd[2][:], func=AF.Exp, scale=-KAPPA), R=["dd2"], W=["dd2"])
                          Gt, Ginv, Gp, Gaft = g0t, dd[0], dd[1], dd[2]
                          dbg_dump("s2", thb[:])
                          gcol = 63 if fwd else 0
                          C.op("pool", lambda e, p=p: e.tensor_copy(out=GCt[:, p, :], in_=g0t[:, gcol:TT:64]), R=["g0t"], W=[("GCt", p)])
                          C.op("act", lambda e, p=p: e.activation(out=ksq[:], in_=zsf["k"][:], func=AF.Square,
                                                                  scale=cv("kk", p, p + 1)), R=["zs_k", "cs"], W=["ksq"])
                          pa = PA[2]
                          mm(pa[:], bonesb[:], ksq[:], True, True, R=["bonesb", "ksq"], W=[pk(pa)], inc=True)
                          C.op("act", lambda e, pa=pa: e.activation(out=sgw[:], in_=pa[:], func=AF.Ln, bias=epsc[:, 1:2], scale=1.0),
                               R=[pk(pa), "epsc"], W=["sgw"])
                          C.op("act", lambda e: e.activation(out=sgw[:], in_=sgw[:], func=AF.Exp, scale=-0.5), R=["sgw"], W=["sgw"])
                          C.op("dve", lambda e, p=p: e.scalar_tensor_tensor(out=kkn[:], in0=zsf["k"][:], scalar=cv("kk", p, p + 1),
                                                                            in1=sgw[:], op0=ALU.mult, op1=ALU.mult),
                               R=["zs_k", "cs", "sgw"], W=["kkn"])
                          ARp = ARt[:, p].rearrange("p c (two t) -> p c two t", two=2)
                          C.op("dve", lambda e: e.scalar_tensor_tensor(out=ARp[:, :, 0, :], in0=c3(kkn[:]), scalar=-1.0,
                                                                       in1=c3(Gp[:]), op0=ALU.mult, op1=ALU.mult),
                               R=["kkn", "dd1"], W=[("ARt", p)])
                          rg = Gt if fwd else Gp
                          if not state_only:
                              C.op("pool", lambda e: e.tensor_tensor(out=ARp[:, :, 1, :], in0=c3(zsf["r"][:]), in1=c3(rg[:]), op=ALU.mult),
                                   R=["zs_r", "g0t", "dd1"], W=[("ARt", p)])
                          C.op("pool", lambda e: e.tensor_tensor(out=kkn[:], in0=kkn[:], in1=ar[:], op=ALU.mult), R=["kkn", "ar"], W=["kkn"])
                          C.op("dve", lambda e, p=p: e.tensor_tensor(out=Btt[:, p, :], in0=kkn[:], in1=Ginv[:], op=ALU.mult), R=["kkn", "dd0"], W=[("Btt", p)])
                          C.op("pool", lambda e, p=p: e.tensor_tensor(out=Bht[:, p, :], in0=kkn[:], in1=Gaft[:], op=ALU.mult), R=["kkn", "dd2"], W=[("Bht", p)])
                          C.op("dve", lambda e, p=p: e.tensor_scalar(out=ar[:], in0=ar[:], scalar1=cv("ka", p, p + 1), scalar2=omka[:, p:p + 1],
                                                                     op0=ALU.mult, op1=ALU.add), R=["ar", "cs"] + DVK, W=["ar"])
                          kd = zsf["k"]
                          C.op("pool", lambda e: e.tensor_tensor(out=kd[:], in0=zsf["k"][:], in1=ar[:], op=ALU.mult), R=["zs_k", "ar"], W=["zs_k"])
                          C.op("dve", lambda e, p=p: e.tensor_tensor(out=Ktt[:, p, :], in0=kd[:], in1=Ginv[:], op=ALU.mult), R=["zs_k", "dd0"], W=[("Ktt", p)])
                          C.op("pool", lambda e, p=p: e.tensor_tensor(out=Kht[:, p, :], in0=kd[:], in1=Gaft[:], op=ALU.mult), R=["zs_k", "dd2"], W=[("Kht", p)])
                          dbg_dump("s3", thb[:])
                          if not state_only:
                              C.op("dve", lambda e, p=p: e.scalar_tensor_tensor(out=prodb[:], in0=zsf["r"][:], scalar=hrk[:, p:p + 1], in1=kd[:],
                                                                                op0=ALU.mult, op1=ALU.mult), R=["zs_r", "zs_k"] + DVK, W=["prodb"])
                              for c in range(8):
                                  for h in range(2):
                                      mm(QX.t[64 * h:64 * h + 64, h * 512 + c:h * 512 + c + 1], prodb[64 * h:64 * h + 64, c * 64:(c + 1) * 64],
                                         onesb[64 * h:64 * h + 64, 0:1], True, True, R=["prodb", "onesb"], W=[pk(QX.b[h])],
                                         inc=(c == 7 and h == 1))
                              for h in range(2):
                                  hs = slice(64 * h, 64 * h + 64)
                                  C.op("dve", lambda e, p=p, t5=t5, h=h, hs=hs: e.tensor_tensor(
                                      out=sbon[hs, t5 * 8:(t5 + 1) * 8, p], in0=sbon[hs, t5 * 8:(t5 + 1) * 8, p],
                                      in1=QX.t[hs, h * 512:h * 512 + 8], op=ALU.add), R=dk(QX) + ["sbon"], W=["sbon"])
                          dbg_dump("s4", thb[:])
                          C.op("act", lambda e: e.activation(out=vb[:], in_=zsf["v"][:], func=AF.Copy), R=["zs_v"], W=["vb"])
                          for src3, dstT, nm, sk in ((None, Vst, "Vst", "vb"), (Bht, BhT, "BhT", ("Bht", p)), (Kht, KhT, "KhT", ("Kht", p))):
                              for c in range(8):
                                  src_ap = (vb[:, c * 64:(c + 1) * 64] if src3 is None else src3[:, p, c * 64:(c + 1) * 64])
                                  for h in range(2):
                                      mm(PS6[64 * h:64 * h + 64, c * 64:(c + 1) * 64], src_ap, identb[:, 64 * h:64 * h + 64], True, True,
                                         R=[sk, "identb"], W=[pk(PS6)], inc=(c == 7 and h == 1))
                              evac(dstT[:, p].rearrange("p c v -> p (c v)"), PS6[:, 0:512], R=[pk(PS6)], W=[(nm, p)])
                      dbg_dump("s5", thb[:])
                      chunks = range(8) if fwd else range(7, -1, -1)
                      def chunk_fns(c):
                          gc = t5 * 8 + c
                          def st_gram(p):
                              At_ = lambda h: ARt[64 * h:64 * h + 64, p, c, 0:64]
                              Rt_ = lambda h: ARt[64 * h:64 * h + 64, p, c, 64:128]
                              Bt_ = lambda h: Btt[64 * h:64 * h + 64, p, c * 64:(c + 1) * 64]
                              Kt_ = lambda h: Ktt[64 * h:64 * h + 64, p, c * 64:(c + 1) * 64]
                              RK = [("ARt", p), ("Btt", p), ("Ktt", p)]
                              for h in range(2):
                                  hs = slice(64 * h, 64 * h + 64)
                                  o = h * 512
                                  WK = [pk(QX.b[h])]
                                  mm(QX.t[hs, o:o + 64], Bt_(h), At_(h), True, True, R=RK, W=WK)
                                  mm(QX.t[hs, o + 64:o + 128], At_(h), Bt_(h), True, True, R=RK, W=WK)
                                  mm(QX.t[hs, o + 128:o + 192], Kt_(h), At_(h), True, True, R=RK, W=WK, inc=(state_only and h == 1))
                                  if not state_only:
                                      mm(QX.t[hs, o + 192:o + 256], Bt_(h), Rt_(h), True, True, R=RK, W=WK)
                                      mm(QX.t[hs, o + 256:o + 320], Kt_(h), Rt_(h), True, True, R=RK, W=WK, inc=(h == 1))
                              for h in range(2):
                                  hs = slice(64 * h, 64 * h + 64)
                                  o = h * 512
                                  dsl = slice(64 * h, 64 * h + 64)
                                  C.op("dve", lambda e, hs=hs, o=o, dsl=dsl: e.tensor_tensor(
                                      out=Lp[p][hs, 128:512].rearrange("p (s t) -> p s t", s=3)[:, :, dsl],
                                      in0=QX.t[hs, o:o + 192].rearrange("p (s t) -> p s t", s=3),
                                      in1=mX[hs, :].rearrange("p (s t) -> p s t", s=3)[:, :, dsl], op=ALU.mult),
                                      R=dk(QX) + ["mskb"], W=[("LP", p), ("LM", p)])
                                  if state_only:
                                      continue
                                  C.op("dve", lambda e, hs=hs, o=o, dsl=dsl: e.tensor_tensor(
                                      out=Ygb[hs, p, :].rearrange("p (s t) -> p s t", s=2)[:, :, dsl],
                                      in0=QX.t[hs, o + 192:o + 320].rearrange("p (s t) -> p s t", s=2),
                                      in1=mY[hs, :].rearrange("p (s t) -> p s t", s=2)[:, :, dsl], op=ALU.mult),
                                      R=dk(QX) + ["mskb"], W=[("Ygb", p)])
                              C.op("pool", lambda e: e.tensor_tensor(out=Lp[p][:, 0:128], in0=Lp[p][:, 128:256], in1=identb[:], op=ALU.add),
                                   R=[("LP", p), "identb"], W=[("LT", p)])

                          def st_level(p, lev):
                              pd = PA[p]
                              Lt = Lp[p]
                              T_, P_, PT_ = Lt[:, 0:128], Lt[:, 128:256], Lt[:, 256:384]
                              LTk, LPk = ("LT", p), ("LP", p)
                              if lev >= 2:
                                  mm(pd[:, 0:128], identb[:], T_, True, False, R=[LTk, "identb"], W=[pk(pd)])
                                  mm(pd[:, 0:128], PT_, T_, False, True, R=[LPk, LTk], W=[pk(pd)], inc=(lev == 6))
                              if lev <= 5:
                                  mm(pd[:, 128:256], PT_, P_, True, True, R=[LPk], W=[pk(pd)])
                                  mm(pd[:, 256:384], P_, PT_, True, True, R=[LPk], W=[pk(pd)], inc=True)
                              if lev == 1:
                                  evac(Lt[:, 128:384], pd[:, 128:384], R=[pk(pd)], W=[LPk])
                              elif lev <= 5:
                                  evac(Lt[:, 0:384], pd[:, 0:384], R=[pk(pd)], W=[LTk, LPk])
                              else:
                                  evac(Lt[:, 0:128], pd[:, 0:128], R=[pk(pd)], W=[LTk])

                          def D_slot(tt):
                              for p in range(4):
                                  sg = tt - p
                                  if sg == 0:
                                      st_gram(p)
                                  elif 1 <= sg <= 6:
                                      st_level(p, sg)

                          def C1():
                              for p in range(4):
                                  for h in range(2):
                                      hs = slice(64 * h, 64 * h + 64)
                                      o = h * 512 + p * 64
                                      mm(QB.t[hs, o:o + 64], Lp[p][:, 384 + 64 * h:384 + 64 * h + 64], Vst[:, p, c, :], True, False,
                                         R=[("LM", p), ("Vst", p)], W=[pk(QB.b[h])])
                                      mm(QB.t[hs, o:o + 64], ARt[hs, p, c, 0:64], Sbf[hs, p, :], False, True,
                                         R=[("ARt", p), "Sbf"], W=[pk(QB.b[h])], inc=(p == 3 and h == 1))
                              for h in range(2):
                                  hs = slice(64 * h, 64 * h + 64)
                                  evac(Upb[hs].rearrange("p a v -> p (a v)"), QB.t[hs, h * 512:h * 512 + 256], R=dk(QB), W=["Upb"])

                          def C2():
                              pu2 = PS6
                              for p in range(4):
                                  o = p * 64
                                  mm(pu2[:, o:o + 64], Lp[p][:, 0:128], Upb[:, p, :], True, True, R=[("LT", p), "Upb"], W=[pk(pu2)], inc=(p == 3))
                              for h in range(2):
                                  hs = slice(64 * h, 64 * h + 64)
                                  evac(Ubz[hs, h].rearrange("p a v -> p (a v)"), pu2[hs, 0:256], R=[pk(pu2)], W=["Ub"])

                          def CY():
                              pass
                              if not state_only:
                                  for p in range(4):
                                      for h in range(2):
                                          hs = slice(64 * h, 64 * h + 64)
                                          o = h * 512 + p * 64
                                          WK = [pk(QX.b[h])]
                                          mm(QX.t[hs, o:o + 64], Ygb[:, p, 64 * h:64 * h + 64], Ubz[:, h, p, :], True, False, R=[("Ygb", p), "Ub"], W=WK)
                                          mm(QX.t[hs, o:o + 64], Ygb[:, p, 128 + 64 * h:128 + 64 * h + 64], Vst[:, p, c, :], False, False,
                                             R=[("Ygb", p), ("Vst", p)], W=WK)
                                          mm(QX.t[hs, o:o + 64], ARt[hs, p, c, 64:128], Sbf[hs, p, :], False, True,
                                             R=[("ARt", p), "Sbf"], W=WK, inc=(p == 3 and h == 1))
                                  for h in range(2):
                                      hs = slice(64 * h, 64 * h + 64)
                                      ysrc = QX.t[hs, h * 512:h * 512 + 256]
                                      if fwd:
                                          evac(Ofb[hs, gc].rearrange("p a v -> p (a v)"), ysrc, R=dk(QX), W=[("Ofb", gc)])
                                      else:
                                          C.op("dve", lambda e, hs=hs, ysrc=ysrc, gc=gc, c=c: e.tensor_tensor(
                                              out=fina[hs, c % 4].rearrange("p a v -> p (a v)"), in0=Ofb[hs, gc].rearrange("p a v -> p (a v)"),
                                              in1=ysrc, op=ALU.add), R=dk(QX) + [("Ofb", gc)], W=["fina"])

                          def C3():
                              QS = QX if state_only else QB
                              for p in range(4):
                                  for h in range(2):
                                      hs = slice(64 * h, 64 * h + 64)
                                      o = h * 512 + p * 64
                                      mm(QS.t[hs, o:o + 64], BhT[:, p, c, :], Ubz[:, h, p, :], True, False, R=[("BhT", p), "Ub"], W=[pk(QS.b[h])])
                                      mm(QS.t[hs, o:o + 64], KhT[hs, p, c, :], Vst[hs, p, c, :], False, True, R=[("KhT", p), ("Vst", p)],
                                         W=[pk(QS.b[h])], inc=(p == 3 and h == 1))
                              C.op("dve", lambda e, c=c: e.tensor_tensor(out=Stmp[:], in0=Sst[:], in1=GCt[:, :, c:c + 1].to_broadcast([128, 4, 64]), op=ALU.mult),
                                   R=["Sst"] + [("GCt", p) for p in range(4)], W=["Stmp"])
                              for h in range(2):
                                  hs = slice(64 * h, 64 * h + 64)
                                  C.op("dve", lambda e, hs=hs, h=h: e.tensor_tensor(
                                      out=Sst[hs].rearrange("p a v -> p (a v)"), in0=Stmp[hs].rearrange("p a v -> p (a v)"),
                                      in1=QS.t[hs, h * 512:h * 512 + 256], op=ALU.add), R=["Stmp"] + dk(QS), W=["Sst"])
                              C.op("act", lambda e: e.activation(out=Sbf[:], in_=Sst[:], func=AF.Copy), R=["Sst"], W=["Sbf"])

                          def FIN():
                              if (not fwd) and (not state_only) and c % 4 == 0:
                                  finalize(t5, c // 4)
                          return D_slot, C1, C2, CY, C3, FIN

                      clist = list(chunks)
                      fns = {c: chunk_fns(c) for c in clist}
                      if state_only:
                          for idx, c in enumerate(clist):
                              D_, C1_, C2_, CY_, C3_, FIN_ = fns[c]
                              if idx == 0:
                                  for tt in range(10):
                                      D_(tt)
                              C1_()
                              C2_()
                              if idx + 1 < len(clist):
                                  Dn = fns[clist[idx + 1]][0]
                                  Dn(0)
                                  C3_()
                                  for tt in range(1, 10):
                                      Dn(tt)
                              else:
                                  C3_()
                      else:
                          for c in clist:
                              D_, C1_, C2_, CY_, C3_, FIN_ = fns[c]
                              for tt in range(10):
                                  D_(tt)
                              C1_()
                              C2_()
                              CY_()
                              C3_()
                              FIN_()
                      C.maybe_rotate()

              if s == 0 and WITH_XCORE:
                  xt_v = [fina[:].rearrange("p c a v -> p (c a v)")]
                  xb_v = [uab[:].rearrange("p c a v -> p (c a v)")]
                  selb = slp[:, 280:288]

                  def boundary(i):
                      C.op("dve", lambda e: e.scalar_tensor_tensor(out=Ssave[:, 1], in0=Sst[:], scalar=selb[:, i:i + 1], in1=Ssave[:, 1],
                                                                   op0=ALU.mult, op1=ALU.add), R=["Sst", "slp", ("Ssave", 1)], W=[("Ssave", 1)])
                      C.op("dve", lambda e: e.tensor_scalar(out=Stmp[:, 0, 0:1], in0=selb[:, i:i + 1], scalar1=-1.0, scalar2=1.0,
                                                            op0=ALU.mult, op1=ALU.add), R=["slp"], W=["Stmp"])
                      C.op("dve", lambda e: e.tensor_scalar(out=Sst[:], in0=Sst[:], scalar1=Stmp[:, 0, 0:1], scalar2=None, op0=ALU.mult),
                           R=["Sst", "Stmp"], W=["Sst"])
                      C.op("act", lambda e: e.activation(out=Sbf[:], in_=Sst[:], func=AF.Copy), R=["Sst"], W=["Sbf"])

                  C.op("dve", lambda e: e.memset(Ssave[:], 0.0), W=[("Ssave", 0), ("Ssave", 1)])
                  C.op("dve", lambda e: e.memset(Sst[:], 0.0), W=["Sst"])
                  C.op("dve", lambda e: e.memset(Sbf[:], 0.0), W=["Sbf"])
                  for j in range(7):
                      boundary(j)
                      fill_hT(xo, j * SLOT_EXT, list(range(1, 1 + SLOT_EXT // 128)), xt_v, xb_v, ["fina"], ["uab"])
                      rwkv_pass(0, state_only=True, init="keep", slot=j)
                  boundary(7)
                  C.op("dve", lambda e: e.tensor_copy(out=Ssave[:, 0], in_=Sst[:]), R=["Sst"], W=[("Ssave", 0)])
                  fill_hT(xs, xoff, list(range(EXT // 128)), xt_v, xb_v, ["fina"], ["uab"])
                  rwkv_pass(0, init="saved")
                  rwkv_pass(1, init="saved")
              else:
                  rwkv_pass(0)
                  dbg_dump("p3f", Ofb[:].rearrange("p c a v -> p (c a v)"))
                  rwkv_pass(1)
              dbg_dump("p3", UaT[:].rearrange("p k t -> p (k t)"))
              C.barrier()
          C.maybe_rotate()

          with ExitStack() as es4:
              alloc_wbuf(es4, f"p4s{s}", 512)
              xT = sb("xT", [128, 8, TT], F32, es4)
              xin = [sb(f"xin{i}", [128, D], F32, es4) for i in range(2)]
              identf = sb("identf", [128, 128], F32, es4)
              hb = sb("hb", [128, 8, TT], BF, es4)
              mT = sb("mT", [128, 8, TT], BF, es4)
              aT = sb("aT", [128, 22, TT], BF, es4)
              sqb = sb("sqb", [128, 8, TT], BF, es4)
              rsb = sb("rsb", [128, TT], F32, es4)
              sga = sb("sga", [128, TT], F32, es4)
              sgn = sb("sgn", [128, TT], F32, es4)
              t1 = sb("t1", [128, TT], F32, es4)
              pin = [sb(f"pin{i}", [128, 256], F32, es4) for i in range(2)]
              pbf = sb("pbf", [128, 256], BF, es4)
              pT = sb("pT", [128, 2, TT], BF, es4)
              yo = [sb(f"yo{i}", [128, D], F32, es4) for i in range(2)]
              C.op("dve", lambda e: e.tensor_copy(out=identf[:], in_=cv("ident")), R=["cs"], W=["identf"])

              def rms_bcast(gname, dst):
                  C.op("act", lambda e: e.activation(out=sqb[:], in_=xT[:], func=AF.Square), R=["xT"], W=["sqb"])
                  pa = PA[0]
                  for kc in range(8):
                      mm(pa[:], onesb[:], sqb[:, kc, :], kc == 0, kc == 7, R=["onesb", "sqb"], W=[pk(pa)], inc=(kc == 7))
                  C.op("act", lambda e: e.activation(out=rsb[:], in_=pa[:], func=AF.Ln, bias=epsc[:, 0:1], scale=1.0 / D),
                       R=[pk(pa), "epsc"], W=["rsb"])
                  C.op("act", lambda e: e.activation(out=rsb[:], in_=rsb[:], func=AF.Exp, scale=-0.5), R=["rsb"], W=["rsb"])
                  for kc in range(8):
                      C.op("dve", lambda e, kc=kc: e.scalar_tensor_tensor(out=dst[:, kc, :], in0=xT[:, kc, :], scalar=cv(gname, kc, kc + 1),
                                                                          in1=rsb[:], op0=ALU.mult, op1=ALU.mult),
                           R=["xT", "cs", "rsb"], W=["hb"])

              for t5 in range(NTILE):
                  e0 = HALO + t5 * TT
                  tsl = slice(t5 * TT, (t5 + 1) * TT)
                  for b4 in range(4):
                      i = b4 % 2
                      C.dma("sp", xin[i][:], xs[xoff + e0 + b4 * 128: xoff + e0 + (b4 + 1) * 128, :], W=[f"xin{i}"])
                      for half in range(2):
                          pa = PA[(b4 * 2 + half) % 4]
                          for q in range(4):
                              kc = half * 4 + q
                              C.op("pe", lambda e, pa=pa, q=q, kc=kc, i=i: e.transpose(
                                  out=pa[:, q * 128:(q + 1) * 128], in_=xin[i][:, kc * 128:(kc + 1) * 128], identity=identf[:]),
                                  R=[f"xin{i}", "identf"], W=[pk(pa)], inc=(q == 3))
                          evac(xT[:, half * 4:half * 4 + 4, b4 * 128:(b4 + 1) * 128],
                               pa[:].rearrange("p (k t) -> p k t", k=4), R=[pk(pa)], W=["xT"])
                  for sweep, (usrc, ukey, wsrc, gbase, sgt) in enumerate(((UaT, "UaT", w_bra, 3456, sga), (UnT, "UnT", w_brn, 4480, sgn))):
                      for half in range(2):
                          wb_, wbk = load_w(wsrc, 0, half * 512, 512, nk=4)
                          wg_, wgk = load_w(w_in, 0, gbase + half * 512, 512)
                          for q in range(4):
                              dc = half * 4 + q
                              pg = PA[0]
                              pyv = PA[1]
                              for kc in range(8):
                                  mm(pg[:], wg_[:, kc, q * 128:(q + 1) * 128], hT[:, kc, e0:e0 + TT], kc == 0, kc == 7,
                                     R=[wgk, ("hT", e0 // 512), ("hT", e0 // 512 + 1)], W=[pk(pg)], inc=(kc == 7))
                              for kc in range(4):
                                  mm(pyv[:], wb_[:, kc, q * 128:(q + 1) * 128], usrc[:, kc, tsl], kc == 0, kc == 3,
                                     R=[wbk, (ukey, t5)], W=[pk(pyv)], inc=(kc == 3))
                              C.op("act", lambda e, pg=pg, sgt=sgt: e.activation(out=sgt[:], in_=pg[:], func=AF.Sigmoid), R=[pk(pg)], W=[sgt.name])
                              if sweep == 0:
                                  C.op("dve", lambda e, pyv=pyv, dc=dc, sgt=sgt: e.tensor_tensor(out=hb[:, dc, :], in0=sgt[:], in1=pyv[:], op=ALU.mult),
                                       R=[sgt.name, pk(pyv)], W=["hb"])
                              else:
                                  C.op("dve", lambda e, pyv=pyv, sgt=sgt: e.tensor_tensor(out=t1[:], in0=sgt[:], in1=pyv[:], op=ALU.mult),
                                       R=[sgt.name, pk(pyv)], W=["t1"])
                                  C.op("pool", lambda e, dc=dc: e.tensor_tensor(out=mT[:, dc, :], in0=t1[:], in1=hb[:, dc, :], op=ALU.add),
                                       R=["t1", "hb"], W=["mT"])
                  for half in range(2):
                      wo, wok = load_w(w_out, 0, half * 512, 512)
                      for q in range(4):
                          dc = half * 4 + q
                          pa = PA[dc % 2]
                          for kc in range(8):
                              mm(pa[:], wo[:, kc, q * 128:(q + 1) * 128], mT[:, kc, :], kc == 0, kc == 7, R=[wok, "mT"], W=[pk(pa)], inc=(kc == 7))
                          C.op("dve", lambda e, pa=pa, dc=dc: e.tensor_tensor(out=xT[:, dc, :], in0=xT[:, dc, :], in1=pa[:], op=ALU.add),
                               R=["xT", pk(pa)], W=["xT"])
                  rms_bcast("gffn", hb)
                  for f0 in range(0, DFF, 512):
                      nc_ = min(512, DFF - f0)
                      wg_, wgk = load_w(w_gate, 0, f0, nc_)
                      wu_, wuk = load_w(w_up, 0, f0, nc_)
                      for q in range(nc_ // 128):
                          fc = f0 // 128 + q
                          pg = PA[(fc % 2) * 2]
                          pu = PA[(fc % 2) * 2 + 1]
                          for kc in range(8):
                              mm(pg[:], wg_[:, kc, q * 128:(q + 1) * 128], hb[:, kc, :], kc == 0, kc == 7, R=[wgk, "hb"], W=[pk(pg)], inc=(kc == 7))
                          for kc in range(8):
                              mm(pu[:], wu_[:, kc, q * 128:(q + 1) * 128], hb[:, kc, :], kc == 0, kc == 7, R=[wuk, "hb"], W=[pk(pu)], inc=(kc == 7))
                          C.op("act", lambda e, pg=pg: e.activation(out=sga[:], in_=pg[:], func=AF.Silu), R=[pk(pg)], W=["sga"])
                          C.op("dve", lambda e, pu=pu, fc=fc: e.tensor_tensor(out=aT[:, fc, :], in0=sga[:], in1=pu[:], op=ALU.mult),
                               R=["sga", pk(pu)], W=[("aT", fc)])
                  for half in range(2):
                      pas = [PA[q] for q in range(4)]
                      for g0 in range(0, 22, 8):
                          ng = min(8, 22 - g0)
                          wd_, wdk = load_w(w_down, g0 * 128, half * 512, 512, nk=ng)
                          for q in range(4):
                              for k in range(ng):
                                  fc = g0 + k
                                  mm(pas[q][:], wd_[:, k, q * 128:(q + 1) * 128], aT[:, fc, :], fc == 0, fc == 21,
                                     R=[wdk, ("aT", fc)], W=[pk(pas[q])], inc=(fc == 21 or k == ng - 1))
                      for q in range(4):
                          dc = half * 4 + q
                          C.op("dve", lambda e, q=q, dc=dc, pas=pas: e.tensor_tensor(out=xT[:, dc, :], in0=xT[:, dc, :], in1=pas[q][:], op=ALU.add),
                               R=["xT", pk(pas[q])], W=["xT"])
                  for b4 in range(4):
                      i = b4 % 2
                      C.dma("sp", pin[i][:], pp[s * SEQT + t5 * TT + b4 * 128: s * SEQT + t5 * TT + (b4 + 1) * 128, :], W=[f"pin{i}"])
                      C.op("pool", lambda e, i=i: e.tensor_copy(out=pbf[:], in_=pin[i][:]), R=[f"pin{i}"], W=["pbf"])
                      pt = PT[0]
                      for kc in range(2):
                          tp(pt[:, kc * 128:(kc + 1) * 128], pbf[:, kc * 128:(kc + 1) * 128], R=["pbf"], W=[pk(pt)], inc=(kc == 1))
                      evac(pT[:, :, b4 * 128:(b4 + 1) * 128], pt[:, 0:256].rearrange("p (k t) -> p k t", k=2), R=[pk(pt)], W=["pT"])
                  rms_bcast("gple", hb)
                  for half in range(2):
                      wp_, wpk = load_w(w_pg, 0, half * 512, 512)
                      we_, wek = load_w(w_ple, 0, half * 512, 512, nk=2)
                      for q in range(4):
                          dc = half * 4 + q
                          pg = PA[(dc % 2) * 2]
                          pe_ = PA[(dc % 2) * 2 + 1]
                          for kc in range(8):
                              mm(pg[:], wp_[:, kc, q * 128:(q + 1) * 128], hb[:, kc, :], kc == 0, kc == 7, R=[wpk, "hb"], W=[pk(pg)], inc=(kc == 7))
                          for kc in range(2):
                              mm(pe_[:], we_[:, kc, q * 128:(q + 1) * 128], pT[:, kc, :], kc == 0, kc == 1, R=[wek, "pT"], W=[pk(pe_)], inc=(kc == 1))
                          C.op("act", lambda e, pg=pg: e.activation(out=sga[:], in_=pg[:], func=AF.Sigmoid), R=[pk(pg)], W=["sga"])
                          C.op("dve", lambda e, pe_=pe_: e.tensor_tensor(out=t1[:], in0=sga[:], in1=pe_[:], op=ALU.mult), R=["sga", pk(pe_)], W=["t1"])
                          C.op("pool", lambda e, dc=dc: e.tensor_tensor(out=xT[:, dc, :], in0=xT[:, dc, :], in1=t1[:], op=ALU.add), R=["xT", "t1"], W=["xT"])
                  C.op("act", lambda e: e.activation(out=sqb[:], in_=xT[:], func=AF.Square), R=["xT"], W=["sqb"])
                  pa = PA[0]
                  for kc in range(8):
                      mm(pa[:], onesb[:], sqb[:, kc, :], kc == 0, kc == 7, R=["onesb", "sqb"], W=[pk(pa)], inc=(kc == 7))
                  C.op("act", lambda e, pa=pa: e.activation(out=rsb[:], in_=pa[:], func=AF.Ln, bias=epsc[:, 0:1], scale=1.0 / D), R=[pk(pa), "epsc"], W=["rsb"])
                  C.op("act", lambda e: e.activation(out=rsb[:], in_=rsb[:], func=AF.Exp, scale=-0.5), R=["rsb"], W=["rsb"])
                  for kc in range(8):
                      C.op("dve", lambda e, kc=kc: e.scalar_tensor_tensor(out=xT[:, kc, :], in0=xT[:, kc, :], scalar=cv("gfin", kc, kc + 1),
                                                                          in1=rsb[:], op0=ALU.mult, op1=ALU.mult), R=["xT", "cs", "rsb"], W=["xT"])
                  for b4 in range(4):
                      i = b4 % 2
                      for half in range(2):
                          pa = PA[(b4 * 2 + half) % 4]
                          for q in range(4):
                              kc = half * 4 + q
                              C.op("pe", lambda e, pa=pa, q=q, kc=kc, b4=b4: e.transpose(
                                  out=pa[:, q * 128:(q + 1) * 128], in_=xT[:, kc, b4 * 128:(b4 + 1) * 128], identity=identf[:]),
                                  R=["xT", "identf"], W=[pk(pa)], inc=(q == 3))
                          evac(yo[i][:, half * 512:(half + 1) * 512], pa[:], R=[pk(pa)], W=[f"yo{i}"])
                      C.dma("sp", y_d[s * SEQT + t5 * TT + b4 * 128: s * SEQT + t5 * TT + (b4 + 1) * 128, :], yo[i][:], R=[f"yo{i}"])
                  C.maybe_rotate()
              C.barrier()

    except _Stop:
        print("instructions:", C.nins)
        nc._ctx = C
        return nc
    C.barrier()
    ES.close()
    print("instructions:", C.nins)
    nc._ctx = C
    return nc


def simulate_sync(C):
    pcs = {e: 0 for e in C.trace}
    sems = {}
    progress = True
    while progress:
        progress = False
        for e, tr in C.trace.items():
            while pcs[e] < len(tr):
                k, key, v = tr[pcs[e]]
                if k == "w":
                    if sems.get(key, 0) >= v:
                        pcs[e] += 1
                        progress = True
                    else:
                        break
                else:
                    sems[key] = sems.get(key, 0) + v
                    pcs[e] += 1
                    progress = True
    stuck = {e: (pcs[e], len(tr), tr[pcs[e]] if pcs[e] < len(tr) else None) for e, tr in C.trace.items()}
    ok = all(pcs[e] == len(tr) for e, tr in C.trace.items())
    return ok, stuck, sems


_PROG = {}


def kernel(**inp):
    inp = {k: np.asarray(v) for k, v in inp.items()}
    xp = inp["x_prompt"][0]
    xsm = inp["x_sample"]
    ppm = inp["p_prompt"][0, 0]
    psm = inp["p_sample"][0]
    f32 = lambda a: np.ascontiguousarray(a, dtype=np.float32)
    shared = {
        "nab": _build_nab(inp["rpb"][0]),
        "msk": np.ascontiguousarray(np.concatenate([_masks()[k] for k in ("mXf", "mYf", "mXb", "mYb")], 1)),
        "w_in": f32(inp["w_in"][0]),
        "w_lora": f32(np.concatenate([np.concatenate([inp["w2_f"][0], inp["w2_b"][0]], 0),
                                      np.concatenate([inp["a2_f"][0], inp["a2_b"][0]], 0)], 1)),
        "g2": f32(inp["g2"][0]),
        "w_br_a": f32(inp["w_br_a"][0]), "w_br_n": f32(inp["w_br_n"][0]), "w_out": f32(inp["w_out"][0]),
        "w_gate": f32(inp["w_gate"][0]), "w_up": f32(inp["w_up"][0]), "w_down": f32(inp["w_down"][0]),
        "w_ple": f32(inp["w_ple"][0]), "w_pg": f32(inp["w_pg"][0]),
    }
    in_maps = []
    for c in range(NCORE):
        xs = np.zeros((3, EXT, D), np.float32)
        lo = c * SEQT - HALO
        hi = c * SEQT + SEQT + HALO
        a, b = max(lo, 0), min(hi, xp.shape[0])
        xs[0, a - lo:b - lo] = xp[a:b]
        xs[1, HALO:HALO + SEQT] = xsm[2 * c]
        xs[2, HALO:HALO + SEQT] = xsm[2 * c + 1]
        pp = np.stack([ppm[c * SEQT:(c + 1) * SEQT], psm[2 * c], psm[2 * c + 1]], 0)
        m = dict(shared)
        xo = np.zeros((7, SLOT_EXT, D), np.float32)
        slp = np.zeros((128, 7 * 40 + 8), np.float32)
        nb = NCORE - 1 - c
        for i in range(7):
            isb = i < nb
            g = (NCORE - 1 - i) if isb else (i - nb)
            lo2 = g * SEQT - 128
            hi2 = g * SEQT + SEQT + 128
            a2, b2 = max(lo2, 0), min(hi2, xp.shape[0])
            seg = np.zeros((SLOT_EXT, D), np.float32)
            seg[a2 - lo2:b2 - lo2] = xp[a2:b2]
            xo[i] = seg[::-1] if isb else seg
            o = i * 40
            slp[:, o:o + 4] = _pp(inp["w0_b" if isb else "w0_f"][0], 4)
            slp[:, o + 4:o + 8] = _pp(inp["a0_b" if isb else "a0_f"][0], 4)
            slp[:, o + 8:o + 23] = _pp(inp["mu_next" if isb else "mu_prev"][0], 15)
            slp[:, o + 23:o + 38] = _pp(inp["mu_prev" if isb else "mu_next"][0], 15)
            slp[64:128 if isb else 0:64, o + 38] = 0.0
            slp[(64 if isb else 0):(128 if isb else 64), o + 38] = 1.0
        slp[:, 280 + nb] = 1.0
        m["xo"] = xo.reshape(7 * SLOT_EXT, D)
        m["slp"] = slp
        m["xs"] = xs.reshape(3 * EXT, D)
        m["pp"] = f32(pp.reshape(3 * SEQT, 256))
        m["cst"] = _build_cst(inp, c)
        in_maps.append(m)
    if "nc" not in _PROG:
        _PROG["nc"] = build_program()
    res = run_bass_kernel_spmd(_PROG["nc"], in_maps, core_ids=list(range(NCORE)))
    ys = [np.asarray(r["y"]).reshape(3, SEQT, D) for r in res.results]
    y_prompt = np.concatenate([y[0] for y in ys], 0)[None]
    y_sample = np.stack([ys[c][1 + k] for c in range(NCORE) for k in range(2)], 0)
    return (y_prompt.astype(np.float32), y_sample.astype(np.float32))
```

```python
from contextlib import ExitStack
import numpy as np
import concourse.bass as bass
import concourse.mybir as mybir
from concourse.bass_utils import run_bass_kernel_spmd

F32 = mybir.dt.float32
BF = mybir.dt.bfloat16
AF = mybir.ActivationFunctionType
ALU = mybir.AluOpType
AX = mybir.AxisListType

NCORE = 8
D = 1024
SEQT = 2048
HALO = 256
EXT = SEQT + 2 * HALO
TT = 512
NTILE = SEQT // TT
DIN = 5504
DFF = 2816
KAPPA = float(np.exp(-0.5))
NEG = -30000.0
WITH_XCORE = True
SLOT_EXT = SEQT + 256

CST_SPEC = [
    ("ident", 128), ("bones", 128),
    ("colmask", 64), ("narm", 576), ("ones", 128),
    ("gmix", 8), ("gffn", 8), ("gple", 8), ("gfin", 8), ("mp", 15), ("mn", 15),
    ("w0f", 4), ("w0b", 4), ("a0f", 4), ("a0b", 4), ("kk", 4), ("ka", 4), ("rk", 4),
    ("lnw", 256), ("lnb", 256), ("first", 1),
]
CST_OFF = {}
_o = 0
for _n, _w in CST_SPEC:
    CST_OFF[_n] = (_o, _w)
    _o += _w
NCST = _o


def _pp(v, nch):
    return np.ascontiguousarray(np.asarray(v, np.float32).reshape(nch, 128).T)


def _masks():
    p = np.arange(128)[:, None]
    f = np.arange(128)[None, :]
    same = (p // 64) == (f // 64)
    a = p % 64
    b = f % 64
    out = {}
    up = same & (a < b)
    lo = same & (a > b)
    upi = same & (a <= b)
    out["mXf"] = np.concatenate([up, lo, up], 1).astype(np.float32)
    out["mYf"] = np.concatenate([upi, upi], 1).astype(np.float32)
    out["mXb"] = np.concatenate([lo, up, lo], 1).astype(np.float32)
    out["mYb"] = np.concatenate([lo, lo], 1).astype(np.float32)
    return out


def _narm(core):
    out = np.zeros((128, 3, 32, 6), np.float32)
    for s in range(3):
        fc = (s > 0) or (core == 0)
        lc = (s > 0) or (core == NCORE - 1)
        for r in range(32):
            if r < 4 and fc:
                lo, hi = 4, 11
            elif r >= 28 and lc:
                lo, hi = 28, 35
            else:
                lo, hi = r, r + 7
            m0 = min(max(r // 2, 0), 14)
            for j in range(6):
                m = m0 + j
                for half in range(2):
                    e = 2 * m + half
                    if not (lo <= e <= hi):
                        out[64 * half:64 * half + 64, s, r, j] = NEG
    return out.reshape(128, 576)


def _build_cst(inp, core):
    c = np.zeros((128, NCST), np.float32)

    def put(name, arr):
        o, w = CST_OFF[name]
        c[:, o:o + w] = np.asarray(arr, np.float32).reshape(128, w)

    put("ident", np.eye(128))
    p = np.arange(128)
    put("bones", (p[:, None] // 64 == p[None, :] // 64))
    kc = np.arange(64)[:, None]
    cc = np.arange(64)[None, :]
    cs = np.clip(cc - 8, 0, 48)
    cm = ((kc >= cs) & (kc <= cs + 15)).astype(np.float32)
    put("colmask", np.concatenate([cm, cm], 0))
    put("narm", _narm(core))
    put("ones", np.ones((128, 128)))
    put("gmix", _pp(inp["g_mix"][0], 8))
    put("gffn", _pp(inp["g_ffn"][0], 8))
    put("gple", _pp(inp["g_ple"][0], 8))
    put("gfin", _pp(inp["g_final"], 8))
    put("mp", _pp(inp["mu_prev"][0], 15))
    put("mn", _pp(inp["mu_next"][0], 15))
    for nm, key in (("w0f", "w0_f"), ("w0b", "w0_b"), ("a0f", "a0_f"), ("a0b", "a0_b"),
                    ("kk", "k_k"), ("ka", "k_a")):
        put(nm, _pp(inp[key][0], 4))
    put("rk", _pp(inp["r_k"][0].reshape(-1), 4))
    for nm, key in (("lnw", "lnx_w"), ("lnb", "lnx_b")):
        v = np.asarray(inp[key][0], np.float32).reshape(4, 2, 64)
        a = np.transpose(v, (1, 0, 2))
        a = np.repeat(a[:, None], 64, axis=1).reshape(128, 256)
        put(nm, a)
    put("first", np.full((128, 1), 1.0 if core == 0 else 0.0))
    return c


def _build_nab(rpb):
    rpb = np.asarray(rpb, np.float32)
    kc = np.arange(64)[:, None]
    cc = np.arange(64)[None, :]
    idx = np.clip(kc - cc + 15, 0, 30)
    g = rpb[:, :, idx]
    out = np.zeros((2, 64, 8, 14, 64), np.float32)
    for half in range(2):
        out[half] = np.transpose(g[:, half:half + 14], (2, 0, 1, 3))
    return np.ascontiguousarray(out.reshape(128, 8 * 14 * 64))


class Ctx:
    NDS = 24

    def __init__(self, nc):
        self.nc = nc
        self.eng = {"pe": nc.tensor, "dve": nc.vector, "act": nc.scalar, "pool": nc.gpsimd,
                    "sp": nc.sync}
        self.sem = {e: nc.alloc_semaphore(name=f"pg_{e}_0") for e in self.eng}
        self.epoch = 0
        self.cnt = {e: 0 for e in self.eng}
        self.seen = {e: {} for e in self.eng}
        self.lastw = {}
        self.readers = {}
        self.dsems = [nc.alloc_semaphore(name=f"dq{i}") for i in range(self.NDS)]
        self.dcnt = [0] * self.NDS
        self.dnext = 0
        self.nins = 0
        self.trace = {e: [] for e in self.eng}

    def _semof(self, src):
        return self.sem[src[1]] if src[0] == "e" else self.dsems[src[1]]

    def _wait(self, e, src, val):
        if val <= 0:
            return
        if self.seen[e].get(src, 0) >= val:
            return
        self.eng[e].wait_ge(self._semof(src), val)
        self.trace[e].append(("w", (src, self.epoch if src[0] == "e" else 0), val))
        self.seen[e][src] = val

    def _deps(self, R, W):
        deps = {}

        def add(st):
            if st is None:
                return
            s, v = st
            if deps.get(s, 0) < v:
                deps[s] = v
        for k in R:
            add(self.lastw.get(k))
        for k in W:
            add(self.lastw.get(k))
            for s, v in self.readers.get(k, {}).items():
                add((s, v))
        return deps

    def _upd(self, R, W, stamp):
        for k in W:
            self.lastw[k] = stamp
            self.readers[k] = {}
        for k in R:
            d = self.readers.setdefault(k, {})
            if d.get(stamp[0], 0) < stamp[1]:
                d[stamp[0]] = stamp[1]

    def op(self, e, fn, R=(), W=(), inc=True):
        for s, v in self._deps(R, W).items():
            if e == "pe" and s == ("e", "pe"):
                continue
            self._wait(e, s, v)
        ins = fn(self.eng[e])
        self.nins += 1
        if inc:
            self.cnt[e] += 1
            ins.then_inc(self.sem[e], 1)
            self.trace[e].append(("i", ((("e", e)), self.epoch), 1))
            stamp = (("e", e), self.cnt[e])
        else:
            stamp = (("e", e), self.cnt[e] + 1)
        self._upd(R, W, stamp)
        return ins

    def dma(self, q, out, in_, R=(), W=()):
        i = self.dnext
        self.dnext = (self.dnext + 1) % self.NDS
        self._wait(q, ("d", i), self.dcnt[i])
        for s, v in self._deps(R, W).items():
            self._wait(q, s, v)
        self.eng[q].dma_start(out=out, in_=in_).then_inc(self.dsems[i], 16)
        self.trace[q].append(("i", (("d", i), 0), 16))
        self.nins += 1
        self.dcnt[i] += 16
        self._upd(R, W, (("d", i), self.dcnt[i]))

    def barrier(self):
        for e in self.eng:
            for e2 in self.eng:
                if e2 != e:
                    self._wait(e, ("e", e2), self.cnt[e2])
            for i in range(self.NDS):
                self._wait(e, ("d", i), self.dcnt[i])
        self.lastw = {}
        self.readers = {}

    def maybe_rotate(self, limit=24000):
        if max(self.cnt.values()) < limit:
            return
        self.barrier()
        self.epoch += 1
        for e in self.eng:
            self.sem[e] = self.nc.alloc_semaphore(name=f"pg_{e}_{self.epoch}")
            self.cnt[e] = 0
        for e in self.eng:
            for e2 in self.eng:
                self.seen[e].pop(("e", e2), None)


class _Stop(Exception):
    pass


def build_program(debug=None):
    nc = bass.Bass("TRN2", target_bir_lowering=False)
    dt = lambda n, s: nc.dram_tensor(n, s, F32, kind="ExternalInput").ap()
    xs = dt("xs", [3 * EXT, D])
    pp = dt("pp", [3 * SEQT, 256])
    cst_d = dt("cst", [128, NCST])
    nab_d = dt("nab", [128, 8 * 14 * 64])
    msk_d = dt("msk", [128, 1280])
    w_in = dt("w_in", [D, DIN])
    w_lora = dt("w_lora", [128, 2 * 512])
    g2_d = dt("g2", [128, 512])
    w_bra = dt("w_br_a", [512, D])
    w_brn = dt("w_br_n", [512, D])
    w_out = dt("w_out", [D, D])
    w_gate = dt("w_gate", [D, DFF])
    w_up = dt("w_up", [D, DFF])
    w_down = dt("w_down", [DFF, D])
    w_ple = dt("w_ple", [256, D])
    w_pg = dt("w_pg", [D, D])
    xo = dt("xo", [7 * SLOT_EXT, D])
    slp_d = dt("slp", [128, 7 * 40 + 8])
    y_d = nc.dram_tensor("y", [3 * SEQT, D], F32, kind="ExternalOutput").ap()

    C = Ctx(nc)
    ES = ExitStack()
    dbg_d = None
    if debug:
        dbg_d = nc.dram_tensor("dbg", [128, debug.get("n", 8 * EXT)], BF if debug.get("bf", True) else F32, kind="ExternalOutput").ap()

    def dbg_dump(tag, ap2d):
        if debug and debug.get("stop") == tag:
            C.barrier()
            C.dma("sp", dbg_d[:, 0:ap2d.shape[1]], ap2d, R=[], W=[])
            C.barrier()
            ex = _Stop()
            ex.nc = nc
            raise ex

    uid = {"n": 0}

    def sb(name, shape, dtype=F32, es=ES):
        uid["n"] += 1
        return es.enter_context(nc.sbuf_tensor(f"{name}_u{uid['n']}", shape, dtype))

    class PSV:
        def __init__(self, name, ap):
            self.name = name
            self.ap = ap

        def __getitem__(self, idx):
            return self.ap[idx]

    class PSD:
        def __init__(self, name):
            self.t = nc.alloc_psum_tensor(name, [128, 1024], F32)
            self.b = [PSV(f"{name}_b{h}", self.t[:, h * 512:(h + 1) * 512]) for h in range(2)]

    QA, QB, QX = PSD("qa"), PSD("qb"), PSD("qx")
    PA = QA.b + QB.b
    PX = QX.b
    PS6 = PSV("ps6", nc.alloc_psum_tensor("ps6t", [128, 512], F32)[:, :])
    PT = [PSV("pt0", nc.alloc_psum_tensor("pt0t", [128, 1024], BF)[:, :])]
    for t in PA + PX + [PS6]:
        C.op("dve", lambda e, t=t: e.memset(t[:], 0.0), W=[("ps", t.name)])

    def dk(Q):
        return [("ps", Q.b[0].name), ("ps", Q.b[1].name)]

    def pk(t, sub=None):
        return ("ps", t.name) if sub is None else ("ps", t.name, sub)

    cs = sb("cs", [128, NCST])
    C.dma("sp", cs[:], cst_d, W=["cs"])

    def cv(name, a=0, b=None):
        o, w = CST_OFF[name]
        return cs[:, o + a:o + (w if b is None else b)]

    identb = sb("identb", [128, 128], BF)
    bonesb = sb("bonesb", [128, 128], BF)
    onesb = sb("onesb", [128, 128], BF)
    C.op("dve", lambda e: e.tensor_copy(out=identb[:], in_=cv("ident")), R=["cs"], W=["identb"])
    C.op("dve", lambda e: e.tensor_copy(out=bonesb[:], in_=cv("bones")), R=["cs"], W=["bonesb"])
    C.op("dve", lambda e: e.tensor_copy(out=onesb[:], in_=cv("ones")), R=["cs"], W=["onesb"])
    dv = sb("dv", [128, 64])
    c0 = dv[:, 0:15]
    omka = dv[:, 16:20]
    hrk = dv[:, 20:24]
    C.op("dve", lambda e: e.tensor_tensor(out=c0, in0=cv("mp"), in1=cv("mn"), op=ALU.add), R=["cs"], W=["dv0"])
    C.op("dve", lambda e: e.tensor_scalar(out=c0, in0=c0, scalar1=-1.0, scalar2=1.0, op0=ALU.mult, op1=ALU.add), R=["dv0"], W=["dv0"])
    C.op("dve", lambda e: e.tensor_scalar(out=omka, in0=cv("ka"), scalar1=-1.0, scalar2=1.0, op0=ALU.mult, op1=ALU.add), R=["cs"], W=["dv1"])
    C.op("dve", lambda e: e.tensor_scalar(out=hrk, in0=cv("rk"), scalar1=0.5, scalar2=None, op0=ALU.mult), R=["cs"], W=["dv2"])
    DVK = ["dv0", "dv1", "dv2"]
    epsc = sb("epsc", [128, 4])
    C.op("dve", lambda e: e.memset(epsc[:, 0:1], 1e-6), W=["epsc"])
    C.op("dve", lambda e: e.memset(epsc[:, 1:2], 1e-24), W=["epsc"])
    C.op("dve", lambda e: e.memset(epsc[:, 2:3], 64e-5), W=["epsc"])
    C.op("dve", lambda e: e.memset(epsc[:, 3:4], 0.0), W=["epsc"])

    mskb = sb("mskb", [128, 1280], BF)
    C.dma("pool", mskb[:], msk_d, W=["mskb"])
    MSK = {"mXf": mskb[:, 0:384], "mYf": mskb[:, 384:640], "mXb": mskb[:, 640:1024], "mYb": mskb[:, 1024:1280]}
    lorab = sb("lorab", [128, 1024], BF)
    g2b = sb("g2b", [128, 512], BF)
    C.dma("pool", lorab[:], w_lora, W=["lorab"])
    C.dma("pool", g2b[:], g2_d, W=["g2b"])

    def build_epb(es0):
        epb = sb("epb", [128, 8 * 14 * 64], BF, es0)
        stg = sb("nabstg", [128, 1792], F32, es0)
        for q in range(4):
            C.dma("sp", stg[:], nab_d[:, q * 1792:(q + 1) * 1792], W=["stg"])
            C.op("act", lambda e: e.activation(out=stg[:], in_=stg[:], func=AF.Exp), R=["stg"], W=["stg"])
            C.op("dve", lambda e, q=q: e.tensor_tensor(
                out=epb[:, q * 1792:(q + 1) * 1792].rearrange("p (g c) -> p g c", c=64),
                in0=stg[:].rearrange("p (g c) -> p g c", c=64),
                in1=cv("colmask").unsqueeze(1).to_broadcast([128, 28, 64]), op=ALU.mult),
                R=["stg", "cs"], W=["epb"])
        return epb[:].rearrange("p (h d c) -> p h d c", h=8, d=14)

    dbg_dump("p0", lorab[:])
    hT = sb("hT", [128, 8, EXT], BF)
    UnT = sb("UnT", [128, 4, SEQT], BF)
    UaT = sb("UaT", [128, 4, SEQT], BF)
    wbuf = [None, None]

    def alloc_wbuf(es, tag, width):
        for i in range(2):
            wbuf[i] = sb(f"wbuf_{tag}_{i}", [128, 8, width], BF, es)
    wstate = {"i": 0}

    def load_w(src, r0, c0_, ncols, nk=8):
        i = wstate["i"]
        wstate["i"] = 1 - i
        t = wbuf[i]
        v = src[r0:r0 + nk * 128, c0_:c0_ + ncols].rearrange("(k p) c -> p k c", p=128)
        C.dma("pool", t[:, 0:nk, 0:ncols], v, W=[f"wbuf{i}"])
        return t, f"wbuf{i}"

    evac_rr = {"i": 0}

    def evac(out, in_, R, W, scale=None):
        evac_rr["i"] ^= 1
        if evac_rr["i"]:
            C.op("act", lambda e: e.activation(out=out, in_=in_, func=AF.Copy), R=R, W=W)
        else:
            C.op("dve", lambda e: e.tensor_copy(out=out, in_=in_), R=R, W=W)

    def mm(out, lhsT, rhs, start, stop, R, W, inc=False):
        return C.op("pe", lambda e: e.matmul(out, lhsT=lhsT, rhs=rhs, start=start, stop=stop,
                                             skip_group_check=True), R=R, W=W, inc=inc)

    def tp(out, in_, R, W, inc=False):
        b0 = in_.base_partition()
        n0 = in_.shape[0]
        return C.op("pe", lambda e: e.transpose(out=out, in_=in_, identity=identb[b0:b0 + n0, b0:b0 + n0]),
                    R=list(R) + ["identb"], W=W, inc=inc)

    st_ = sb("p1st", [128, 4], F32)
    Ssave = sb("Ssave", [128, 2, 4, 64], F32)
    slp = sb("slp", [128, 7 * 40 + 8], F32)
    C.dma("sp", slp[:], slp_d, W=["slp"])

    def fill_hT(src, row0, blks, xt_t, xb_t, kx, kb):
        for n_, blk in enumerate(blks):
            i = n_ % len(xt_t)
            xt_a, xb_a = xt_t[i], xb_t[i]
            kxi, kbi = kx[i], kb[i]
            C.dma("sp", xt_a, src[row0 + n_ * 128: row0 + (n_ + 1) * 128, :], W=[kxi])
            C.op("act", lambda e, xt_a=xt_a, xb_a=xb_a, i=i: e.activation(out=xb_a, in_=xt_a, func=AF.Square,
                                                                  accum_out=st_[:, 2 * i:2 * i + 1]),
                 R=[kxi], W=[kbi, f"p1st{i}"])
            C.op("act", lambda e, i=i: e.activation(out=st_[:, 2 * i + 1:2 * i + 2], in_=st_[:, 2 * i:2 * i + 1], func=AF.Sqrt,
                                                    bias=epsc[:, 0:1], scale=1.0 / D),
                 R=[f"p1st{i}", "epsc"], W=[f"p1st{i}b"])
            C.op("dve", lambda e, i=i: e.reciprocal(out=st_[:, 2 * i + 1:2 * i + 2], in_=st_[:, 2 * i + 1:2 * i + 2]),
                 R=[f"p1st{i}b"], W=[f"p1st{i}b"])
            C.op("dve", lambda e, xt_a=xt_a, xb_a=xb_a, i=i: e.tensor_scalar(out=xb_a, in0=xt_a, scalar1=st_[:, 2 * i + 1:2 * i + 2],
                                                                     scalar2=None, op0=ALU.mult),
                 R=[kxi, f"p1st{i}b"], W=[kbi])
            pt = PT[0]
            for kc in range(8):
                tp(pt[:, kc * 128:(kc + 1) * 128], xb_a[:, kc * 128:(kc + 1) * 128],
                   R=[kbi], W=[pk(pt)], inc=(kc == 7))
            C.op("dve", lambda e, pt=pt, blk=blk: e.tensor_tensor(
                out=hT[:, :, blk * 128:(blk + 1) * 128],
                in0=pt[:].rearrange("p (k t) -> p k t", k=8),
                in1=cv("gmix").unsqueeze(2).to_broadcast([128, 8, 128]), op=ALU.mult),
                R=[pk(pt), "cs"], W=[("hT", blk // 4)])

    try:
      for s in (debug["seqs"] if debug and "seqs" in debug else range(3)):
          xoff = s * EXT
          with ExitStack() as es1:
              xt = [sb(f"p1x{i}", [128, D], F32, es1) for i in range(2)]
              xb = [sb(f"p1xb{i}", [128, D], BF, es1) for i in range(2)]
              fill_hT(xs, xoff, list(range(EXT // 128)), [t[:] for t in xt], [t[:] for t in xb],
                      ["p1x0", "p1x1"], ["p1xb0", "p1xb1"])
              dbg_dump("p1", hT[:].rearrange("p k t -> p (k t)"))
              C.barrier()

          with ExitStack() as es2:
              alloc_wbuf(es2, f"p2s{s}", 512)
              epb4 = build_epb(es2)
              qT = sb("qT", [128, 4, SEQT], BF, es2)
              kT = sb("kT", [128, 4, EXT], BF, es2)
              Vt = sb("Vt", [128, EXT // 128, 512], BF, es2)
              Eb = [sb(f"Eb{j}", [128, 512], BF, es2) for j in range(6)]
              Pb = [sb(f"Pb{j}", [128, 512], BF, es2) for j in range(6)]
              rinv = sb("rinv", [128, 256], F32, es2)
              wt, wk = load_w(w_in, 0, 1920, 512)
              ai = 0
              for t5 in range(NTILE):
                  for cc in range(4):
                      pa = PA[ai % 4]; ai += 1
                      for kc in range(8):
                          mm(pa[:], wt[:, kc, cc * 128:(cc + 1) * 128], hT[:, kc, HALO + t5 * TT: HALO + (t5 + 1) * TT],
                             kc == 0, kc == 7, R=[wk, ("hT", (HALO + t5 * TT) // 512), ("hT", (HALO + t5 * TT) // 512 + 1)],
                             W=[pk(pa)], inc=(kc == 7))
                      evac(qT[:, cc, t5 * TT:(t5 + 1) * TT], pa[:], R=[pk(pa)], W=[("qT", t5)])
              wt, wk = load_w(w_in, 0, 2432, 512)
              for t5 in range(EXT // TT):
                  for cc in range(4):
                      pa = PA[ai % 4]; ai += 1
                      for kc in range(8):
                          mm(pa[:], wt[:, kc, cc * 128:(cc + 1) * 128], hT[:, kc, t5 * TT:(t5 + 1) * TT],
                             kc == 0, kc == 7, R=[wk, ("hT", t5)], W=[pk(pa)], inc=(kc == 7))
                      evac(kT[:, cc, t5 * TT:(t5 + 1) * TT], pa[:], R=[pk(pa)], W=[("kT", t5)])
              wt, wk = load_w(w_in, 0, 2944, 512)
              for blk in range(EXT // 128):
                  pa = PA[ai % 4]; ai += 1
                  for kc in range(8):
                      mm(pa[:], hT[:, kc, blk * 128:(blk + 1) * 128], wt[:, kc, :], kc == 0, kc == 7,
                         R=[wk, ("hT", blk // 4)], W=[pk(pa)], inc=(kc == 7))
                  evac(Vt[:, blk, :], pa[:], R=[pk(pa)], W=[("Vt", blk)])
              for r in range(32):
                  m0 = min(max(r // 2, 0), 14)
                  q0 = r * 64
                  for j in range(6):
                      m = m0 + j
                      d = 2 * m - r + 3
                      Q = (QA, QB)[j % 2]
                      for h in range(8):
                          b = 64 * (h % 2)
                          o = (h % 2) * 512 + (h // 2) * 64
                          mm(Q.t[:, o:o + 64], kT[b:b + 64, h // 2, m * 128:(m + 1) * 128],
                             qT[b:b + 64, h // 2, q0:q0 + 64], True, True,
                             R=[("kT", m // 4), ("qT", r // 8)], W=dk(Q), inc=(h == 7))
                      o, _ = CST_OFF["narm"]
                      col = o + (s * 32 + r) * 6 + j
                      C.op("act", lambda e, Q=Q, j=j, col=col: e.activation(
                          out=Eb[j][:].rearrange("p (b c) -> p b c", b=2),
                          in_=Q.t[:].rearrange("p (b c) -> p b c", b=2)[:, :, 0:256],
                          func=AF.Exp, bias=cs[:, col:col + 1], scale=0.125),
                          R=dk(Q) + ["cs"], W=[f"Eb{j}"])
                      C.op("pool", lambda e, j=j, d=d: e.tensor_tensor(
                          out=Pb[j][:].rearrange("p (b hh c) -> p b hh c", b=2, hh=4),
                          in0=Eb[j][:].rearrange("p (b hh c) -> p b hh c", b=2, hh=4),
                          in1=epb4[:, :, d, :].rearrange("p (hh b) c -> p b hh c", b=2), op=ALU.mult),
                          R=[f"Eb{j}", "epb"], W=[f"Pb{j}"])
                  pv = PX[0]
                  sm = PX[1]
                  for h in range(8):
                      b = 64 * (h % 2)
                      for j in range(6):
                          mm(pv[b:b + 64, (h // 2) * 64:(h // 2 + 1) * 64], Vt[:, m0 + j, h * 64:(h + 1) * 64],
                             Pb[j][:, (h % 2) * 256 + (h // 2) * 64:(h % 2) * 256 + (h // 2) * 64 + 64], j == 0, j == 5,
                             R=[("Vt", m0 + j), f"Pb{j}"], W=[pk(pv)], inc=False)
                  for h in range(8):
                      b = 64 * (h % 2)
                      for j in range(6):
                          mm(sm[b:b + 64, (h // 2) * 64:(h // 2 + 1) * 64], onesb[:, 0:64],
                             Pb[j][:, (h % 2) * 256 + (h // 2) * 64:(h % 2) * 256 + (h // 2) * 64 + 64], j == 0, j == 5,
                             R=["onesb", f"Pb{j}"], W=[pk(sm)], inc=(h == 7 and j == 5))
                  C.op("act", lambda e: e.activation(out=rinv[:], in_=sm[:, 0:256], func=AF.Ln),
                       R=[pk(sm)], W=["rinv"])
                  C.op("act", lambda e: e.activation(out=rinv[:], in_=rinv[:], func=AF.Exp, scale=-1.0),
                       R=["rinv"], W=["rinv"])
                  C.op("dve", lambda e, q0=q0: e.tensor_tensor(
                      out=UnT[:, :, q0:q0 + 64], in0=pv[:, 0:256].rearrange("p (g c) -> p g c", g=4),
                      in1=rinv[:].rearrange("p (g c) -> p g c", g=4), op=ALU.mult),
                      R=[pk(pv), "rinv"], W=[("UnT", r // 8)])
              dbg_dump("p2", UnT[:].rearrange("p k t -> p (k t)"))
              C.barrier()
          C.maybe_rotate()

          with ExitStack() as es3:
              alloc_wbuf(es3, f"p3s{s}", 384)
              Ofb = sb("Ofb", [128, 32, 4, 64], BF, es3)
              sbon = sb("sbon", [128, 32, 4], F32, es3)
              Sst = sb("Sst", [128, 4, 64], F32, es3)
              Sbf = sb("Sbf", [128, 4, 64], BF, es3)
              Stmp = sb("Stmp", [128, 4, 64], F32, es3)
              zr = [sb("zr0", [128, TT + 2], F32, es3)]
              zsf = {n: sb(f"zs_{n}", [128, TT], F32, es3) for n in ("r", "k", "v")}
              lzs = sb("lzs", [128, TT], F32, es3)
              thb = sb("thb", [128, TT], BF, es3)
              adb = sb("adb", [128, TT], BF, es3)
              sgb = sb("sgb", [128, TT], BF, es3)
              cumx = sb("cumx", [128, TT + 1], F32, es3)
              sgw = sb("sgw", [128, TT], F32, es3)
              ar = sb("ar", [128, TT], F32, es3)
              dd = [sb(f"dd{i}", [128, TT], F32, es3) for i in range(3)]
              g0t = sb("g0t", [128, TT], F32, es3)
              ksq = sb("ksq", [128, TT], BF, es3)
              kkn = sb("kkn", [128, TT], F32, es3)
              prodb = sb("prodb", [128, TT], BF, es3)
              vb = sb("vb", [128, TT], BF, es3)
              ARt = sb("ARt", [128, 4, 8, 128], BF, es3)
              Btt = sb("Btt", [128, 4, TT], BF, es3)
              Ktt = sb("Ktt", [128, 4, TT], BF, es3)
              Bht = sb("Bht", [128, 4, TT], BF, es3)
              Kht = sb("Kht", [128, 4, TT], BF, es3)
              GCt = sb("GCt", [128, 4, 8], F32, es3)
              Vst = sb("Vst", [128, 4, 8, 64], BF, es3)
              BhT = sb("BhT", [128, 4, 8, 64], BF, es3)
              KhT = sb("KhT", [128, 4, 8, 64], BF, es3)
              Lp = [sb(f"Lp{p}", [128, 512], BF, es3) for p in range(4)]
              Ygb = sb("Ygb", [128, 4, 256], BF, es3)
              Upb = sb("Upb", [128, 4, 64], BF, es3)
              Ubz = sb("Ubz", [128, 2, 4, 64], BF, es3)
              fina = sb("fina", [128, 4, 4, 64], F32, es3)
              finb = sb("finb", [128, 4, 4, 64], F32, es3)
              fst = sb("fst", [128, 2, 16], F32, es3)
              uab = sb("uab", [128, 4, 4, 64], BF, es3)

              ones_bc = cv("ones")[:, 0:1].to_broadcast([128, TT])
              C.op("pool", lambda e: e.memset(cumx[:, 0:1], 0.0), W=["cumx0"])
              C.op("pool", lambda e: e.memset(sbon[:], 0.0), W=["sbon"])
              for p_ in range(4):
                  C.op("pool", lambda e, p_=p_: e.memset(Lp[p_][:], 0.0), W=[("LT", p_), ("LP", p_), ("LM", p_)])
              C.op("pool", lambda e: e.memset(Ubz[:], 0.0), W=["Ub"])
              C.op("pool", lambda e: e.memset(Ygb[:], 0.0), W=[("Ygb", p) for p in range(4)])

              def zproj(wt, wk, wc, e0, ci, dst, after=None, mus=None):
                  pa = PA[zproj.i % 2]
                  zproj.i += 1
                  px = PX[0]
                  z = zr[0]
                  zk = "zr0"
                  hk = [("hT", (e0 - 1) // 512), ("hT", e0 // 512), ("hT", min((e0 + 512) // 512, 4))]
                  for kc in range(8):
                      mm(pa[:], wt[:, kc, wc:wc + 128], hT[:, kc, e0:e0 + TT], kc == 0, kc == 7,
                         R=[wk] + hk, W=[pk(pa)], inc=False)
                  hbv = hT[:, :, e0 - 1:e0 + TT + 1]
                  for kc in range(8):
                      mm(px[:, 0:2], wt[:, kc, wc:wc + 128], hbv[:, kc, 0:TT + 2:TT + 1], kc == 0, kc == 7,
                         R=[wk] + hk, W=[pk(px)], inc=(kc == 7))
                  C.op("act", lambda e: e.activation(out=z[:, 1:TT + 1], in_=pa[:], func=AF.Copy),
                       R=[pk(pa)], W=[zk])
                  C.op("dve", lambda e: e.tensor_copy(out=z[:, 0:TT + 2:TT + 1], in_=px[:, 0:2]),
                       R=[pk(px)], W=[zk])
                  mpc = cv("mp", ci, ci + 1) if mus is None else mus[0][:, ci:ci + 1]
                  mnc = cv("mn", ci, ci + 1) if mus is None else mus[1][:, ci:ci + 1]
                  C.op("dve", lambda e: e.tensor_scalar(out=dst, in0=z[:, 1:TT + 1], scalar1=c0[:, ci:ci + 1],
                                                        scalar2=None, op0=ALU.mult), R=[zk] + DVK, W=[after])
                  C.op("dve", lambda e: e.scalar_tensor_tensor(out=dst, in0=z[:, 0:TT], scalar=mpc, in1=dst,
                                                               op0=ALU.mult, op1=ALU.add), R=[zk, "cs", "slp", after], W=[after])
                  C.op("dve", lambda e: e.scalar_tensor_tensor(out=dst, in0=z[:, 2:TT + 2], scalar=mnc, in1=dst,
                                                               op0=ALU.mult, op1=ALU.add), R=[zk, "cs", "slp", after], W=[after])
              zproj.i = 0
              c3 = lambda a: a.rearrange("p (c t) -> p c t", t=64)

              def finalize(t5, hf):
                  cb = t5 * 8 + hf * 4
                  A_ = fina[:].rearrange("p c a v -> p (c a) v")
                  B_ = finb[:].rearrange("p c a v -> p (c a) v")
                  mean = fst[:, 0, :]
                  var = fst[:, 1, :]
                  C.op("dve", lambda e: e.tensor_reduce(out=mean, in_=A_, axis=AX.X, op=ALU.add), R=["fina"], W=["fst0"])
                  C.op("dve", lambda e: e.tensor_scalar(out=mean, in0=mean, scalar1=1.0 / 64, scalar2=None, op0=ALU.mult), R=["fst0"], W=["fst0"])
                  C.op("dve", lambda e: e.tensor_tensor(out=A_, in0=A_, in1=mean.unsqueeze(2).to_broadcast([128, 16, 64]), op=ALU.subtract),
                       R=["fina", "fst0"], W=["fina"])
                  C.op("pool", lambda e: e.tensor_tensor(out=B_, in0=A_, in1=A_, op=ALU.mult), R=["fina"], W=["finb"])
                  C.op("dve", lambda e: e.tensor_reduce(out=var, in_=B_, axis=AX.X, op=ALU.add), R=["finb"], W=["fst1"])
                  C.op("act", lambda e: e.activation(out=var, in_=var, func=AF.Sqrt, bias=epsc[:, 2:3], scale=1.0 / 64), R=["fst1", "epsc"], W=["fst1"])
                  C.op("dve", lambda e: e.reciprocal(out=var, in_=var), R=["fst1"], W=["fst1"])
                  C.op("dve", lambda e: e.tensor_tensor(out=A_, in0=A_, in1=var.unsqueeze(2).to_broadcast([128, 16, 64]), op=ALU.mult),
                       R=["fina", "fst1"], W=["fina"])
                  lnw = cv("lnw").rearrange("p (a v) -> p a v", a=4).unsqueeze(1).to_broadcast([128, 4, 4, 64])
                  lnb = cv("lnb").rearrange("p (a v) -> p a v", a=4).unsqueeze(1).to_broadcast([128, 4, 4, 64])
                  C.op("pool", lambda e: e.tensor_tensor(out=fina[:], in0=fina[:], in1=lnw, op=ALU.mult), R=["fina", "cs"], W=["fina"])
                  C.op("pool", lambda e: e.tensor_tensor(out=fina[:], in0=fina[:], in1=lnb, op=ALU.add), R=["fina", "cs"], W=["fina"])
                  sb_ = sbon[:, cb:cb + 4, :].unsqueeze(3).to_broadcast([128, 4, 4, 64])
                  VK = [("Vst", p) for p in range(4)]
                  C.op("dve", lambda e: e.tensor_tensor(out=finb[:], in0=Vst[:, :, hf * 4:hf * 4 + 4, :].rearrange("p a c v -> p c a v"), in1=sb_, op=ALU.mult),
                       R=VK + ["sbon"], W=["finb"])
                  C.op("pool", lambda e: e.tensor_tensor(out=fina[:], in0=fina[:], in1=finb[:], op=ALU.add), R=["fina", "finb"], W=["fina"])
                  for c2 in range(2):
                      pg = PA[c2 % 2]
                      for cc in range(2):
                          c = hf * 4 + c2 * 2 + cc
                          for p in range(4):
                              for h in range(2):
                                  o = (cc * 4 + p) * 64
                                  mm(pg[64 * h:64 * h + 64, o:o + 64], sgb[:, c * 64:(c + 1) * 64],
                                     g2b[:, (2 * p + h) * 64:(2 * p + h + 1) * 64], True, True,
                                     R=["sgb", "g2b"], W=[pk(pg)], inc=(cc == 1 and p == 3 and h == 1))
                      C.op("dve", lambda e, pg=pg, c2=c2: e.tensor_tensor(
                          out=uab[:, c2 * 2:c2 * 2 + 2].rearrange("p c a v -> p (c a v)"),
                          in0=fina[:, c2 * 2:c2 * 2 + 2].rearrange("p c a v -> p (c a v)"),
                          in1=pg[:], op=ALU.mult), R=["fina", pk(pg)], W=["uab"])
                  for p in range(4):
                      for c in range(4):
                          for h in range(2):
                              mm(PS6[64 * h:64 * h + 64, c * 64:(c + 1) * 64], uab[:, c, p, :], identb[:, 64 * h:64 * h + 64], True, True,
                                 R=["uab", "identb"], W=[pk(PS6)], inc=(c == 3 and h == 1))
                      evac(UaT[:, p, t5 * TT + hf * 256:t5 * TT + hf * 256 + 256], PS6[:, 0:256], R=[pk(PS6)], W=[("UaT", t5)])

              def rwkv_pass(dr, state_only=False, init="zero", slot=None):
                  fwd = (dr == 0)
                  mus = None
                  if slot is not None:
                      sp_ = slp[:, slot * 40:(slot + 1) * 40]
                      mus = (sp_[:, 8:23], sp_[:, 23:38])
                      dmask = sp_[:, 38:39]
                  if init == "zero":
                      C.op("dve", lambda e: e.memset(Sst[:], 0.0), W=["Sst"])
                      C.op("dve", lambda e: e.memset(Sbf[:], 0.0), W=["Sbf"])
                  elif init == "saved":
                      C.op("dve", lambda e: e.tensor_copy(out=Sst[:], in_=Ssave[:, dr]), R=[("Ssave", dr)], W=["Sst"])
                      C.op("act", lambda e: e.activation(out=Sbf[:], in_=Ssave[:, dr], func=AF.Copy), R=[("Ssave", dr)], W=["Sbf"])
                  tiles = range(NTILE) if fwd else range(NTILE - 1, -1, -1)
                  mX = MSK["mXf" if fwd else "mXb"]
                  mY = MSK["mYf" if fwd else "mYb"]
                  pb = 0 if fwd else 64
                  for t5 in tiles:
                      e0 = HALO + t5 * TT
                      wt, wk = load_w(w_in, 0, 1536, 384)
                      zproj(wt, wk, 0, e0, 12, lzs[:], "lzs", mus=mus)
                      C.op("act", lambda e: e.activation(out=thb[:], in_=lzs[:], func=AF.Tanh), R=["lzs"], W=["thb"])
                      if slot is not None:
                          C.op("dve", lambda e: e.tensor_scalar(out=thb[:], in0=thb[:], scalar1=dmask, scalar2=None, op0=ALU.mult),
                               R=["thb", "slp"], W=["thb"])
                      zproj(wt, wk, 128, e0, 13, lzs[:], "lzs", mus=mus)
                      if slot is not None:
                          C.op("dve", lambda e: e.tensor_scalar(out=adb[:], in0=lzs[:], scalar1=dmask, scalar2=None, op0=ALU.mult),
                               R=["lzs", "slp"], W=["adb"])
                      else:
                          C.op("pool", lambda e: e.tensor_copy(out=adb[:], in_=lzs[:]), R=["lzs"], W=["adb"])
                      dbg_dump("s1", thb[:])
                      if (not fwd) and not state_only:
                          zproj(wt, wk, 256, e0, 14, lzs[:], "lzs")
                          C.op("act", lambda e: e.activation(out=sgb[:], in_=lzs[:], func=AF.Sigmoid), R=["lzs"], W=["sgb"])
                      for p in range(4):
                          for n, base in (("r", 0), ("k", 512), ("v", 1024)):
                              if state_only and n == "r":
                                  continue
                              wt, wk = load_w(w_in, 0, base + p * 128, 128)
                              zproj(wt, wk, 0, e0, (base // 128) + p, zsf[n][:], f"zs_{n}", mus=mus)
                          pa = PA[2]
                          if slot is None:
                              mm(pa[:], lorab[pb:pb + 64, p * 128:(p + 1) * 128], thb[pb:pb + 64, :], True, True,
                                 R=["lorab", "thb"], W=[pk(pa)], inc=True)
                              w0ap = cv("w0f" if fwd else "w0b", p, p + 1)
                              a0ap = cv("a0f" if fwd else "a0b", p, p + 1)
                          else:
                              mm(pa[:], lorab[:, p * 128:(p + 1) * 128], thb[:, :], True, True,
                                 R=["lorab", "thb"], W=[pk(pa)], inc=True)
                              w0ap = sp_[:, p:p + 1]
                              a0ap = sp_[:, 4 + p:5 + p]
                          C.op("act", lambda e, pa=pa, p=p, w0ap=w0ap: e.activation(
                              out=sgw[:], in_=pa[:], func=AF.Sigmoid,
                              bias=w0ap, scale=1.0), R=[pk(pa), "cs", "slp"], W=["sgw"])
                          pa2 = PA[3]
                          if slot is None:
                              mm(pa2[:], lorab[pb:pb + 64, 512 + p * 128:512 + (p + 1) * 128], adb[pb:pb + 64, :], True, True,
                                 R=["lorab", "adb"], W=[pk(pa2)], inc=True)
                          else:
                              mm(pa2[:], lorab[:, 512 + p * 128:512 + (p + 1) * 128], adb[:, :], True, True,
                                 R=["lorab", "adb"], W=[pk(pa2)], inc=True)
                          C.op("act", lambda e, pa2=pa2, p=p, a0ap=a0ap: e.activation(
                              out=ar[:], in_=pa2[:], func=AF.Sigmoid,
                              bias=a0ap, scale=1.0), R=[pk(pa2), "cs", "slp"], W=["ar"])
                          C.op("dve", lambda e: e.tensor_tensor_scan(out=cumx[:, 1:TT + 1], data0=ones_bc, data1=sgw[:],
                                                                     initial=0.0, op0=ALU.mult, op1=ALU.add),
                               R=["cs", "sgw"], W=["cumx"])
                          cst_ = c3(cumx[:, 0:TT])[:, :, 0:1].to_broadcast([128, 8, 64])
                          cen_ = c3(cumx[:, 1:TT + 1])[:, :, 63:64].to_broadcast([128, 8, 64])
                          cK = ["cumx", "cumx0"]
                          if fwd:
                              C.op("pool", lambda e: e.tensor_tensor(out=c3(dd[0][:]), in0=c3(cumx[:, 1:TT + 1]), in1=cst_, op=ALU.subtract), R=cK, W=["dd0"])
                              C.op("pool", lambda e: e.tensor_tensor(out=c3(dd[1][:]), in0=c3(cumx[:, 0:TT]), in1=cst_, op=ALU.subtract), R=cK, W=["dd1"])
                              C.op("pool", lambda e: e.tensor_tensor(out=c3(dd[2][:]), in0=cen_, in1=c3(cumx[:, 1:TT + 1]), op=ALU.subtract), R=cK, W=["dd2"])
                          else:
                              C.op("pool", lambda e: e.tensor_tensor(out=c3(dd[0][:]), in0=cen_, in1=c3(cumx[:, 0:TT]), op=ALU.subtract), R=cK, W=["dd0"])
                              C.op("pool", lambda e: e.tensor_tensor(out=c3(dd[1][:]), in0=cen_, in1=c3(cumx[:, 1:TT + 1]), op=ALU.subtract), R=cK, W=["dd1"])
                              C.op("pool", lambda e: e.tensor_tensor(out=c3(dd[2][:]), in0=c3(cumx[:, 0:TT]), in1=cst_, op=ALU.subtract), R=cK, W=["dd2"])
                          C.op("act", lambda e: e.activation(out=g0t[:], in_=dd[0][:], func=AF.Exp, scale=-KAPPA), R=["dd0"], W=["g0t"])
                          C.op("act", lambda e: e.activation(out=dd[0][:], in_=dd[0][:], func=AF.Exp, scale=KAPPA), R=["dd0"], W=["dd0"])
                          C.op("act", lambda e: e.activation(out=dd[1][:], in_=dd[1][:], func=AF.Exp, scale=-KAPPA), R=["dd1"], W=["dd1"])
                          C.op("act", lambda e: e.activation(out=dd[2][:], in_=dd[2][:], func=AF.Exp, scale=-KAPPA), R=["dd2"], W=["dd2"])
                          Gt, Ginv, Gp, Gaft = g0t, dd[0], dd[1], dd[2]
                          dbg_dump("s2", thb[:])
                          gcol = 63 if fwd else 0
                          C.op("pool", lambda e, p=p: e.tensor_copy(out=GCt[:, p, :], in_=g0t[:, gcol:TT:64]), R=["g0t"], W=[("GCt", p)])
                          C.op("act", lambda e, p=p: e.activation(out=ksq[:], in_=zsf["k"][:], func=AF.Square,
                                                                  scale=cv("kk", p, p + 1)), R=["zs_k", "cs"], W=["ksq"])
                          pa = PA[2]
                          mm(pa[:], bonesb[:], ksq[:], True, True, R=["bonesb", "ksq"], W=[pk(pa)], inc=True)
                          C.op("act", lambda e, pa=pa: e.activation(out=sgw[:], in_=pa[:], func=AF.Ln, bias=epsc[:, 1:2], scale=1.0),
                               R=[pk(pa), "epsc"], W=["sgw"])
                          C.op("act", lambda e: e.activation(out=sgw[:], in_=sgw[:], func=AF.Exp, scale=-0.5), R=["sgw"], W=["sgw"])
                          C.op("dve", lambda e, p=p: e.scalar_tensor_tensor(out=kkn[:], in0=zsf["k"][:], scalar=cv("kk", p, p + 1),
                                                                            in1=sgw[:], op0=ALU.mult, op1=ALU.mult),
                               R=["zs_k", "cs", "sgw"], W=["kkn"])
                          ARp = ARt[:, p].rearrange("p c (two t) -> p c two t", two=2)
                          C.op("dve", lambda e: e.scalar_tensor_tensor(out=ARp[:, :, 0, :], in0=c3(kkn[:]), scalar=-1.0,
                                                                       in1=c3(Gp[:]), op0=ALU.mult, op1=ALU.mult),
                               R=["kkn", "dd1"], W=[("ARt", p)])
                          rg = Gt if fwd else Gp
                          if not state_only:
                              C.op("pool", lambda e: e.tensor_tensor(out=ARp[:, :, 1, :], in0=c3(zsf["r"][:]), in1=c3(rg[:]), op=ALU.mult),
                                   R=["zs_r", "g0t", "dd1"], W=[("ARt", p)])
                          C.op("pool", lambda e: e.tensor_tensor(out=kkn[:], in0=kkn[:], in1=ar[:], op=ALU.mult), R=["kkn", "ar"], W=["kkn"])
                          C.op("dve", lambda e, p=p: e.tensor_tensor(out=Btt[:, p, :], in0=kkn[:], in1=Ginv[:], op=ALU.mult), R=["kkn", "dd0"], W=[("Btt", p)])
                          C.op("pool", lambda e, p=p: e.tensor_tensor(out=Bht[:, p, :], in0=kkn[:], in1=Gaft[:], op=ALU.mult), R=["kkn", "dd2"], W=[("Bht", p)])
                          C.op("dve", lambda e, p=p: e.tensor_scalar(out=ar[:], in0=ar[:], scalar1=cv("ka", p, p + 1), scalar2=omka[:, p:p + 1],
                                                                     op0=ALU.mult, op1=ALU.add), R=["ar", "cs"] + DVK, W=["ar"])
                          kd = zsf["k"]
                          C.op("pool", lambda e: e.tensor_tensor(out=kd[:], in0=zsf["k"][:], in1=ar[:], op=ALU.mult), R=["zs_k", "ar"], W=["zs_k"])
                          C.op("dve", lambda e, p=p: e.tensor_tensor(out=Ktt[:, p, :], in0=kd[:], in1=Ginv[:], op=ALU.mult), R=["zs_k", "dd0"], W=[("Ktt", p)])
                          C.op("pool", lambda e, p=p: e.tensor_tensor(out=Kht[:, p, :], in0=kd[:], in1=Gaft[:], op=ALU.mult), R=["zs_k", "dd2"], W=[("Kht", p)])
                          dbg_dump("s3", thb[:])
                          if not state_only:
                              C.op("dve", lambda e, p=p: e.scalar_tensor_tensor(out=prodb[:], in0=zsf["r"][:], scalar=hrk[:, p:p + 1], in1=kd[:],
                                                                                op0=ALU.mult, op1=ALU.mult), R=["zs_r", "zs_k"] + DVK, W=["prodb"])
                              for c in range(8):
                                  for h in range(2):
                                      mm(QX.t[64 * h:64 * h + 64, h * 512 + c:h * 512 + c + 1], prodb[64 * h:64 * h + 64, c * 64:(c + 1) * 64],
                                         onesb[64 * h:64 * h + 64, 0:1], True, True, R=["prodb", "onesb"], W=[pk(QX.b[h])],
                                         inc=(c == 7 and h == 1))
                              for h in range(2):
                                  hs = slice(64 * h, 64 * h + 64)
                                  C.op("dve", lambda e, p=p, t5=t5, h=h, hs=hs: e.tensor_tensor(
                                      out=sbon[hs, t5 * 8:(t5 + 1) * 8, p], in0=sbon[hs, t5 * 8:(t5 + 1) * 8, p],
                                      in1=QX.t[hs, h * 512:h * 512 + 8], op=ALU.add), R=dk(QX) + ["sbon"], W=["sbon"])
                          dbg_dump("s4", thb[:])
                          C.op("act", lambda e: e.activation(out=vb[:], in_=zsf["v"][:], func=AF.Copy), R=["zs_v"], W=["vb"])
                          for src3, dstT, nm, sk in ((None, Vst, "Vst", "vb"), (Bht, BhT, "BhT", ("Bht", p)), (Kht, KhT, "KhT", ("Kht", p))):
                              pbk = {"Vst": PS6, "BhT": PX[1], "KhT": PA[3]}[nm]
                              for c in range(8):
                                  src_ap = (vb[:, c * 64:(c + 1) * 64] if src3 is None else src3[:, p, c * 64:(c + 1) * 64])
                                  for h in range(2):
                                      mm(pbk[64 * h:64 * h + 64, c * 64:(c + 1) * 64], src_ap, identb[:, 64 * h:64 * h + 64], True, True,
                                         R=[sk, "identb"], W=[pk(pbk)], inc=(c == 7 and h == 1))
                              evac(dstT[:, p].rearrange("p c v -> p (c v)"), pbk[:, 0:512], R=[pk(pbk)], W=[(nm, p)])
                      dbg_dump("s5", thb[:])
                      chunks = range(8) if fwd else range(7, -1, -1)
                      def chunk_fns(c):
                          gc = t5 * 8 + c
                          def st_gram(p):
                              At_ = lambda h: ARt[64 * h:64 * h + 64, p, c, 0:64]
                              Rt_ = lambda h: ARt[64 * h:64 * h + 64, p, c, 64:128]
                              Bt_ = lambda h: Btt[64 * h:64 * h + 64, p, c * 64:(c + 1) * 64]
                              Kt_ = lambda h: Ktt[64 * h:64 * h + 64, p, c * 64:(c + 1) * 64]
                              RK = [("ARt", p), ("Btt", p), ("Ktt", p)]
                              for h in range(2):
                                  hs = slice(64 * h, 64 * h + 64)
                                  o = h * 512
                                  WK = [pk(QX.b[h])]
                                  mm(QX.t[hs, o:o + 64], Bt_(h), At_(h), True, True, R=RK, W=WK)
                                  mm(QX.t[hs, o + 64:o + 128], At_(h), Bt_(h), True, True, R=RK, W=WK)
                                  mm(QX.t[hs, o + 128:o + 192], Kt_(h), At_(h), True, True, R=RK, W=WK, inc=(state_only and h == 1))
                                  if not state_only:
                                      mm(QX.t[hs, o + 192:o + 256], Bt_(h), Rt_(h), True, True, R=RK, W=WK)
                                      mm(QX.t[hs, o + 256:o + 320], Kt_(h), Rt_(h), True, True, R=RK, W=WK, inc=(h == 1))
                              for h in range(2):
                                  hs = slice(64 * h, 64 * h + 64)
                                  o = h * 512
                                  dsl = slice(64 * h, 64 * h + 64)
                                  C.op("dve", lambda e, hs=hs, o=o, dsl=dsl: e.tensor_tensor(
                                      out=Lp[p][hs, 128:512].rearrange("p (s t) -> p s t", s=3)[:, :, dsl],
                                      in0=QX.t[hs, o:o + 192].rearrange("p (s t) -> p s t", s=3),
                                      in1=mX[hs, :].rearrange("p (s t) -> p s t", s=3)[:, :, dsl], op=ALU.mult),
                                      R=dk(QX) + ["mskb"], W=[("LP", p), ("LM", p)])
                                  if state_only:
                                      continue
                                  C.op("dve", lambda e, hs=hs, o=o, dsl=dsl: e.tensor_tensor(
                                      out=Ygb[hs, p, :].rearrange("p (s t) -> p s t", s=2)[:, :, dsl],
                                      in0=QX.t[hs, o + 192:o + 320].rearrange("p (s t) -> p s t", s=2),
                                      in1=mY[hs, :].rearrange("p (s t) -> p s t", s=2)[:, :, dsl], op=ALU.mult),
                                      R=dk(QX) + ["mskb"], W=[("Ygb", p)])
                              C.op("pool", lambda e: e.tensor_tensor(out=Lp[p][:, 0:128], in0=Lp[p][:, 128:256], in1=identb[:], op=ALU.add),
                                   R=[("LP", p), "identb"], W=[("LT", p)])

                          def st_level(p, lev):
                              pd = PA[p]
                              Lt = Lp[p]
                              T_, P_, PT_ = Lt[:, 0:128], Lt[:, 128:256], Lt[:, 256:384]
                              LTk, LPk = ("LT", p), ("LP", p)
                              if lev >= 2:
                                  mm(pd[:, 0:128], identb[:], T_, True, False, R=[LTk, "identb"], W=[pk(pd)])
                                  mm(pd[:, 0:128], PT_, T_, False, True, R=[LPk, LTk], W=[pk(pd)], inc=(lev == 6))
                              if lev <= 5:
                                  mm(pd[:, 128:256], PT_, P_, True, True, R=[LPk], W=[pk(pd)])
                                  mm(pd[:, 256:384], P_, PT_, True, True, R=[LPk], W=[pk(pd)], inc=True)
                              if lev == 1:
                                  evac(Lt[:, 128:384], pd[:, 128:384], R=[pk(pd)], W=[LPk])
                              elif lev <= 5:
                                  evac(Lt[:, 0:384], pd[:, 0:384], R=[pk(pd)], W=[LTk, LPk])
                              else:
                                  evac(Lt[:, 0:128], pd[:, 0:128], R=[pk(pd)], W=[LTk])

                          def D_slot(tt):
                              for p in range(4):
                                  sg = tt - p
                                  if sg == 0:
                                      st_gram(p)
                                  elif 1 <= sg <= 6:
                                      st_level(p, sg)

                          def C1():
                              for p in range(4):
                                  for h in range(2):
                                      hs = slice(64 * h, 64 * h + 64)
                                      o = h * 512 + p * 64
                                      mm(QB.t[hs, o:o + 64], Lp[p][:, 384 + 64 * h:384 + 64 * h + 64], Vst[:, p, c, :], True, False,
                                         R=[("LM", p), ("Vst", p)], W=[pk(QB.b[h])])
                                      mm(QB.t[hs, o:o + 64], ARt[hs, p, c, 0:64], Sbf[hs, p, :], False, True,
                                         R=[("ARt", p), "Sbf"], W=[pk(QB.b[h])], inc=(p == 3 and h == 1))
                              for h in range(2):
                                  hs = slice(64 * h, 64 * h + 64)
                                  evac(Upb[hs].rearrange("p a v -> p (a v)"), QB.t[hs, h * 512:h * 512 + 256], R=dk(QB), W=["Upb"])

                          def C2():
                              pu2 = PS6
                              for p in range(4):
                                  o = p * 64
                                  mm(pu2[:, o:o + 64], Lp[p][:, 0:128], Upb[:, p, :], True, True, R=[("LT", p), "Upb"], W=[pk(pu2)], inc=(p == 3))
                              for h in range(2):
                                  hs = slice(64 * h, 64 * h + 64)
                                  evac(Ubz[hs, h].rearrange("p a v -> p (a v)"), pu2[hs, 0:256], R=[pk(pu2)], W=["Ub"])

                          def CY():
                              pass
                              if not state_only:
                                  for p in range(4):
                                      for h in range(2):
                                          hs = slice(64 * h, 64 * h + 64)
                                          o = h * 512 + p * 64
                                          WK = [pk(QX.b[h])]
                                          mm(QX.t[hs, o:o + 64], Ygb[:, p, 64 * h:64 * h + 64], Ubz[:, h, p, :], True, False, R=[("Ygb", p), "Ub"], W=WK)
                                          mm(QX.t[hs, o:o + 64], Ygb[:, p, 128 + 64 * h:128 + 64 * h + 64], Vst[:, p, c, :], False, False,
                                             R=[("Ygb", p), ("Vst", p)], W=WK)
                                          mm(QX.t[hs, o:o + 64], ARt[hs, p, c, 64:128], Sbf[hs, p, :], False, True,
                                             R=[("ARt", p), "Sbf"], W=WK, inc=(p == 3 and h == 1))
                                  for h in range(2):
                                      hs = slice(64 * h, 64 * h + 64)
                                      ysrc = QX.t[hs, h * 512:h * 512 + 256]
                                      if fwd:
                                          evac(Ofb[hs, gc].rearrange("p a v -> p (a v)"), ysrc, R=dk(QX), W=[("Ofb", gc)])
                                      else:
                                          C.op("dve", lambda e, hs=hs, ysrc=ysrc, gc=gc, c=c: e.tensor_tensor(
                                              out=fina[hs, c % 4].rearrange("p a v -> p (a v)"), in0=Ofb[hs, gc].rearrange("p a v -> p (a v)"),
                                              in1=ysrc, op=ALU.add), R=dk(QX) + [("Ofb", gc)], W=["fina"])

                          def C3():
                              QS = QX if state_only else QB
                              for p in range(4):
                                  for h in range(2):
                                      hs = slice(64 * h, 64 * h + 64)
                                      o = h * 512 + p * 64
                                      mm(QS.t[hs, o:o + 64], BhT[:, p, c, :], Ubz[:, h, p, :], True, False, R=[("BhT", p), "Ub"], W=[pk(QS.b[h])])
                                      mm(QS.t[hs, o:o + 64], KhT[hs, p, c, :], Vst[hs, p, c, :], False, True, R=[("KhT", p), ("Vst", p)],
                                         W=[pk(QS.b[h])], inc=(p == 3 and h == 1))
                              C.op("dve", lambda e, c=c: e.tensor_tensor(out=Stmp[:], in0=Sst[:], in1=GCt[:, :, c:c + 1].to_broadcast([128, 4, 64]), op=ALU.mult),
                                   R=["Sst"] + [("GCt", p) for p in range(4)], W=["Stmp"])
                              for h in range(2):
                                  hs = slice(64 * h, 64 * h + 64)
                                  C.op("dve", lambda e, hs=hs, h=h: e.tensor_tensor(
                                      out=Sst[hs].rearrange("p a v -> p (a v)"), in0=Stmp[hs].rearrange("p a v -> p (a v)"),
                                      in1=QS.t[hs, h * 512:h * 512 + 256], op=ALU.add), R=["Stmp"] + dk(QS), W=["Sst"])
                              C.op("act", lambda e: e.activation(out=Sbf[:], in_=Sst[:], func=AF.Copy), R=["Sst"], W=["Sbf"])

                          def FIN():
                              if (not fwd) and (not state_only) and c % 4 == 0:
                                  finalize(t5, c // 4)
                          return D_slot, C1, C2, CY, C3, FIN

                      clist = list(chunks)
                      fns = {c: chunk_fns(c) for c in clist}
                      if state_only:
                          for idx, c in enumerate(clist):
                              D_, C1_, C2_, CY_, C3_, FIN_ = fns[c]
                              if idx == 0:
                                  for tt in range(10):
                                      D_(tt)
                              C1_()
                              C2_()
                              if idx + 1 < len(clist):
                                  Dn = fns[clist[idx + 1]][0]
                                  Dn(0)
                                  C3_()
                                  for tt in range(1, 10):
                                      Dn(tt)
                              else:
                                  C3_()
                      else:
                          for c in clist:
                              D_, C1_, C2_, CY_, C3_, FIN_ = fns[c]
                              for tt in range(10):
                                  D_(tt)
                              C1_()
                              C2_()
                              CY_()
                              C3_()
                              FIN_()
                      C.maybe_rotate()

              if s == 0 and WITH_XCORE:
                  xt_v = [fina[:].rearrange("p c a v -> p (c a v)")]
                  xb_v = [uab[:].rearrange("p c a v -> p (c a v)")]
                  selb = slp[:, 280:288]

                  def boundary(i):
                      C.op("dve", lambda e: e.scalar_tensor_tensor(out=Ssave[:, 1], in0=Sst[:], scalar=selb[:, i:i + 1], in1=Ssave[:, 1],
                                                                   op0=ALU.mult, op1=ALU.add), R=["Sst", "slp", ("Ssave", 1)], W=[("Ssave", 1)])
                      C.op("dve", lambda e: e.tensor_scalar(out=Stmp[:, 0, 0:1], in0=selb[:, i:i + 1], scalar1=-1.0, scalar2=1.0,
                                                            op0=ALU.mult, op1=ALU.add), R=["slp"], W=["Stmp"])
                      C.op("dve", lambda e: e.tensor_scalar(out=Sst[:], in0=Sst[:], scalar1=Stmp[:, 0, 0:1], scalar2=None, op0=ALU.mult),
                           R=["Sst", "Stmp"], W=["Sst"])
                      C.op("act", lambda e: e.activation(out=Sbf[:], in_=Sst[:], func=AF.Copy), R=["Sst"], W=["Sbf"])

                  C.op("dve", lambda e: e.memset(Ssave[:], 0.0), W=[("Ssave", 0), ("Ssave", 1)])
                  C.op("dve", lambda e: e.memset(Sst[:], 0.0), W=["Sst"])
                  C.op("dve", lambda e: e.memset(Sbf[:], 0.0), W=["Sbf"])
                  for j in range(7):
                      boundary(j)
                      fill_hT(xo, j * SLOT_EXT, list(range(1, 1 + SLOT_EXT // 128)), xt_v, xb_v, ["fina"], ["uab"])
                      rwkv_pass(0, state_only=True, init="keep", slot=j)
                  boundary(7)
                  C.op("dve", lambda e: e.tensor_copy(out=Ssave[:, 0], in_=Sst[:]), R=["Sst"], W=[("Ssave", 0)])
                  fill_hT(xs, xoff, list(range(EXT // 128)), xt_v, xb_v, ["fina"], ["uab"])
                  rwkv_pass(0, init="saved")
                  rwkv_pass(1, init="saved")
              else:
                  rwkv_pass(0)
                  dbg_dump("p3f", Ofb[:].rearrange("p c a v -> p (c a v)"))
                  rwkv_pass(1)
              dbg_dump("p3", UaT[:].rearrange("p k t -> p (k t)"))
              C.barrier()
          C.maybe_rotate()

          with ExitStack() as es4:
              alloc_wbuf(es4, f"p4s{s}", 512)
              xT = sb("xT", [128, 8, TT], F32, es4)
              xin = [sb(f"xin{i}", [128, D], F32, es4) for i in range(2)]
              identf = sb("identf", [128, 128], F32, es4)
              hb = sb("hb", [128, 8, TT], BF, es4)
              mT = sb("mT", [128, 8, TT], BF, es4)
              aT = sb("aT", [128, 22, TT], BF, es4)
              sqb = sb("sqb", [128, 8, TT], BF, es4)
              rsb = sb("rsb", [128, TT], F32, es4)
              sga = sb("sga", [128, TT], F32, es4)
              sgn = sb("sgn", [128, TT], F32, es4)
              t1 = sb("t1", [128, TT], F32, es4)
              pin = [sb(f"pin{i}", [128, 256], F32, es4) for i in range(2)]
              pbf = sb("pbf", [128, 256], BF, es4)
              pT = sb("pT", [128, 2, TT], BF, es4)
              yo = [sb(f"yo{i}", [128, D], F32, es4) for i in range(2)]
              C.op("dve", lambda e: e.tensor_copy(out=identf[:], in_=cv("ident")), R=["cs"], W=["identf"])

              def rms_bcast(gname, dst):
                  C.op("act", lambda e: e.activation(out=sqb[:], in_=xT[:], func=AF.Square), R=["xT"], W=["sqb"])
                  pa = PA[0]
                  for kc in range(8):
                      mm(pa[:], onesb[:], sqb[:, kc, :], kc == 0, kc == 7, R=["onesb", "sqb"], W=[pk(pa)], inc=(kc == 7))
                  C.op("act", lambda e: e.activation(out=rsb[:], in_=pa[:], func=AF.Ln, bias=epsc[:, 0:1], scale=1.0 / D),
                       R=[pk(pa), "epsc"], W=["rsb"])
                  C.op("act", lambda e: e.activation(out=rsb[:], in_=rsb[:], func=AF.Exp, scale=-0.5), R=["rsb"], W=["rsb"])
                  for kc in range(8):
                      C.op("dve", lambda e, kc=kc: e.scalar_tensor_tensor(out=dst[:, kc, :], in0=xT[:, kc, :], scalar=cv(gname, kc, kc + 1),
                                                                          in1=rsb[:], op0=ALU.mult, op1=ALU.mult),
                           R=["xT", "cs", "rsb"], W=["hb"])

              for t5 in range(NTILE):
                  e0 = HALO + t5 * TT
                  tsl = slice(t5 * TT, (t5 + 1) * TT)
                  for b4 in range(4):
                      i = b4 % 2
                      C.dma("sp", xin[i][:], xs[xoff + e0 + b4 * 128: xoff + e0 + (b4 + 1) * 128, :], W=[f"xin{i}"])
                      for half in range(2):
                          pa = PA[(b4 * 2 + half) % 4]
                          for q in range(4):
                              kc = half * 4 + q
                              C.op("pe", lambda e, pa=pa, q=q, kc=kc, i=i: e.transpose(
                                  out=pa[:, q * 128:(q + 1) * 128], in_=xin[i][:, kc * 128:(kc + 1) * 128], identity=identf[:]),
                                  R=[f"xin{i}", "identf"], W=[pk(pa)], inc=(q == 3))
                          evac(xT[:, half * 4:half * 4 + 4, b4 * 128:(b4 + 1) * 128],
                               pa[:].rearrange("p (k t) -> p k t", k=4), R=[pk(pa)], W=["xT"])
                  for sweep, (usrc, ukey, wsrc, gbase, sgt) in enumerate(((UaT, "UaT", w_bra, 3456, sga), (UnT, "UnT", w_brn, 4480, sgn))):
                      for half in range(2):
                          wb_, wbk = load_w(wsrc, 0, half * 512, 512, nk=4)
                          wg_, wgk = load_w(w_in, 0, gbase + half * 512, 512)
                          for q in range(4):
                              dc = half * 4 + q
                              pg = PA[0]
                              pyv = PA[1]
                              for kc in range(8):
                                  mm(pg[:], wg_[:, kc, q * 128:(q + 1) * 128], hT[:, kc, e0:e0 + TT], kc == 0, kc == 7,
                                     R=[wgk, ("hT", e0 // 512), ("hT", e0 // 512 + 1)], W=[pk(pg)], inc=(kc == 7))
                              for kc in range(4):
                                  mm(pyv[:], wb_[:, kc, q * 128:(q + 1) * 128], usrc[:, kc, tsl], kc == 0, kc == 3,
                                     R=[wbk, (ukey, t5)], W=[pk(pyv)], inc=(kc == 3))
                              C.op("act", lambda e, pg=pg, sgt=sgt: e.activation(out=sgt[:], in_=pg[:], func=AF.Sigmoid), R=[pk(pg)], W=[sgt.name])
                              if sweep == 0:
                                  C.op("dve", lambda e, pyv=pyv, dc=dc, sgt=sgt: e.tensor_tensor(out=hb[:, dc, :], in0=sgt[:], in1=pyv[:], op=ALU.mult),
                                       R=[sgt.name, pk(pyv)], W=["hb"])
                              else:
                                  C.op("dve", lambda e, pyv=pyv, sgt=sgt: e.tensor_tensor(out=t1[:], in0=sgt[:], in1=pyv[:], op=ALU.mult),
                                       R=[sgt.name, pk(pyv)], W=["t1"])
                                  C.op("pool", lambda e, dc=dc: e.tensor_tensor(out=mT[:, dc, :], in0=t1[:], in1=hb[:, dc, :], op=ALU.add),
                                       R=["t1", "hb"], W=["mT"])
                  for half in range(2):
                      wo, wok = load_w(w_out, 0, half * 512, 512)
                      for q in range(4):
                          dc = half * 4 + q
                          pa = PA[dc % 2]
                          for kc in range(8):
                              mm(pa[:], wo[:, kc, q * 128:(q + 1) * 128], mT[:, kc, :], kc == 0, kc == 7, R=[wok, "mT"], W=[pk(pa)], inc=(kc == 7))
                          C.op("dve", lambda e, pa=pa, dc=dc: e.tensor_tensor(out=xT[:, dc, :], in0=xT[:, dc, :], in1=pa[:], op=ALU.add),
                               R=["xT", pk(pa)], W=["xT"])
                  rms_bcast("gffn", hb)
                  for f0 in range(0, DFF, 512):
                      nc_ = min(512, DFF - f0)
                      wg_, wgk = load_w(w_gate, 0, f0, nc_)
                      wu_, wuk = load_w(w_up, 0, f0, nc_)
                      for q in range(nc_ // 128):
                          fc = f0 // 128 + q
                          pg = PA[(fc % 2) * 2]
                          pu = PA[(fc % 2) * 2 + 1]
                          for kc in range(8):
                              mm(pg[:], wg_[:, kc, q * 128:(q + 1) * 128], hb[:, kc, :], kc == 0, kc == 7, R=[wgk, "hb"], W=[pk(pg)], inc=(kc == 7))
                          for kc in range(8):
                              mm(pu[:], wu_[:, kc, q * 128:(q + 1) * 128], hb[:, kc, :], kc == 0, kc == 7, R=[wuk, "hb"], W=[pk(pu)], inc=(kc == 7))
                          C.op("act", lambda e, pg=pg: e.activation(out=sga[:], in_=pg[:], func=AF.Silu), R=[pk(pg)], W=["sga"])
                          C.op("dve", lambda e, pu=pu, fc=fc: e.tensor_tensor(out=aT[:, fc, :], in0=sga[:], in1=pu[:], op=ALU.mult),
                               R=["sga", pk(pu)], W=[("aT", fc)])
                  for half in range(2):
                      pas = [PA[q] for q in range(4)]
                      for g0 in range(0, 22, 8):
                          ng = min(8, 22 - g0)
                          wd_, wdk = load_w(w_down, g0 * 128, half * 512, 512, nk=ng)
                          for q in range(4):
                              for k in range(ng):
                                  fc = g0 + k
                                  mm(pas[q][:], wd_[:, k, q * 128:(q + 1) * 128], aT[:, fc, :], fc == 0, fc == 21,
                                     R=[wdk, ("aT", fc)], W=[pk(pas[q])], inc=(fc == 21 or k == ng - 1))
                      for q in range(4):
                          dc = half * 4 + q
                          C.op("dve", lambda e, q=q, dc=dc, pas=pas: e.tensor_tensor(out=xT[:, dc, :], in0=xT[:, dc, :], in1=pas[q][:], op=ALU.add),
                               R=["xT", pk(pas[q])], W=["xT"])
                  for b4 in range(4):
                      i = b4 % 2
                      C.dma("sp", pin[i][:], pp[s * SEQT + t5 * TT + b4 * 128: s * SEQT + t5 * TT + (b4 + 1) * 128, :], W=[f"pin{i}"])
                      C.op("pool", lambda e, i=i: e.tensor_copy(out=pbf[:], in_=pin[i][:]), R=[f"pin{i}"], W=["pbf"])
                      pt = PT[0]
                      for kc in range(2):
                          tp(pt[:, kc * 128:(kc + 1) * 128], pbf[:, kc * 128:(kc + 1) * 128], R=["pbf"], W=[pk(pt)], inc=(kc == 1))
                      evac(pT[:, :, b4 * 128:(b4 + 1) * 128], pt[:, 0:256].rearrange("p (k t) -> p k t", k=2), R=[pk(pt)], W=["pT"])
                  rms_bcast("gple", hb)
                  for half in range(2):
                      wp_, wpk = load_w(w_pg, 0, half * 512, 512)
                      we_, wek = load_w(w_ple, 0, half * 512, 512, nk=2)
                      for q in range(4):
                          dc = half * 4 + q
                          pg = PA[(dc % 2) * 2]
                          pe_ = PA[(dc % 2) * 2 + 1]
                          for kc in range(8):
                              mm(pg[:], wp_[:, kc, q * 128:(q + 1) * 128], hb[:, kc, :], kc == 0, kc == 7, R=[wpk, "hb"], W=[pk(pg)], inc=(kc == 7))
                          for kc in range(2):
                              mm(pe_[:], we_[:, kc, q * 128:(q + 1) * 128], pT[:, kc, :], kc == 0, kc == 1, R=[wek, "pT"], W=[pk(pe_)], inc=(kc == 1))
                          C.op("act", lambda e, pg=pg: e.activation(out=sga[:], in_=pg[:], func=AF.Sigmoid), R=[pk(pg)], W=["sga"])
                          C.op("dve", lambda e, pe_=pe_: e.tensor_tensor(out=t1[:], in0=sga[:], in1=pe_[:], op=ALU.mult), R=["sga", pk(pe_)], W=["t1"])
                          C.op("pool", lambda e, dc=dc: e.tensor_tensor(out=xT[:, dc, :], in0=xT[:, dc, :], in1=t1[:], op=ALU.add), R=["xT", "t1"], W=["xT"])
                  C.op("act", lambda e: e.activation(out=sqb[:], in_=xT[:], func=AF.Square), R=["xT"], W=["sqb"])
                  pa = PA[0]
                  for kc in range(8):
                      mm(pa[:], onesb[:], sqb[:, kc, :], kc == 0, kc == 7, R=["onesb", "sqb"], W=[pk(pa)], inc=(kc == 7))
                  C.op("act", lambda e, pa=pa: e.activation(out=rsb[:], in_=pa[:], func=AF.Ln, bias=epsc[:, 0:1], scale=1.0 / D), R=[pk(pa), "epsc"], W=["rsb"])
                  C.op("act", lambda e: e.activation(out=rsb[:], in_=rsb[:], func=AF.Exp, scale=-0.5), R=["rsb"], W=["rsb"])
                  for kc in range(8):
                      C.op("dve", lambda e, kc=kc: e.scalar_tensor_tensor(out=xT[:, kc, :], in0=xT[:, kc, :], scalar=cv("gfin", kc, kc + 1),
                                                                          in1=rsb[:], op0=ALU.mult, op1=ALU.mult), R=["xT", "cs", "rsb"], W=["xT"])
                  for b4 in range(4):
                      i = b4 % 2
                      for half in range(2):
                          pa = PA[(b4 * 2 + half) % 4]
                          for q in range(4):
                              kc = half * 4 + q
                              C.op("pe", lambda e, pa=pa, q=q, kc=kc, b4=b4: e.transpose(
                                  out=pa[:, q * 128:(q + 1) * 128], in_=xT[:, kc, b4 * 128:(b4 + 1) * 128], identity=identf[:]),
                                  R=["xT", "identf"], W=[pk(pa)], inc=(q == 3))
                          evac(yo[i][:, half * 512:(half + 1) * 512], pa[:], R=[pk(pa)], W=[f"yo{i}"])
                      C.dma("sp", y_d[s * SEQT + t5 * TT + b4 * 128: s * SEQT + t5 * TT + (b4 + 1) * 128, :], yo[i][:], R=[f"yo{i}"])
                  C.maybe_rotate()
              C.barrier()

    except _Stop:
        print("instructions:", C.nins)
        nc._ctx = C
        return nc
    C.barrier()
    ES.close()
    print("instructions:", C.nins)
    nc._ctx = C
    return nc


def simulate_sync(C):
    pcs = {e: 0 for e in C.trace}
    sems = {}
    progress = True
    while progress:
        progress = False
        for e, tr in C.trace.items():
            while pcs[e] < len(tr):
                k, key, v = tr[pcs[e]]
                if k == "w":
                    if sems.get(key, 0) >= v:
                        pcs[e] += 1
                        progress = True
                    else:
                        break
                else:
                    sems[key] = sems.get(key, 0) + v
                    pcs[e] += 1
                    progress = True
    stuck = {e: (pcs[e], len(tr), tr[pcs[e]] if pcs[e] < len(tr) else None) for e, tr in C.trace.items()}
    ok = all(pcs[e] == len(tr) for e, tr in C.trace.items())
    return ok, stuck, sems


_PROG = {}


def kernel(**inp):
    inp = {k: np.asarray(v) for k, v in inp.items()}
    xp = inp["x_prompt"][0]
    xsm = inp["x_sample"]
    ppm = inp["p_prompt"][0, 0]
    psm = inp["p_sample"][0]
    f32 = lambda a: np.ascontiguousarray(a, dtype=np.float32)
    shared = {
        "nab": _build_nab(inp["rpb"][0]),
        "msk": np.ascontiguousarray(np.concatenate([_masks()[k] for k in ("mXf", "mYf", "mXb", "mYb")], 1)),
        "w_in": f32(inp["w_in"][0]),
        "w_lora": f32(np.concatenate([np.concatenate([inp["w2_f"][0], inp["w2_b"][0]], 0),
                                      np.concatenate([inp["a2_f"][0], inp["a2_b"][0]], 0)], 1)),
        "g2": f32(inp["g2"][0]),
        "w_br_a": f32(inp["w_br_a"][0]), "w_br_n": f32(inp["w_br_n"][0]), "w_out": f32(inp["w_out"][0]),
        "w_gate": f32(inp["w_gate"][0]), "w_up": f32(inp["w_up"][0]), "w_down": f32(inp["w_down"][0]),
        "w_ple": f32(inp["w_ple"][0]), "w_pg": f32(inp["w_pg"][0]),
    }
    in_maps = []
    for c in range(NCORE):
        xs = np.zeros((3, EXT, D), np.float32)
        lo = c * SEQT - HALO
        hi = c * SEQT + SEQT + HALO
        a, b = max(lo, 0), min(hi, xp.shape[0])
        xs[0, a - lo:b - lo] = xp[a:b]
        xs[1, HALO:HALO + SEQT] = xsm[2 * c]
        xs[2, HALO:HALO + SEQT] = xsm[2 * c + 1]
        pp = np.stack([ppm[c * SEQT:(c + 1) * SEQT], psm[2 * c], psm[2 * c + 1]], 0)
        m = dict(shared)
        xo = np.zeros((7, SLOT_EXT, D), np.float32)
        slp = np.zeros((128, 7 * 40 + 8), np.float32)
        nb = NCORE - 1 - c
        for i in range(7):
            isb = i < nb
            g = (NCORE - 1 - i) if isb else (i - nb)
            lo2 = g * SEQT - 128
            hi2 = g * SEQT + SEQT + 128
            a2, b2 = max(lo2, 0), min(hi2, xp.shape[0])
            seg = np.zeros((SLOT_EXT, D), np.float32)
            seg[a2 - lo2:b2 - lo2] = xp[a2:b2]
            xo[i] = seg[::-1] if isb else seg
            o = i * 40
            slp[:, o:o + 4] = _pp(inp["w0_b" if isb else "w0_f"][0], 4)
            slp[:, o + 4:o + 8] = _pp(inp["a0_b" if isb else "a0_f"][0], 4)
            slp[:, o + 8:o + 23] = _pp(inp["mu_next" if isb else "mu_prev"][0], 15)
            slp[:, o + 23:o + 38] = _pp(inp["mu_prev" if isb else "mu_next"][0], 15)
            slp[64:128 if isb else 0:64, o + 38] = 0.0
            slp[(64 if isb else 0):(128 if isb else 64), o + 38] = 1.0
        slp[:, 280 + nb] = 1.0
        m["xo"] = xo.reshape(7 * SLOT_EXT, D)
        m["slp"] = slp
        m["xs"] = xs.reshape(3 * EXT, D)
        m["pp"] = f32(pp.reshape(3 * SEQT, 256))
        m["cst"] = _build_cst(inp, c)
        in_maps.append(m)
    if "nc" not in _PROG:
        _PROG["nc"] = build_program()
    res = run_bass_kernel_spmd(_PROG["nc"], in_maps, core_ids=list(range(NCORE)))
    ys = [np.asarray(r["y"]).reshape(3, SEQT, D) for r in res.results]
    y_prompt = np.concatenate([y[0] for y in ys], 0)[None]
    y_sample = np.stack([ys[c][1 + k] for c in range(NCORE) for k in range(2)], 0)
    return (y_prompt.astype(np.float32), y_sample.astype(np.float32))
```

```python
from contextlib import ExitStack
import numpy as np
import concourse.bass as bass
import concourse.mybir as mybir
from concourse.bass_utils import run_bass_kernel_spmd

F32 = mybir.dt.float32
BF = mybir.dt.bfloat16
AF = mybir.ActivationFunctionType
ALU = mybir.AluOpType
AX = mybir.AxisListType

NCORE = 8
D = 1024
SEQT = 2048
HALO = 256
EXT = SEQT + 2 * HALO
TT = 512
NTILE = SEQT // TT
DIN = 5504
DFF = 2816
KAPPA = float(np.exp(-0.5))
NEG = -30000.0
WITH_XCORE = True
SLOT_EXT = SEQT + 256

CST_SPEC = [
    ("ident", 128), ("bones", 128),
    ("colmask", 64), ("narm", 576), ("ones", 128),
    ("gmix", 8), ("gffn", 8), ("gple", 8), ("gfin", 8), ("mp", 15), ("mn", 15),
    ("w0f", 4), ("w0b", 4), ("a0f", 4), ("a0b", 4), ("kk", 4), ("ka", 4), ("rk", 4),
    ("lnw", 256), ("lnb", 256), ("first", 1),
]
CST_OFF = {}
_o = 0
for _n, _w in CST_SPEC:
    CST_OFF[_n] = (_o, _w)
    _o += _w
NCST = _o


def _pp(v, nch):
    return np.ascontiguousarray(np.asarray(v, np.float32).reshape(nch, 128).T)


def _masks():
    p = np.arange(128)[:, None]
    f = np.arange(128)[None, :]
    same = (p // 64) == (f // 64)
    a = p % 64
    b = f % 64
    out = {}
    up = same & (a < b)
    lo = same & (a > b)
    upi = same & (a <= b)
    out["mXf"] = np.concatenate([up, lo, up], 1).astype(np.float32)
    out["mYf"] = np.concatenate([upi, upi], 1).astype(np.float32)
    out["mXb"] = np.concatenate([lo, up, lo], 1).astype(np.float32)
    out["mYb"] = np.concatenate([lo, lo], 1).astype(np.float32)
    return out


def _narm(core):
    out = np.zeros((128, 3, 32, 6), np.float32)
    for s in range(3):
        fc = (s > 0) or (core == 0)
        lc = (s > 0) or (core == NCORE - 1)
        for r in range(32):
            if r < 4 and fc:
                lo, hi = 4, 11
            elif r >= 28 and lc:
                lo, hi = 28, 35
            else:
                lo, hi = r, r + 7
            m0 = min(max(r // 2, 0), 14)
            for j in range(6):
                m = m0 + j
                for half in range(2):
                    e = 2 * m + half
                    if not (lo <= e <= hi):
                        out[64 * half:64 * half + 64, s, r, j] = NEG
    return out.reshape(128, 576)


def _build_cst(inp, core):
    c = np.zeros((128, NCST), np.float32)

    def put(name, arr):
        o, w = CST_OFF[name]
        c[:, o:o + w] = np.asarray(arr, np.float32).reshape(128, w)

    put("ident", np.eye(128))
    p = np.arange(128)
    put("bones", (p[:, None] // 64 == p[None, :] // 64))
    kc = np.arange(64)[:, None]
    cc = np.arange(64)[None, :]
    cs = np.clip(cc - 8, 0, 48)
    cm = ((kc >= cs) & (kc <= cs + 15)).astype(np.float32)
    put("colmask", np.concatenate([cm, cm], 0))
    put("narm", _narm(core))
    put("ones", np.ones((128, 128)))
    put("gmix", _pp(inp["g_mix"][0], 8))
    put("gffn", _pp(inp["g_ffn"][0], 8))
    put("gple", _pp(inp["g_ple"][0], 8))
    put("gfin", _pp(inp["g_final"], 8))
    put("mp", _pp(inp["mu_prev"][0], 15))
    put("mn", _pp(inp["mu_next"][0], 15))
    for nm, key in (("w0f", "w0_f"), ("w0b", "w0_b"), ("a0f", "a0_f"), ("a0b", "a0_b"),
                    ("kk", "k_k"), ("ka", "k_a")):
        put(nm, _pp(inp[key][0], 4))
    put("rk", _pp(inp["r_k"][0].reshape(-1), 4))
    for nm, key in (("lnw", "lnx_w"), ("lnb", "lnx_b")):
        v = np.asarray(inp[key][0], np.float32).reshape(4, 2, 64)
        a = np.transpose(v, (1, 0, 2))
        a = np.repeat(a[:, None], 64, axis=1).reshape(128, 256)
        put(nm, a)
    put("first", np.full((128, 1), 1.0 if core == 0 else 0.0))
    return c


def _build_nab(rpb):
    rpb = np.asarray(rpb, np.float32)
    kc = np.arange(64)[:, None]
    cc = np.arange(64)[None, :]
    idx = np.clip(kc - cc + 15, 0, 30)
    g = rpb[:, :, idx]
    out = np.zeros((2, 64, 8, 14, 64), np.float32)
    for half in range(2):
        out[half] = np.transpose(g[:, half:half + 14], (2, 0, 1, 3))
    return np.ascontiguousarray(out.reshape(128, 8 * 14 * 64))


class Ctx:
    NDS = 24

    def __init__(self, nc):
        self.nc = nc
        self.eng = {"pe": nc.tensor, "dve": nc.vector, "act": nc.scalar, "pool": nc.gpsimd,
                    "sp": nc.sync}
        self.sem = {e: nc.alloc_semaphore(name=f"pg_{e}_0") for e in self.eng}
        self.epoch = 0
        self.cnt = {e: 0 for e in self.eng}
        self.seen = {e: {} for e in self.eng}
        self.lastw = {}
        self.readers = {}
        self.dsems = [nc.alloc_semaphore(name=f"dq{i}") for i in range(self.NDS)]
        self.dcnt = [0] * self.NDS
        self.dnext = 0
        self.nins = 0
        self.trace = {e: [] for e in self.eng}

    def _semof(self, src):
        return self.sem[src[1]] if src[0] == "e" else self.dsems[src[1]]

    def _wait(self, e, src, val):
        if val <= 0:
            return
        if self.seen[e].get(src, 0) >= val:
            return
        self.eng[e].wait_ge(self._semof(src), val)
        self.trace[e].append(("w", (src, self.epoch if src[0] == "e" else 0), val))
        self.seen[e][src] = val

    def _deps(self, R, W):
        deps = {}

        def add(st):
            if st is None:
                return
            s, v = st
            if deps.get(s, 0) < v:
                deps[s] = v
        for k in R:
            add(self.lastw.get(k))
        for k in W:
            add(self.lastw.get(k))
            for s, v in self.readers.get(k, {}).items():
                add((s, v))
        return deps

    def _upd(self, R, W, stamp):
        for k in W:
            self.lastw[k] = stamp
            self.readers[k] = {}
        for k in R:
            d = self.readers.setdefault(k, {})
            if d.get(stamp[0], 0) < stamp[1]:
                d[stamp[0]] = stamp[1]

    def op(self, e, fn, R=(), W=(), inc=True):
        for s, v in self._deps(R, W).items():
            if e == "pe" and s == ("e", "pe"):
                continue
            self._wait(e, s, v)
        ins = fn(self.eng[e])
        self.nins += 1
        if inc:
            self.cnt[e] += 1
            ins.then_inc(self.sem[e], 1)
            self.trace[e].append(("i", ((("e", e)), self.epoch), 1))
            stamp = (("e", e), self.cnt[e])
        else:
            stamp = (("e", e), self.cnt[e] + 1)
        self._upd(R, W, stamp)
        return ins

    def dma(self, q, out, in_, R=(), W=()):
        i = self.dnext
        self.dnext = (self.dnext + 1) % self.NDS
        self._wait(q, ("d", i), self.dcnt[i])
        for s, v in self._deps(R, W).items():
            self._wait(q, s, v)
        self.eng[q].dma_start(out=out, in_=in_).then_inc(self.dsems[i], 16)
        self.trace[q].append(("i", (("d", i), 0), 16))
        self.nins += 1
        self.dcnt[i] += 16
        self._upd(R, W, (("d", i), self.dcnt[i]))

    def barrier(self):
        for e in self.eng:
            for e2 in self.eng:
                if e2 != e:
                    self._wait(e, ("e", e2), self.cnt[e2])
            for i in range(self.NDS):
                self._wait(e, ("d", i), self.dcnt[i])
        self.lastw = {}
        self.readers = {}

    def maybe_rotate(self, limit=24000):
        if max(self.cnt.values()) < limit:
            return
        self.barrier()
        self.epoch += 1
        for e in self.eng:
            self.sem[e] = self.nc.alloc_semaphore(name=f"pg_{e}_{self.epoch}")
            self.cnt[e] = 0
        for e in self.eng:
            for e2 in self.eng:
                self.seen[e].pop(("e", e2), None)


class _Stop(Exception):
    pass


def build_program(debug=None):
    nc = bass.Bass("TRN2", target_bir_lowering=False)
    dt = lambda n, s: nc.dram_tensor(n, s, F32, kind="ExternalInput").ap()
    xs = dt("xs", [3 * EXT, D])
    pp = dt("pp", [3 * SEQT, 256])
    cst_d = dt("cst", [128, NCST])
    nab_d = dt("nab", [128, 8 * 14 * 64])
    msk_d = dt("msk", [128, 1280])
    w_in = dt("w_in", [D, DIN])
    w_lora = dt("w_lora", [128, 2 * 512])
    g2_d = dt("g2", [128, 512])
    w_bra = dt("w_br_a", [512, D])
    w_brn = dt("w_br_n", [512, D])
    w_out = dt("w_out", [D, D])
    w_gate = dt("w_gate", [D, DFF])
    w_up = dt("w_up", [D, DFF])
    w_down = dt("w_down", [DFF, D])
    w_ple = dt("w_ple", [256, D])
    w_pg = dt("w_pg", [D, D])
    xo = dt("xo", [7 * SLOT_EXT, D])
    slp_d = dt("slp", [128, 7 * 40 + 8])
    y_d = nc.dram_tensor("y", [3 * SEQT, D], F32, kind="ExternalOutput").ap()

    C = Ctx(nc)
    ES = ExitStack()
    dbg_d = None
    if debug:
        dbg_d = nc.dram_tensor("dbg", [128, debug.get("n", 8 * EXT)], BF if debug.get("bf", True) else F32, kind="ExternalOutput").ap()

    def dbg_dump(tag, ap2d):
        if debug and debug.get("stop") == tag:
            C.barrier()
            C.dma("sp", dbg_d[:, 0:ap2d.shape[1]], ap2d, R=[], W=[])
            C.barrier()
            ex = _Stop()
            ex.nc = nc
            raise ex

    uid = {"n": 0}

    def sb(name, shape, dtype=F32, es=ES):
        uid["n"] += 1
        return es.enter_context(nc.sbuf_tensor(f"{name}_u{uid['n']}", shape, dtype))

    class PSV:
        def __init__(self, name, ap):
            self.name = name
            self.ap = ap

        def __getitem__(self, idx):
            return self.ap[idx]

    class PSD:
        def __init__(self, name):
            self.t = nc.alloc_psum_tensor(name, [128, 1024], F32)
            self.b = [PSV(f"{name}_b{h}", self.t[:, h * 512:(h + 1) * 512]) for h in range(2)]

    QA, QB, QX = PSD("qa"), PSD("qb"), PSD("qx")
    PA = QA.b + QB.b
    PX = QX.b
    PS6 = PSV("ps6", nc.alloc_psum_tensor("ps6t", [128, 512], F32)[:, :])
    PT = [PSV("pt0", nc.alloc_psum_tensor("pt0t", [128, 1024], BF)[:, :])]
    for t in PA + PX + [PS6]:
        C.op("dve", lambda e, t=t: e.memset(t[:], 0.0), W=[("ps", t.name)])

    def dk(Q):
        return [("ps", Q.b[0].name), ("ps", Q.b[1].name)]

    def pk(t, sub=None):
        return ("ps", t.name) if sub is None else ("ps", t.name, sub)

    cs = sb("cs", [128, NCST])
    C.dma("sp", cs[:], cst_d, W=["cs"])

    def cv(name, a=0, b=None):
        o, w = CST_OFF[name]
        return cs[:, o + a:o + (w if b is None else b)]

    identb = sb("identb", [128, 128], BF)
    bonesb = sb("bonesb", [128, 128], BF)
    onesb = sb("onesb", [128, 128], BF)
    C.op("dve", lambda e: e.tensor_copy(out=identb[:], in_=cv("ident")), R=["cs"], W=["identb"])
    C.op("dve", lambda e: e.tensor_copy(out=bonesb[:], in_=cv("bones")), R=["cs"], W=["bonesb"])
    C.op("dve", lambda e: e.tensor_copy(out=onesb[:], in_=cv("ones")), R=["cs"], W=["onesb"])
    dv = sb("dv", [128, 64])
    c0 = dv[:, 0:15]
    omka = dv[:, 16:20]
    hrk = dv[:, 20:24]
    C.op("dve", lambda e: e.tensor_tensor(out=c0, in0=cv("mp"), in1=cv("mn"), op=ALU.add), R=["cs"], W=["dv0"])
    C.op("dve", lambda e: e.tensor_scalar(out=c0, in0=c0, scalar1=-1.0, scalar2=1.0, op0=ALU.mult, op1=ALU.add), R=["dv0"], W=["dv0"])
    C.op("dve", lambda e: e.tensor_scalar(out=omka, in0=cv("ka"), scalar1=-1.0, scalar2=1.0, op0=ALU.mult, op1=ALU.add), R=["cs"], W=["dv1"])
    C.op("dve", lambda e: e.tensor_scalar(out=hrk, in0=cv("rk"), scalar1=0.5, scalar2=None, op0=ALU.mult), R=["cs"], W=["dv2"])
    DVK = ["dv0", "dv1", "dv2"]
    epsc = sb("epsc", [128, 4])
    C.op("dve", lambda e: e.memset(epsc[:, 0:1], 1e-6), W=["epsc"])
    C.op("dve", lambda e: e.memset(epsc[:, 1:2], 1e-24), W=["epsc"])
    C.op("dve", lambda e: e.memset(epsc[:, 2:3], 64e-5), W=["epsc"])
    C.op("dve", lambda e: e.memset(epsc[:, 3:4], 0.0), W=["epsc"])

    mskb = sb("mskb", [128, 1280], BF)
    C.dma("pool", mskb[:], msk_d, W=["mskb"])
    MSK = {"mXf": mskb[:, 0:384], "mYf": mskb[:, 384:640], "mXb": mskb[:, 640:1024], "mYb": mskb[:, 1024:1280]}
    lorab = sb("lorab", [128, 1024], BF)
    g2b = sb("g2b", [128, 512], BF)
    C.dma("pool", lorab[:], w_lora, W=["lorab"])
    C.dma("pool", g2b[:], g2_d, W=["g2b"])

    def build_epb(es0):
        epb = sb("epb", [128, 8 * 14 * 64], BF, es0)
        stg = sb("nabstg", [128, 1792], F32, es0)
        for q in range(4):
            C.dma("sp", stg[:], nab_d[:, q * 1792:(q + 1) * 1792], W=["stg"])
            C.op("act", lambda e: e.activation(out=stg[:], in_=stg[:], func=AF.Exp), R=["stg"], W=["stg"])
            C.op("dve", lambda e, q=q: e.tensor_tensor(
                out=epb[:, q * 1792:(q + 1) * 1792].rearrange("p (g c) -> p g c", c=64),
                in0=stg[:].rearrange("p (g c) -> p g c", c=64),
                in1=cv("colmask").unsqueeze(1).to_broadcast([128, 28, 64]), op=ALU.mult),
                R=["stg", "cs"], W=["epb"])
        return epb[:].rearrange("p (h d c) -> p h d c", h=8, d=14)

    dbg_dump("p0", lorab[:])
    hT = sb("hT", [128, 8, EXT], BF)
    UnT = sb("UnT", [128, 4, SEQT], BF)
    UaT = sb("UaT", [128, 4, SEQT], BF)
    wbuf = [None, None]

    def alloc_wbuf(es, tag, width):
        for i in range(2):
            wbuf[i] = sb(f"wbuf_{tag}_{i}", [128, 8, width], BF, es)
    wstate = {"i": 0}

    def load_w(src, r0, c0_, ncols, nk=8):
        i = wstate["i"]
        wstate["i"] = 1 - i
        t = wbuf[i]
        v = src[r0:r0 + nk * 128, c0_:c0_ + ncols].rearrange("(k p) c -> p k c", p=128)
        C.dma("pool", t[:, 0:nk, 0:ncols], v, W=[f"wbuf{i}"])
        return t, f"wbuf{i}"

    evac_rr = {"i": 0}

    def evac(out, in_, R, W, scale=None):
        evac_rr["i"] ^= 1
        if evac_rr["i"]:
            C.op("act", lambda e: e.activation(out=out, in_=in_, func=AF.Copy), R=R, W=W)
        else:
            C.op("dve", lambda e: e.tensor_copy(out=out, in_=in_), R=R, W=W)

    def mm(out, lhsT, rhs, start, stop, R, W, inc=False):
        return C.op("pe", lambda e: e.matmul(out, lhsT=lhsT, rhs=rhs, start=start, stop=stop,
                                             skip_group_check=True), R=R, W=W, inc=inc)

    def tp(out, in_, R, W, inc=False):
        b0 = in_.base_partition()
        n0 = in_.shape[0]
        return C.op("pe", lambda e: e.transpose(out=out, in_=in_, identity=identb[b0:b0 + n0, b0:b0 + n0]),
                    R=list(R) + ["identb"], W=W, inc=inc)

    st_ = sb("p1st", [128, 4], F32)
    Ssave = sb("Ssave", [128, 2, 4, 64], F32)
    slp = sb("slp", [128, 7 * 40 + 8], F32)
    C.dma("sp", slp[:], slp_d, W=["slp"])

    def fill_hT(src, row0, blks, xt_t, xb_t, kx, kb):
        for n_, blk in enumerate(blks):
            i = n_ % len(xt_t)
            xt_a, xb_a = xt_t[i], xb_t[i]
            kxi, kbi = kx[i], kb[i]
            C.dma("sp", xt_a, src[row0 + n_ * 128: row0 + (n_ + 1) * 128, :], W=[kxi])
            C.op("act", lambda e, xt_a=xt_a, xb_a=xb_a, i=i: e.activation(out=xb_a, in_=xt_a, func=AF.Square,
                                                                  accum_out=st_[:, 2 * i:2 * i + 1]),
                 R=[kxi], W=[kbi, f"p1st{i}"])
            C.op("act", lambda e, i=i: e.activation(out=st_[:, 2 * i + 1:2 * i + 2], in_=st_[:, 2 * i:2 * i + 1], func=AF.Sqrt,
                                                    bias=epsc[:, 0:1], scale=1.0 / D),
                 R=[f"p1st{i}", "epsc"], W=[f"p1st{i}b"])
            C.op("dve", lambda e, i=i: e.reciprocal(out=st_[:, 2 * i + 1:2 * i + 2], in_=st_[:, 2 * i + 1:2 * i + 2]),
                 R=[f"p1st{i}b"], W=[f"p1st{i}b"])
            C.op("dve", lambda e, xt_a=xt_a, xb_a=xb_a, i=i: e.tensor_scalar(out=xb_a, in0=xt_a, scalar1=st_[:, 2 * i + 1:2 * i + 2],
                                                                     scalar2=None, op0=ALU.mult),
                 R=[kxi, f"p1st{i}b"], W=[kbi])
            pt = PT[0]
            for kc in range(8):
                tp(pt[:, kc * 128:(kc + 1) * 128], xb_a[:, kc * 128:(kc + 1) * 128],
                   R=[kbi], W=[pk(pt)], inc=(kc == 7))
            C.op("dve", lambda e, pt=pt, blk=blk: e.tensor_tensor(
                out=hT[:, :, blk * 128:(blk + 1) * 128],
                in0=pt[:].rearrange("p (k t) -> p k t", k=8),
                in1=cv("gmix").unsqueeze(2).to_broadcast([128, 8, 128]), op=ALU.mult),
                R=[pk(pt), "cs"], W=[("hT", blk // 4)])

    try:
      for s in (debug["seqs"] if debug and "seqs" in debug else range(3)):
          xoff = s * EXT
          with ExitStack() as es1:
              xt = [sb(f"p1x{i}", [128, D], F32, es1) for i in range(2)]
              xb = [sb(f"p1xb{i}", [128, D], BF, es1) for i in range(2)]
              fill_hT(xs, xoff, list(range(EXT // 128)), [t[:] for t in xt], [t[:] for t in xb],
                      ["p1x0", "p1x1"], ["p1xb0", "p1xb1"])
              dbg_dump("p1", hT[:].rearrange("p k t -> p (k t)"))
              C.barrier()

          with ExitStack() as es2:
              alloc_wbuf(es2, f"p2s{s}", 512)
              epb4 = build_epb(es2)
              qT = sb("qT", [128, 4, SEQT], BF, es2)
              kT = sb("kT", [128, 4, EXT], BF, es2)
              Vt = sb("Vt", [128, EXT // 128, 512], BF, es2)
              Eb = [sb(f"Eb{j}", [128, 512], BF, es2) for j in range(6)]
              Pb = [sb(f"Pb{j}", [128, 512], BF, es2) for j in range(6)]
              rinv = sb("rinv", [128, 256], F32, es2)
              wt, wk = load_w(w_in, 0, 1920, 512)
              ai = 0
              for t5 in range(NTILE):
                  for cc in range(4):
                      pa = PA[ai % 4]; ai += 1
                      for kc in range(8):
                          mm(pa[:], wt[:, kc, cc * 128:(cc + 1) * 128], hT[:, kc, HALO + t5 * TT: HALO + (t5 + 1) * TT],
                             kc == 0, kc == 7, R=[wk, ("hT", (HALO + t5 * TT) // 512), ("hT", (HALO + t5 * TT) // 512 + 1)],
                             W=[pk(pa)], inc=(kc == 7))
                      evac(qT[:, cc, t5 * TT:(t5 + 1) * TT], pa[:], R=[pk(pa)], W=[("qT", t5)])
              wt, wk = load_w(w_in, 0, 2432, 512)
              for t5 in range(EXT // TT):
                  for cc in range(4):
                      pa = PA[ai % 4]; ai += 1
                      for kc in range(8):
                          mm(pa[:], wt[:, kc, cc * 128:(cc + 1) * 128], hT[:, kc, t5 * TT:(t5 + 1) * TT],
                             kc == 0, kc == 7, R=[wk, ("hT", t5)], W=[pk(pa)], inc=(kc == 7))
                      evac(kT[:, cc, t5 * TT:(t5 + 1) * TT], pa[:], R=[pk(pa)], W=[("kT", t5)])
              wt, wk = load_w(w_in, 0, 2944, 512)
              for blk in range(EXT // 128):
                  pa = PA[ai % 4]; ai += 1
                  for kc in range(8):
                      mm(pa[:], hT[:, kc, blk * 128:(blk + 1) * 128], wt[:, kc, :], kc == 0, kc == 7,
                         R=[wk, ("hT", blk // 4)], W=[pk(pa)], inc=(kc == 7))
                  evac(Vt[:, blk, :], pa[:], R=[pk(pa)], W=[("Vt", blk)])
              for r in range(32):
                  m0 = min(max(r // 2, 0), 14)
                  q0 = r * 64
                  for j in range(6):
                      m = m0 + j
                      d = 2 * m - r + 3
                      Q = (QA, QB)[j % 2]
                      for h in range(8):
                          b = 64 * (h % 2)
                          o = (h % 2) * 512 + (h // 2) * 64
                          mm(Q.t[:, o:o + 64], kT[b:b + 64, h // 2, m * 128:(m + 1) * 128],
                             qT[b:b + 64, h // 2, q0:q0 + 64], True, True,
                             R=[("kT", m // 4), ("qT", r // 8)], W=dk(Q), inc=(h == 7))
                      o, _ = CST_OFF["narm"]
                      col = o + (s * 32 + r) * 6 + j
                      C.op("act", lambda e, Q=Q, j=j, col=col: e.activation(
                          out=Eb[j][:].rearrange("p (b c) -> p b c", b=2),
                          in_=Q.t[:].rearrange("p (b c) -> p b c", b=2)[:, :, 0:256],
                          func=AF.Exp, bias=cs[:, col:col + 1], scale=0.125),
                          R=dk(Q) + ["cs"], W=[f"Eb{j}"])
                      C.op("pool", lambda e, j=j, d=d: e.tensor_tensor(
                          out=Pb[j][:].rearrange("p (b hh c) -> p b hh c", b=2, hh=4),
                          in0=Eb[j][:].rearrange("p (b hh c) -> p b hh c", b=2, hh=4),
                          in1=epb4[:, :, d, :].rearrange("p (hh b) c -> p b hh c", b=2), op=ALU.mult),
                          R=[f"Eb{j}", "epb"], W=[f"Pb{j}"])
                  pv = PX[0]
                  sm = PX[1]
                  for h in range(8):
                      b = 64 * (h % 2)
                      for j in range(6):
                          mm(pv[b:b + 64, (h // 2) * 64:(h // 2 + 1) * 64], Vt[:, m0 + j, h * 64:(h + 1) * 64],
                             Pb[j][:, (h % 2) * 256 + (h // 2) * 64:(h % 2) * 256 + (h // 2) * 64 + 64], j == 0, j == 5,
                             R=[("Vt", m0 + j), f"Pb{j}"], W=[pk(pv)], inc=False)
                  for h in range(8):
                      b = 64 * (h % 2)
                      for j in range(6):
                          mm(sm[b:b + 64, (h // 2) * 64:(h // 2 + 1) * 64], onesb[:, 0:64],
                             Pb[j][:, (h % 2) * 256 + (h // 2) * 64:(h % 2) * 256 + (h // 2) * 64 + 64], j == 0, j == 5,
                             R=["onesb", f"Pb{j}"], W=[pk(sm)], inc=(h == 7 and j == 5))
                  C.op("act", lambda e: e.activation(out=rinv[:], in_=sm[:, 0:256], func=AF.Ln),
                       R=[pk(sm)], W=["rinv"])
                  C.op("act", lambda e: e.activation(out=rinv[:], in_=rinv[:], func=AF.Exp, scale=-1.0),
                       R=["rinv"], W=["rinv"])
                  C.op("dve", lambda e, q0=q0: e.tensor_tensor(
                      out=UnT[:, :, q0:q0 + 64], in0=pv[:, 0:256].rearrange("p (g c) -> p g c", g=4),
                      in1=rinv[:].rearrange("p (g c) -> p g c", g=4), op=ALU.mult),
                      R=[pk(pv), "rinv"], W=[("UnT", r // 8)])
              dbg_dump("p2", UnT[:].rearrange("p k t -> p (k t)"))
              C.barrier()
          C.maybe_rotate()

          with ExitStack() as es3:
              alloc_wbuf(es3, f"p3s{s}", 384)
              Ofb = sb("Ofb", [128, 32, 4, 64], BF, es3)
              sbon = sb("sbon", [128, 32, 4], F32, es3)
              Sst = sb("Sst", [128, 4, 64], F32, es3)
              Sbf = sb("Sbf", [128, 4, 64], BF, es3)
              Stmp = sb("Stmp", [128, 4, 64], F32, es3)
              zr = [sb("zr0", [128, TT + 2], F32, es3)]
              zsf = {n: sb(f"zs_{n}", [128, TT], F32, es3) for n in ("r", "k", "v")}
              lzs = sb("lzs", [128, TT], F32, es3)
              thb = sb("thb", [128, TT], BF, es3)
              adb = sb("adb", [128, TT], BF, es3)
              sgb = sb("sgb", [128, TT], BF, es3)
              cumx = sb("cumx", [128, TT + 1], F32, es3)
              sgw = sb("sgw", [128, TT], F32, es3)
              ar = sb("ar", [128, TT], F32, es3)
              dd = [sb(f"dd{i}", [128, TT], F32, es3) for i in range(3)]
              g0t = sb("g0t", [128, TT], F32, es3)
              ksq = sb("ksq", [128, TT], BF, es3)
              kkn = sb("kkn", [128, TT], F32, es3)
              prodb = sb("prodb", [128, TT], BF, es3)
              vb = sb("vb", [128, TT], BF, es3)
              ARt = sb("ARt", [128, 4, 8, 128], BF, es3)
              Btt = sb("Btt", [128, 4, TT], BF, es3)
              Ktt = sb("Ktt", [128, 4, TT], BF, es3)
              Bht = sb("Bht", [128, 4, TT], BF, es3)
              Kht = sb("Kht", [128, 4, TT], BF, es3)
              GCt = sb("GCt", [128, 4, 8], F32, es3)
              Vst = sb("Vst", [128, 4, 8, 64], BF, es3)
              BhT = sb("BhT", [128, 4, 8, 64], BF, es3)
              KhT = sb("KhT", [128, 4, 8, 64], BF, es3)
              Lp = [sb(f"Lp{p}", [128, 512], BF, es3) for p in range(4)]
              Ygb = sb("Ygb", [128, 4, 256], BF, es3)
              Upb = sb("Upb", [128, 4, 64], BF, es3)
              Ubz = sb("Ubz", [128, 2, 4, 64], BF, es3)
              fina = sb("fina", [128, 4, 4, 64], F32, es3)
              finb = sb("finb", [128, 4, 4, 64], F32, es3)
              fst = sb("fst", [128, 2, 16], F32, es3)
              uab = sb("uab", [128, 4, 4, 64], BF, es3)

              ones_bc = cv("ones")[:, 0:1].to_broadcast([128, TT])
              C.op("pool", lambda e: e.memset(cumx[:, 0:1], 0.0), W=["cumx0"])
              C.op("pool", lambda e: e.memset(sbon[:], 0.0), W=["sbon"])
              for p_ in range(4):
                  C.op("pool", lambda e, p_=p_: e.memset(Lp[p_][:], 0.0), W=[("LT", p_), ("LP", p_), ("LM", p_)])
              C.op("pool", lambda e: e.memset(Ubz[:], 0.0), W=["Ub"])
              C.op("pool", lambda e: e.memset(Ygb[:], 0.0), W=[("Ygb", p) for p in range(4)])

              def zproj(wt, wk, wc, e0, ci, dst, after=None, mus=None):
                  pa = PA[zproj.i % 2]
                  zproj.i += 1
                  px = PX[0]
                  z = zr[0]
                  zk = "zr0"
                  hk = [("hT", (e0 - 1) // 512), ("hT", e0 // 512), ("hT", min((e0 + 512) // 512, 4))]
                  for kc in range(8):
                      mm(pa[:], wt[:, kc, wc:wc + 128], hT[:, kc, e0:e0 + TT], kc == 0, kc == 7,
                         R=[wk] + hk, W=[pk(pa)], inc=False)
                  hbv = hT[:, :, e0 - 1:e0 + TT + 1]
                  for kc in range(8):
                      mm(px[:, 0:2], wt[:, kc, wc:wc + 128], hbv[:, kc, 0:TT + 2:TT + 1], kc == 0, kc == 7,
                         R=[wk] + hk, W=[pk(px)], inc=(kc == 7))
                  C.op("act", lambda e: e.activation(out=z[:, 1:TT + 1], in_=pa[:], func=AF.Copy),
                       R=[pk(pa)], W=[zk])
                  C.op("dve", lambda e: e.tensor_copy(out=z[:, 0:TT + 2:TT + 1], in_=px[:, 0:2]),
                       R=[pk(px)], W=[zk])
                  mpc = cv("mp", ci, ci + 1) if mus is None else mus[0][:, ci:ci + 1]
                  mnc = cv("mn", ci, ci + 1) if mus is None else mus[1][:, ci:ci + 1]
                  C.op("dve", lambda e: e.tensor_scalar(out=dst, in0=z[:, 1:TT + 1], scalar1=c0[:, ci:ci + 1],
                                                        scalar2=None, op0=ALU.mult), R=[zk] + DVK, W=[after])
                  C.op("dve", lambda e: e.scalar_tensor_tensor(out=dst, in0=z[:, 0:TT], scalar=mpc, in1=dst,
                                                               op0=ALU.mult, op1=ALU.add), R=[zk, "cs", "slp", after], W=[after])
                  C.op("dve", lambda e: e.scalar_tensor_tensor(out=dst, in0=z[:, 2:TT + 2], scalar=mnc, in1=dst,
                                                               op0=ALU.mult, op1=ALU.add), R=[zk, "cs", "slp", after], W=[after])
              zproj.i = 0
              c3 = lambda a: a.rearrange("p (c t) -> p c t", t=64)

              def finalize(t5, hf):
                  cb = t5 * 8 + hf * 4
                  A_ = fina[:].rearrange("p c a v -> p (c a) v")
                  B_ = finb[:].rearrange("p c a v -> p (c a) v")
                  mean = fst[:, 0, :]
                  var = fst[:, 1, :]
                  C.op("dve", lambda e: e.tensor_reduce(out=mean, in_=A_, axis=AX.X, op=ALU.add), R=["fina"], W=["fst0"])
                  C.op("dve", lambda e: e.tensor_scalar(out=mean, in0=mean, scalar1=1.0 / 64, scalar2=None, op0=ALU.mult), R=["fst0"], W=["fst0"])
                  C.op("dve", lambda e: e.tensor_tensor(out=A_, in0=A_, in1=mean.unsqueeze(2).to_broadcast([128, 16, 64]), op=ALU.subtract),
                       R=["fina", "fst0"], W=["fina"])
                  C.op("pool", lambda e: e.tensor_tensor(out=B_, in0=A_, in1=A_, op=ALU.mult), R=["fina"], W=["finb"])
                  C.op("dve", lambda e: e.tensor_reduce(out=var, in_=B_, axis=AX.X, op=ALU.add), R=["finb"], W=["fst1"])
                  C.op("act", lambda e: e.activation(out=var, in_=var, func=AF.Sqrt, bias=epsc[:, 2:3], scale=1.0 / 64), R=["fst1", "epsc"], W=["fst1"])
                  C.op("dve", lambda e: e.reciprocal(out=var, in_=var), R=["fst1"], W=["fst1"])
                  C.op("dve", lambda e: e.tensor_tensor(out=A_, in0=A_, in1=var.unsqueeze(2).to_broadcast([128, 16, 64]), op=ALU.mult),
                       R=["fina", "fst1"], W=["fina"])
                  lnw = cv("lnw").rearrange("p (a v) -> p a v", a=4).unsqueeze(1).to_broadcast([128, 4, 4, 64])
                  lnb = cv("lnb").rearrange("p (a v) -> p a v", a=4).unsqueeze(1).to_broadcast([128, 4, 4, 64])
                  C.op("pool", lambda e: e.tensor_tensor(out=fina[:], in0=fina[:], in1=lnw, op=ALU.mult), R=["fina", "cs"], W=["fina"])
                  C.op("pool", lambda e: e.tensor_tensor(out=fina[:], in0=fina[:], in1=lnb, op=ALU.add), R=["fina", "cs"], W=["fina"])
                  sb_ = sbon[:, cb:cb + 4, :].unsqueeze(3).to_broadcast([128, 4, 4, 64])
                  VK = [("Vst", p) for p in range(4)]
                  C.op("dve", lambda e: e.tensor_tensor(out=finb[:], in0=Vst[:, :, hf * 4:hf * 4 + 4, :].rearrange("p a c v -> p c a v"), in1=sb_, op=ALU.mult),
                       R=VK + ["sbon"], W=["finb"])
                  C.op("pool", lambda e: e.tensor_tensor(out=fina[:], in0=fina[:], in1=finb[:], op=ALU.add), R=["fina", "finb"], W=["fina"])
                  for c2 in range(2):
                      pg = PA[c2 % 2]
                      for cc in range(2):
                          c = hf * 4 + c2 * 2 + cc
                          for p in range(4):
                              for h in range(2):
                                  o = (cc * 4 + p) * 64
                                  mm(pg[64 * h:64 * h + 64, o:o + 64], sgb[:, c * 64:(c + 1) * 64],
                                     g2b[:, (2 * p + h) * 64:(2 * p + h + 1) * 64], True, True,
                                     R=["sgb", "g2b"], W=[pk(pg)], inc=(cc == 1 and p == 3 and h == 1))
                      C.op("dve", lambda e, pg=pg, c2=c2: e.tensor_tensor(
                          out=uab[:, c2 * 2:c2 * 2 + 2].rearrange("p c a v -> p (c a v)"),
                          in0=fina[:, c2 * 2:c2 * 2 + 2].rearrange("p c a v -> p (c a v)"),
                          in1=pg[:], op=ALU.mult), R=["fina", pk(pg)], W=["uab"])
                  for p in range(4):
                      for c in range(4):
                          for h in range(2):
                              mm(PS6[64 * h:64 * h + 64, c * 64:(c + 1) * 64], uab[:, c, p, :], identb[:, 64 * h:64 * h + 64], True, True,
                                 R=["uab", "identb"], W=[pk(PS6)], inc=(c == 3 and h == 1))
                      evac(UaT[:, p, t5 * TT + hf * 256:t5 * TT + hf * 256 + 256], PS6[:, 0:256], R=[pk(PS6)], W=[("UaT", t5)])

              def rwkv_pass(dr, state_only=False, init="zero", slot=None):
                  fwd = (dr == 0)
                  mus = None
                  if slot is not None:
                      sp_ = slp[:, slot * 40:(slot + 1) * 40]
                      mus = (sp_[:, 8:23], sp_[:, 23:38])
                      dmask = sp_[:, 38:39]
                  if init == "zero":
                      C.op("dve", lambda e: e.memset(Sst[:], 0.0), W=["Sst"])
                      C.op("dve", lambda e: e.memset(Sbf[:], 0.0), W=["Sbf"])
                  elif init == "saved":
                      C.op("dve", lambda e: e.tensor_copy(out=Sst[:], in_=Ssave[:, dr]), R=[("Ssave", dr)], W=["Sst"])
                      C.op("act", lambda e: e.activation(out=Sbf[:], in_=Ssave[:, dr], func=AF.Copy), R=[("Ssave", dr)], W=["Sbf"])
                  tiles = range(NTILE) if fwd else range(NTILE - 1, -1, -1)
                  mX = MSK["mXf" if fwd else "mXb"]
                  mY = MSK["mYf" if fwd else "mYb"]
                  pb = 0 if fwd else 64
                  for t5 in tiles:
                      e0 = HALO + t5 * TT
                      wt, wk = load_w(w_in, 0, 1536, 384)
                      zproj(wt, wk, 0, e0, 12, lzs[:], "lzs", mus=mus)
                      C.op("act", lambda e: e.activation(out=thb[:], in_=lzs[:], func=AF.Tanh), R=["lzs"], W=["thb"])
                      if slot is not None:
                          C.op("dve", lambda e: e.tensor_scalar(out=thb[:], in0=thb[:], scalar1=dmask, scalar2=None, op0=ALU.mult),
                               R=["thb", "slp"], W=["thb"])
                      zproj(wt, wk, 128, e0, 13, lzs[:], "lzs", mus=mus)
                      if slot is not None:
                          C.op("dve", lambda e: e.tensor_scalar(out=adb[:], in0=lzs[:], scalar1=dmask, scalar2=None, op0=ALU.mult),
                               R=["lzs", "slp"], W=["adb"])
                      else:
                          C.op("pool", lambda e: e.tensor_copy(out=adb[:], in_=lzs[:]), R=["lzs"], W=["adb"])
                      dbg_dump("s1", thb[:])
                      if (not fwd) and not state_only:
                          zproj(wt, wk, 256, e0, 14, lzs[:], "lzs")
                          C.op("act", lambda e: e.activation(out=sgb[:], in_=lzs[:], func=AF.Sigmoid), R=["lzs"], W=["sgb"])
                      for p in range(4):
                          for n, base in (("r", 0), ("k", 512), ("v", 1024)):
                              if state_only and n == "r":
                                  continue
                              wt, wk = load_w(w_in, 0, base + p * 128, 128)
                              zproj(wt, wk, 0, e0, (base // 128) + p, zsf[n][:], f"zs_{n}", mus=mus)
                          pa = PA[2]
                          if slot is None:
                              mm(pa[:], lorab[pb:pb + 64, p * 128:(p + 1) * 128], thb[pb:pb + 64, :], True, True,
                                 R=["lorab", "thb"], W=[pk(pa)], inc=True)
                              w0ap = cv("w0f" if fwd else "w0b", p, p + 1)
                              a0ap = cv("a0f" if fwd else "a0b", p, p + 1)
                          else:
                              mm(pa[:], lorab[:, p * 128:(p + 1) * 128], thb[:, :], True, True,
                                 R=["lorab", "thb"], W=[pk(pa)], inc=True)
                              w0ap = sp_[:, p:p + 1]
                              a0ap = sp_[:, 4 + p:5 + p]
                          C.op("act", lambda e, pa=pa, p=p, w0ap=w0ap: e.activation(
                              out=sgw[:], in_=pa[:], func=AF.Sigmoid,
                              bias=w0ap, scale=1.0), R=[pk(pa), "cs", "slp"], W=["sgw"])
                          pa2 = PA[3]
                          if slot is None:
                              mm(pa2[:], lorab[pb:pb + 64, 512 + p * 128:512 + (p + 1) * 128], adb[pb:pb + 64, :], True, True,
                                 R=["lorab", "adb"], W=[pk(pa2)], inc=True)
                          else:
                              mm(pa2[:], lorab[:, 512 + p * 128:512 + (p + 1) * 128], adb[:, :], True, True,
                                 R=["lorab", "adb"], W=[pk(pa2)], inc=True)
                          C.op("act", lambda e, pa2=pa2, p=p, a0ap=a0ap: e.activation(
                              out=ar[:], in_=pa2[:], func=AF.Sigmoid,
                              bias=a0ap, scale=1.0), R=[pk(pa2), "cs", "slp"], W=["ar"])
                          C.op("act", lambda e, p=p: e.activation(out=ksq[:], in_=zsf["k"][:], func=AF.Square,
                                                                  scale=cv("kk", p, p + 1)), R=["zs_k", "cs"], W=["ksq"])
                          pa = PA[2]
                          mm(pa[:], bonesb[:], ksq[:], True, True, R=["bonesb", "ksq"], W=[pk(pa)], inc=True)
                          C.op("act", lambda e, pa=pa: e.activation(out=g0t[:], in_=pa[:], func=AF.Ln, bias=epsc[:, 1:2], scale=1.0),
                               R=[pk(pa), "epsc"], W=["g0t"])
                          C.op("act", lambda e: e.activation(out=g0t[:], in_=g0t[:], func=AF.Exp, scale=-0.5), R=["g0t"], W=["g0t"])
                          C.op("dve", lambda e, p=p: e.scalar_tensor_tensor(out=kkn[:], in0=zsf["k"][:], scalar=cv("kk", p, p + 1),
                                                                            in1=g0t[:], op0=ALU.mult, op1=ALU.mult),
                               R=["zs_k", "cs", "g0t"], W=["kkn"])
                          C.op("dve", lambda e: e.tensor_tensor_scan(out=cumx[:, 1:TT + 1], data0=ones_bc, data1=sgw[:],
                                                                     initial=0.0, op0=ALU.mult, op1=ALU.add),
                               R=["cs", "sgw"], W=["cumx"])
                          cst_ = c3(cumx[:, 0:TT])[:, :, 0:1].to_broadcast([128, 8, 64])
                          cen_ = c3(cumx[:, 1:TT + 1])[:, :, 63:64].to_broadcast([128, 8, 64])
                          cK = ["cumx", "cumx0"]
                          if fwd:
                              C.op("pool", lambda e: e.tensor_tensor(out=c3(dd[0][:]), in0=c3(cumx[:, 1:TT + 1]), in1=cst_, op=ALU.subtract), R=cK, W=["dd0"])
                              C.op("pool", lambda e: e.tensor_tensor(out=c3(dd[1][:]), in0=c3(cumx[:, 0:TT]), in1=cst_, op=ALU.subtract), R=cK, W=["dd1"])
                              C.op("pool", lambda e: e.tensor_tensor(out=c3(dd[2][:]), in0=cen_, in1=c3(cumx[:, 1:TT + 1]), op=ALU.subtract), R=cK, W=["dd2"])
                          else:
                              C.op("pool", lambda e: e.tensor_tensor(out=c3(dd[0][:]), in0=cen_, in1=c3(cumx[:, 0:TT]), op=ALU.subtract), R=cK, W=["dd0"])
                              C.op("pool", lambda e: e.tensor_tensor(out=c3(dd[1][:]), in0=cen_, in1=c3(cumx[:, 1:TT + 1]), op=ALU.subtract), R=cK, W=["dd1"])
                              C.op("pool", lambda e: e.tensor_tensor(out=c3(dd[2][:]), in0=c3(cumx[:, 0:TT]), in1=cst_, op=ALU.subtract), R=cK, W=["dd2"])
                          C.op("act", lambda e: e.activation(out=g0t[:], in_=dd[0][:], func=AF.Exp, scale=-KAPPA), R=["dd0"], W=["g0t"])
                          C.op("act", lambda e: e.activation(out=dd[0][:], in_=dd[0][:], func=AF.Exp, scale=KAPPA), R=["dd0"], W=["dd0"])
                          C.op("act", lambda e: e.activation(out=dd[1][:], in_=dd[1][:], func=AF.Exp, scale=-KAPPA), R=["dd1"], W=["dd1"])
                          C.op("act", lambda e: e.activation(out=dd[2][:], in_=dd[2][:], func=AF.Exp, scale=-KAPPA), R=["dd2"], W=["dd2"])
                          Gt, Ginv, Gp, Gaft = g0t, dd[0], dd[1], dd[2]
                          dbg_dump("s2", thb[:])
                          gcol = 63 if fwd else 0
                          C.op("pool", lambda e, p=p: e.tensor_copy(out=GCt[:, p, :], in_=g0t[:, gcol:TT:64]), R=["g0t"], W=[("GCt", p)])
                          ARp = ARt[:, p].rearrange("p c (two t) -> p c two t", two=2)
                          C.op("dve", lambda e: e.scalar_tensor_tensor(out=ARp[:, :, 0, :], in0=c3(kkn[:]), scalar=-1.0,
                                                                       in1=c3(Gp[:]), op0=ALU.mult, op1=ALU.mult),
                               R=["kkn", "dd1"], W=[("ARt", p)])
                          rg = Gt if fwd else Gp
                          if not state_only:
                              C.op("pool", lambda e: e.tensor_tensor(out=ARp[:, :, 1, :], in0=c3(zsf["r"][:]), in1=c3(rg[:]), op=ALU.mult),
                                   R=["zs_r", "g0t", "dd1"], W=[("ARt", p)])
                          C.op("pool", lambda e: e.tensor_tensor(out=kkn[:], in0=kkn[:], in1=ar[:], op=ALU.mult), R=["kkn", "ar"], W=["kkn"])
                          C.op("dve", lambda e, p=p: e.tensor_tensor(out=Btt[:, p, :], in0=kkn[:], in1=Ginv[:], op=ALU.mult), R=["kkn", "dd0"], W=[("Btt", p)])
                          C.op("pool", lambda e, p=p: e.tensor_tensor(out=Bht[:, p, :], in0=kkn[:], in1=Gaft[:], op=ALU.mult), R=["kkn", "dd2"], W=[("Bht", p)])
                          C.op("dve", lambda e, p=p: e.tensor_scalar(out=ar[:], in0=ar[:], scalar1=cv("ka", p, p + 1), scalar2=omka[:, p:p + 1],
                                                                     op0=ALU.mult, op1=ALU.add), R=["ar", "cs"] + DVK, W=["ar"])
                          kd = zsf["k"]
                          C.op("pool", lambda e: e.tensor_tensor(out=kd[:], in0=zsf["k"][:], in1=ar[:], op=ALU.mult), R=["zs_k", "ar"], W=["zs_k"])
                          C.op("dve", lambda e, p=p: e.tensor_tensor(out=Ktt[:, p, :], in0=kd[:], in1=Ginv[:], op=ALU.mult), R=["zs_k", "dd0"], W=[("Ktt", p)])
                          C.op("pool", lambda e, p=p: e.tensor_tensor(out=Kht[:, p, :], in0=kd[:], in1=Gaft[:], op=ALU.mult), R=["zs_k", "dd2"], W=[("Kht", p)])
                          dbg_dump("s3", thb[:])
                          if not state_only:
                              C.op("dve", lambda e, p=p: e.scalar_tensor_tensor(out=prodb[:], in0=zsf["r"][:], scalar=hrk[:, p:p + 1], in1=kd[:],
                                                                                op0=ALU.mult, op1=ALU.mult), R=["zs_r", "zs_k"] + DVK, W=["prodb"])
                              for c in range(8):
                                  for h in range(2):
                                      mm(QX.t[64 * h:64 * h + 64, h * 512 + c:h * 512 + c + 1], prodb[64 * h:64 * h + 64, c * 64:(c + 1) * 64],
                                         onesb[64 * h:64 * h + 64, 0:1], True, True, R=["prodb", "onesb"], W=[pk(QX.b[h])],
                                         inc=(c == 7 and h == 1))
                              for h in range(2):
                                  hs = slice(64 * h, 64 * h + 64)
                                  C.op("dve", lambda e, p=p, t5=t5, h=h, hs=hs: e.tensor_tensor(
                                      out=sbon[hs, t5 * 8:(t5 + 1) * 8, p], in0=sbon[hs, t5 * 8:(t5 + 1) * 8, p],
                                      in1=QX.t[hs, h * 512:h * 512 + 8], op=ALU.add), R=dk(QX) + ["sbon"], W=["sbon"])
                          dbg_dump("s4", thb[:])
                          C.op("act", lambda e: e.activation(out=vb[:], in_=zsf["v"][:], func=AF.Copy), R=["zs_v"], W=["vb"])
                          for src3, dstT, nm, sk in ((None, Vst, "Vst", "vb"), (Bht, BhT, "BhT", ("Bht", p)), (Kht, KhT, "KhT", ("Kht", p))):
                              for c in range(8):
                                  src_ap = (vb[:, c * 64:(c + 1) * 64] if src3 is None else src3[:, p, c * 64:(c + 1) * 64])
                                  for h in range(2):
                                      mm(PS6[64 * h:64 * h + 64, c * 64:(c + 1) * 64], src_ap, identb[:, 64 * h:64 * h + 64], True, True,
                                         R=[sk, "identb"], W=[pk(PS6)], inc=(c == 7 and h == 1))
                              evac(dstT[:, p].rearrange("p c v -> p (c v)"), PS6[:, 0:512], R=[pk(PS6)], W=[(nm, p)])
                      dbg_dump("s5", thb[:])
                      chunks = range(8) if fwd else range(7, -1, -1)
                      def chunk_fns(c):
                          gc = t5 * 8 + c
                          def st_gram(p):
                              At_ = lambda h: ARt[64 * h:64 * h + 64, p, c, 0:64]
                              Rt_ = lambda h: ARt[64 * h:64 * h + 64, p, c, 64:128]
                              Bt_ = lambda h: Btt[64 * h:64 * h + 64, p, c * 64:(c + 1) * 64]
                              Kt_ = lambda h: Ktt[64 * h:64 * h + 64, p, c * 64:(c + 1) * 64]
                              RK = [("ARt", p), ("Btt", p), ("Ktt", p)]
                              for h in range(2):
                                  hs = slice(64 * h, 64 * h + 64)
                                  o = h * 512
                                  WK = [pk(QX.b[h])]
                                  mm(QX.t[hs, o:o + 64], Bt_(h), At_(h), True, True, R=RK, W=WK)
                                  mm(QX.t[hs, o + 64:o + 128], At_(h), Bt_(h), True, True, R=RK, W=WK)
                                  mm(QX.t[hs, o + 128:o + 192], Kt_(h), At_(h), True, True, R=RK, W=WK, inc=(state_only and h == 1))
                                  if not state_only:
                                      mm(QX.t[hs, o + 192:o + 256], Bt_(h), Rt_(h), True, True, R=RK, W=WK)
                                      mm(QX.t[hs, o + 256:o + 320], Kt_(h), Rt_(h), True, True, R=RK, W=WK, inc=(h == 1))
                              for h in range(2):
                                  hs = slice(64 * h, 64 * h + 64)
                                  o = h * 512
                                  dsl = slice(64 * h, 64 * h + 64)
                                  C.op("dve", lambda e, hs=hs, o=o, dsl=dsl: e.tensor_tensor(
                                      out=Lp[p][hs, 128:512].rearrange("p (s t) -> p s t", s=3)[:, :, dsl],
                                      in0=QX.t[hs, o:o + 192].rearrange("p (s t) -> p s t", s=3),
                                      in1=mX[hs, :].rearrange("p (s t) -> p s t", s=3)[:, :, dsl], op=ALU.mult),
                                      R=dk(QX) + ["mskb"], W=[("LP", p), ("LM", p)])
                                  if state_only:
                                      continue
                                  C.op("dve", lambda e, hs=hs, o=o, dsl=dsl: e.tensor_tensor(
                                      out=Ygb[hs, p, :].rearrange("p (s t) -> p s t", s=2)[:, :, dsl],
                                      in0=QX.t[hs, o + 192:o + 320].rearrange("p (s t) -> p s t", s=2),
                                      in1=mY[hs, :].rearrange("p (s t) -> p s t", s=2)[:, :, dsl], op=ALU.mult),
                                      R=dk(QX) + ["mskb"], W=[("Ygb", p)])
                              C.op("pool", lambda e: e.tensor_tensor(out=Lp[p][:, 0:128], in0=Lp[p][:, 128:256], in1=identb[:], op=ALU.add),
                                   R=[("LP", p), "identb"], W=[("LT", p)])

                          def st_level(p, lev):
                              pd = PA[p]
                              Lt = Lp[p]
                              T_, P_, PT_ = Lt[:, 0:128], Lt[:, 128:256], Lt[:, 256:384]
                              LTk, LPk = ("LT", p), ("LP", p)
                              if lev >= 2:
                                  mm(pd[:, 0:128], identb[:], T_, True, False, R=[LTk, "identb"], W=[pk(pd)])
                                  mm(pd[:, 0:128], PT_, T_, False, True, R=[LPk, LTk], W=[pk(pd)], inc=(lev == 6))
                              if lev <= 5:
                                  mm(pd[:, 128:256], PT_, P_, True, True, R=[LPk], W=[pk(pd)])
                                  mm(pd[:, 256:384], P_, PT_, True, True, R=[LPk], W=[pk(pd)], inc=True)
                              if lev == 1:
                                  evac(Lt[:, 128:384], pd[:, 128:384], R=[pk(pd)], W=[LPk])
                              elif lev <= 5:
                                  evac(Lt[:, 0:384], pd[:, 0:384], R=[pk(pd)], W=[LTk, LPk])
                              else:
                                  evac(Lt[:, 0:128], pd[:, 0:128], R=[pk(pd)], W=[LTk])

                          def D_slot(tt):
                              for p in range(4):
                                  sg = tt - p
                                  if sg == 0:
                                      st_gram(p)
                                  elif 1 <= sg <= 6:
                                      st_level(p, sg)

                          def C1():
                              for p in range(4):
                                  for h in range(2):
                                      hs = slice(64 * h, 64 * h + 64)
                                      o = h * 512 + p * 64
                                      mm(QB.t[hs, o:o + 64], Lp[p][:, 384 + 64 * h:384 + 64 * h + 64], Vst[:, p, c, :], True, False,
                                         R=[("LM", p), ("Vst", p)], W=[pk(QB.b[h])])
                                      mm(QB.t[hs, o:o + 64], ARt[hs, p, c, 0:64], Sbf[hs, p, :], False, True,
                                         R=[("ARt", p), "Sbf"], W=[pk(QB.b[h])], inc=(p == 3 and h == 1))
                              for h in range(2):
                                  hs = slice(64 * h, 64 * h + 64)
                                  evac(Upb[hs].rearrange("p a v -> p (a v)"), QB.t[hs, h * 512:h * 512 + 256], R=dk(QB), W=["Upb"])

                          def C2():
                              pu2 = PS6
                              for p in range(4):
                                  o = p * 64
                                  mm(pu2[:, o:o + 64], Lp[p][:, 0:128], Upb[:, p, :], True, True, R=[("LT", p), "Upb"], W=[pk(pu2)], inc=(p == 3))
                              for h in range(2):
                                  hs = slice(64 * h, 64 * h + 64)
                                  evac(Ubz[hs, h].rearrange("p a v -> p (a v)"), pu2[hs, 0:256], R=[pk(pu2)], W=["Ub"])

                          def CY():
                              pass
                              if not state_only:
                                  for p in range(4):
                                      for h in range(2):
                                          hs = slice(64 * h, 64 * h + 64)
                                          o = h * 512 + p * 64
                                          WK = [pk(QX.b[h])]
                                          mm(QX.t[hs, o:o + 64], Ygb[:, p, 64 * h:64 * h + 64], Ubz[:, h, p, :], True, False, R=[("Ygb", p), "Ub"], W=WK)
                                          mm(QX.t[hs, o:o + 64], Ygb[:, p, 128 + 64 * h:128 + 64 * h + 64], Vst[:, p, c, :], False, False,
                                             R=[("Ygb", p), ("Vst", p)], W=WK)
                                          mm(QX.t[hs, o:o + 64], ARt[hs, p, c, 64:128], Sbf[hs, p, :], False, True,
                                             R=[("ARt", p), "Sbf"], W=WK, inc=(p == 3 and h == 1))
                                  for h in range(2):
                                      hs = slice(64 * h, 64 * h + 64)
                                      ysrc = QX.t[hs, h * 512:h * 512 + 256]
                                      if fwd:
                                          evac(Ofb[hs, gc].rearrange("p a v -> p (a v)"), ysrc, R=dk(QX), W=[("Ofb", gc)])
                                      else:
                                          C.op("dve", lambda e, hs=hs, ysrc=ysrc, gc=gc, c=c: e.tensor_tensor(
                                              out=fina[hs, c % 4].rearrange("p a v -> p (a v)"), in0=Ofb[hs, gc].rearrange("p a v -> p (a v)"),
                                              in1=ysrc, op=ALU.add), R=dk(QX) + [("Ofb", gc)], W=["fina"])

                          def C3():
                              QS = QX if state_only else QB
                              for p in range(4):
                                  for h in range(2):
                                      hs = slice(64 * h, 64 * h + 64)
                                      o = h * 512 + p * 64
                                      mm(QS.t[hs, o:o + 64], BhT[:, p, c, :], Ubz[:, h, p, :], True, False, R=[("BhT", p), "Ub"], W=[pk(QS.b[h])])
                                      mm(QS.t[hs, o:o + 64], KhT[hs, p, c, :], Vst[hs, p, c, :], False, True, R=[("KhT", p), ("Vst", p)],
                                         W=[pk(QS.b[h])], inc=(p == 3 and h == 1))
                              C.op("dve", lambda e, c=c: e.tensor_tensor(out=Stmp[:], in0=Sst[:], in1=GCt[:, :, c:c + 1].to_broadcast([128, 4, 64]), op=ALU.mult),
                                   R=["Sst"] + [("GCt", p) for p in range(4)], W=["Stmp"])
                              for h in range(2):
                                  hs = slice(64 * h, 64 * h + 64)
                                  C.op("dve", lambda e, hs=hs, h=h: e.tensor_tensor(
                                      out=Sst[hs].rearrange("p a v -> p (a v)"), in0=Stmp[hs].rearrange("p a v -> p (a v)"),
                                      in1=QS.t[hs, h * 512:h * 512 + 256], op=ALU.add), R=["Stmp"] + dk(QS), W=["Sst"])
                              C.op("act", lambda e: e.activation(out=Sbf[:], in_=Sst[:], func=AF.Copy), R=["Sst"], W=["Sbf"])

                          def FIN():
                              if (not fwd) and (not state_only) and c % 4 == 0:
                                  finalize(t5, c // 4)
                          return D_slot, C1, C2, CY, C3, FIN

                      clist = list(chunks)
                      fns = {c: chunk_fns(c) for c in clist}
                      if state_only:
                          for idx, c in enumerate(clist):
                              D_, C1_, C2_, CY_, C3_, FIN_ = fns[c]
                              if idx == 0:
                                  for tt in range(10):
                                      D_(tt)
                              C1_()
                              C2_()
                              if idx + 1 < len(clist):
                                  Dn = fns[clist[idx + 1]][0]
                                  Dn(0)
                                  C3_()
                                  for tt in range(1, 10):
                                      Dn(tt)
                              else:
                                  C3_()
                      else:
                          for c in clist:
                              D_, C1_, C2_, CY_, C3_, FIN_ = fns[c]
                              for tt in range(10):
                                  D_(tt)
                              C1_()
                              C2_()
                              CY_()
                              C3_()
                              FIN_()
                      C.maybe_rotate()

              if s == 0 and WITH_XCORE:
                  xt_v = [fina[:].rearrange("p c a v -> p (c a v)")]
                  xb_v = [uab[:].rearrange("p c a v -> p (c a v)")]
                  selb = slp[:, 280:288]

                  def boundary(i):
                      C.op("dve", lambda e: e.scalar_tensor_tensor(out=Ssave[:, 1], in0=Sst[:], scalar=selb[:, i:i + 1], in1=Ssave[:, 1],
                                                                   op0=ALU.mult, op1=ALU.add), R=["Sst", "slp", ("Ssave", 1)], W=[("Ssave", 1)])
                      C.op("dve", lambda e: e.tensor_scalar(out=Stmp[:, 0, 0:1], in0=selb[:, i:i + 1], scalar1=-1.0, scalar2=1.0,
                                                            op0=ALU.mult, op1=ALU.add), R=["slp"], W=["Stmp"])
                      C.op("dve", lambda e: e.tensor_scalar(out=Sst[:], in0=Sst[:], scalar1=Stmp[:, 0, 0:1], scalar2=None, op0=ALU.mult),
                           R=["Sst", "Stmp"], W=["Sst"])
                      C.op("act", lambda e: e.activation(out=Sbf[:], in_=Sst[:], func=AF.Copy), R=["Sst"], W=["Sbf"])

                  C.op("dve", lambda e: e.memset(Ssave[:], 0.0), W=[("Ssave", 0), ("Ssave", 1)])
                  C.op("dve", lambda e: e.memset(Sst[:], 0.0), W=["Sst"])
                  C.op("dve", lambda e: e.memset(Sbf[:], 0.0), W=["Sbf"])
                  for j in range(7):
                      boundary(j)
                      fill_hT(xo, j * SLOT_EXT, list(range(1, 1 + SLOT_EXT // 128)), xt_v, xb_v, ["fina"], ["uab"])
                      rwkv_pass(0, state_only=True, init="keep", slot=j)
                  boundary(7)
                  C.op("dve", lambda e: e.tensor_copy(out=Ssave[:, 0], in_=Sst[:]), R=["Sst"], W=[("Ssave", 0)])
                  fill_hT(xs, xoff, list(range(EXT // 128)), xt_v, xb_v, ["fina"], ["uab"])
                  rwkv_pass(0, init="saved")
                  rwkv_pass(1, init="saved")
              else:
                  rwkv_pass(0)
                  dbg_dump("p3f", Ofb[:].rearrange("p c a v -> p (c a v)"))
                  rwkv_pass(1)
              dbg_dump("p3", UaT[:].rearrange("p k t -> p (k t)"))
              C.barrier()
          C.maybe_rotate()

          with ExitStack() as es4:
              alloc_wbuf(es4, f"p4s{s}", 512)
              xT = sb("xT", [128, 8, TT], F32, es4)
              xin = [sb(f"xin{i}", [128, D], F32, es4) for i in range(2)]
              identf = sb("identf", [128, 128], F32, es4)
              hb = sb("hb", [128, 8, TT], BF, es4)
              mT = sb("mT", [128, 8, TT], BF, es4)
              aT = sb("aT", [128, 22, TT], BF, es4)
              sqb = sb("sqb", [128, 8, TT], BF, es4)
              rsb = sb("rsb", [128, TT], F32, es4)
              sga = sb("sga", [128, TT], F32, es4)
              sgn = sb("sgn", [128, TT], F32, es4)
              t1 = sb("t1", [128, TT], F32, es4)
              pin = [sb(f"pin{i}", [128, 256], F32, es4) for i in range(2)]
              pbf = sb("pbf", [128, 256], BF, es4)
              pT = sb("pT", [128, 2, TT], BF, es4)
              yo = [sb(f"yo{i}", [128, D], F32, es4) for i in range(2)]
              C.op("dve", lambda e: e.tensor_copy(out=identf[:], in_=cv("ident")), R=["cs"], W=["identf"])

              def rms_bcast(gname, dst):
                  C.op("act", lambda e: e.activation(out=sqb[:], in_=xT[:], func=AF.Square), R=["xT"], W=["sqb"])
                  pa = PA[0]
                  for kc in range(8):
                      mm(pa[:], onesb[:], sqb[:, kc, :], kc == 0, kc == 7, R=["onesb", "sqb"], W=[pk(pa)], inc=(kc == 7))
                  C.op("act", lambda e: e.activation(out=rsb[:], in_=pa[:], func=AF.Ln, bias=epsc[:, 0:1], scale=1.0 / D),
                       R=[pk(pa), "epsc"], W=["rsb"])
                  C.op("act", lambda e: e.activation(out=rsb[:], in_=rsb[:], func=AF.Exp, scale=-0.5), R=["rsb"], W=["rsb"])
                  for kc in range(8):
                      C.op("dve", lambda e, kc=kc: e.scalar_tensor_tensor(out=dst[:, kc, :], in0=xT[:, kc, :], scalar=cv(gname, kc, kc + 1),
                                                                          in1=rsb[:], op0=ALU.mult, op1=ALU.mult),
                           R=["xT", "cs", "rsb"], W=["hb"])

              for t5 in range(NTILE):
                  e0 = HALO + t5 * TT
                  tsl = slice(t5 * TT, (t5 + 1) * TT)
                  for b4 in range(4):
                      i = b4 % 2
                      C.dma("sp", xin[i][:], xs[xoff + e0 + b4 * 128: xoff + e0 + (b4 + 1) * 128, :], W=[f"xin{i}"])
                      for half in range(2):
                          pa = PA[(b4 * 2 + half) % 4]
                          for q in range(4):
                              kc = half * 4 + q
                              C.op("pe", lambda e, pa=pa, q=q, kc=kc, i=i: e.transpose(
                                  out=pa[:, q * 128:(q + 1) * 128], in_=xin[i][:, kc * 128:(kc + 1) * 128], identity=identf[:]),
                                  R=[f"xin{i}", "identf"], W=[pk(pa)], inc=(q == 3))
                          evac(xT[:, half * 4:half * 4 + 4, b4 * 128:(b4 + 1) * 128],
                               pa[:].rearrange("p (k t) -> p k t", k=4), R=[pk(pa)], W=["xT"])
                  for sweep, (usrc, ukey, wsrc, gbase, sgt) in enumerate(((UaT, "UaT", w_bra, 3456, sga), (UnT, "UnT", w_brn, 4480, sgn))):
                      for half in range(2):
                          wb_, wbk = load_w(wsrc, 0, half * 512, 512, nk=4)
                          wg_, wgk = load_w(w_in, 0, gbase + half * 512, 512)
                          for q in range(4):
                              dc = half * 4 + q
                              pg = PA[0]
                              pyv = PA[1]
                              for kc in range(8):
                                  mm(pg[:], wg_[:, kc, q * 128:(q + 1) * 128], hT[:, kc, e0:e0 + TT], kc == 0, kc == 7,
                                     R=[wgk, ("hT", e0 // 512), ("hT", e0 // 512 + 1)], W=[pk(pg)], inc=(kc == 7))
                              for kc in range(4):
                                  mm(pyv[:], wb_[:, kc, q * 128:(q + 1) * 128], usrc[:, kc, tsl], kc == 0, kc == 3,
                                     R=[wbk, (ukey, t5)], W=[pk(pyv)], inc=(kc == 3))
                              C.op("act", lambda e, pg=pg, sgt=sgt: e.activation(out=sgt[:], in_=pg[:], func=AF.Sigmoid), R=[pk(pg)], W=[sgt.name])
                              if sweep == 0:
                                  C.op("dve", lambda e, pyv=pyv, dc=dc, sgt=sgt: e.tensor_tensor(out=hb[:, dc, :], in0=sgt[:], in1=pyv[:], op=ALU.mult),
                                       R=[sgt.name, pk(pyv)], W=["hb"])
                              else:
                                  C.op("dve", lambda e, pyv=pyv, sgt=sgt: e.tensor_tensor(out=t1[:], in0=sgt[:], in1=pyv[:], op=ALU.mult),
                                       R=[sgt.name, pk(pyv)], W=["t1"])
                                  C.op("pool", lambda e, dc=dc: e.tensor_tensor(out=mT[:, dc, :], in0=t1[:], in1=hb[:, dc, :], op=ALU.add),
                                       R=["t1", "hb"], W=["mT"])
                  for half in range(2):
                      wo, wok = load_w(w_out, 0, half * 512, 512)
                      for q in range(4):
                          dc = half * 4 + q
                          pa = PA[dc % 2]
                          for kc in range(8):
                              mm(pa[:], wo[:, kc, q * 128:(q + 1) * 128], mT[:, kc, :], kc == 0, kc == 7, R=[wok, "mT"], W=[pk(pa)], inc=(kc == 7))
                          C.op("dve", lambda e, pa=pa, dc=dc: e.tensor_tensor(out=xT[:, dc, :], in0=xT[:, dc, :], in1=pa[:], op=ALU.add),
                               R=["xT", pk(pa)], W=["xT"])
                  rms_bcast("gffn", hb)
                  for f0 in range(0, DFF, 512):
                      nc_ = min(512, DFF - f0)
                      wg_, wgk = load_w(w_gate, 0, f0, nc_)
                      wu_, wuk = load_w(w_up, 0, f0, nc_)
                      for q in range(nc_ // 128):
                          fc = f0 // 128 + q
                          pg = PA[(fc % 2) * 2]
                          pu = PA[(fc % 2) * 2 + 1]
                          for kc in range(8):
                              mm(pg[:], wg_[:, kc, q * 128:(q + 1) * 128], hb[:, kc, :], kc == 0, kc == 7, R=[wgk, "hb"], W=[pk(pg)], inc=(kc == 7))
                          for kc in range(8):
                              mm(pu[:], wu_[:, kc, q * 128:(q + 1) * 128], hb[:, kc, :], kc == 0, kc == 7, R=[wuk, "hb"], W=[pk(pu)], inc=(kc == 7))
                          C.op("act", lambda e, pg=pg: e.activation(out=sga[:], in_=pg[:], func=AF.Silu), R=[pk(pg)], W=["sga"])
                          C.op("dve", lambda e, pu=pu, fc=fc: e.tensor_tensor(out=aT[:, fc, :], in0=sga[:], in1=pu[:], op=ALU.mult),
                               R=["sga", pk(pu)], W=[("aT", fc)])
                  for half in range(2):
                      pas = [PA[q] for q in range(4)]
                      for g0 in range(0, 22, 8):
                          ng = min(8, 22 - g0)
                          wd_, wdk = load_w(w_down, g0 * 128, half * 512, 512, nk=ng)
                          for q in range(4):
                              for k in range(ng):
                                  fc = g0 + k
                                  mm(pas[q][:], wd_[:, k, q * 128:(q + 1) * 128], aT[:, fc, :], fc == 0, fc == 21,
                                     R=[wdk, ("aT", fc)], W=[pk(pas[q])], inc=(fc == 21 or k == ng - 1))
                      for q in range(4):
                          dc = half * 4 + q
                          C.op("dve", lambda e, q=q, dc=dc, pas=pas: e.tensor_tensor(out=xT[:, dc, :], in0=xT[:, dc, :], in1=pas[q][:], op=ALU.add),
                               R=["xT", pk(pas[q])], W=["xT"])
                  for b4 in range(4):
                      i = b4 % 2
                      C.dma("sp", pin[i][:], pp[s * SEQT + t5 * TT + b4 * 128: s * SEQT + t5 * TT + (b4 + 1) * 128, :], W=[f"pin{i}"])
                      C.op("pool", lambda e, i=i: e.tensor_copy(out=pbf[:], in_=pin[i][:]), R=[f"pin{i}"], W=["pbf"])
                      pt = PT[0]
                      for kc in range(2):
                          tp(pt[:, kc * 128:(kc + 1) * 128], pbf[:, kc * 128:(kc + 1) * 128], R=["pbf"], W=[pk(pt)], inc=(kc == 1))
                      evac(pT[:, :, b4 * 128:(b4 + 1) * 128], pt[:, 0:256].rearrange("p (k t) -> p k t", k=2), R=[pk(pt)], W=["pT"])
                  rms_bcast("gple", hb)
                  for half in range(2):
                      wp_, wpk = load_w(w_pg, 0, half * 512, 512)
                      we_, wek = load_w(w_ple, 0, half * 512, 512, nk=2)
                      for q in range(4):
                          dc = half * 4 + q
                          pg = PA[(dc % 2) * 2]
                          pe_ = PA[(dc % 2) * 2 + 1]
                          for kc in range(8):
                              mm(pg[:], wp_[:, kc, q * 128:(q + 1) * 128], hb[:, kc, :], kc == 0, kc == 7, R=[wpk, "hb"], W=[pk(pg)], inc=(kc == 7))
                          for kc in range(2):
                              mm(pe_[:], we_[:, kc, q * 128:(q + 1) * 128], pT[:, kc, :], kc == 0, kc == 1, R=[wek, "pT"], W=[pk(pe_)], inc=(kc == 1))
                          C.op("act", lambda e, pg=pg: e.activation(out=sga[:], in_=pg[:], func=AF.Sigmoid), R=[pk(pg)], W=["sga"])
                          C.op("dve", lambda e, pe_=pe_: e.tensor_tensor(out=t1[:], in0=sga[:], in1=pe_[:], op=ALU.mult), R=["sga", pk(pe_)], W=["t1"])
                          C.op("pool", lambda e, dc=dc: e.tensor_tensor(out=xT[:, dc, :], in0=xT[:, dc, :], in1=t1[:], op=ALU.add), R=["xT", "t1"], W=["xT"])
                  C.op("act", lambda e: e.activation(out=sqb[:], in_=xT[:], func=AF.Square), R=["xT"], W=["sqb"])
                  pa = PA[0]
                  for kc in range(8):
                      mm(pa[:], onesb[:], sqb[:, kc, :], kc == 0, kc == 7, R=["onesb", "sqb"], W=[pk(pa)], inc=(kc == 7))
                  C.op("act", lambda e, pa=pa: e.activation(out=rsb[:], in_=pa[:], func=AF.Ln, bias=epsc[:, 0:1], scale=1.0 / D), R=[pk(pa), "epsc"], W=["rsb"])
                  C.op("act", lambda e: e.activation(out=rsb[:], in_=rsb[:], func=AF.Exp, scale=-0.5), R=["rsb"], W=["rsb"])
                  for kc in range(8):
                      C.op("dve", lambda e, kc=kc: e.scalar_tensor_tensor(out=xT[:, kc, :], in0=xT[:, kc, :], scalar=cv("gfin", kc, kc + 1),
                                                                          in1=rsb[:], op0=ALU.mult, op1=ALU.mult), R=["xT", "cs", "rsb"], W=["xT"])
                  for b4 in range(4):
                      i = b4 % 2
                      for half in range(2):
                          pa = PA[(b4 * 2 + half) % 4]
                          for q in range(4):
                              kc = half * 4 + q
                              C.op("pe", lambda e, pa=pa, q=q, kc=kc, b4=b4: e.transpose(
                                  out=pa[:, q * 128:(q + 1) * 128], in_=xT[:, kc, b4 * 128:(b4 + 1) * 128], identity=identf[:]),
                                  R=["xT", "identf"], W=[pk(pa)], inc=(q == 3))
                          evac(yo[i][:, half * 512:(half + 1) * 512], pa[:], R=[pk(pa)], W=[f"yo{i}"])
                      C.dma("sp", y_d[s * SEQT + t5 * TT + b4 * 128: s * SEQT + t5 * TT + (b4 + 1) * 128, :], yo[i][:], R=[f"yo{i}"])
                  C.maybe_rotate()
              C.barrier()

    except _Stop:
        print("instructions:", C.nins)
        nc._ctx = C
        return nc
    C.barrier()
    ES.close()
    print("instructions:", C.nins)
    nc._ctx = C
    return nc


def simulate_sync(C):
    pcs = {e: 0 for e in C.trace}
    sems = {}
    progress = True
    while progress:
        progress = False
        for e, tr in C.trace.items():
            while pcs[e] < len(tr):
                k, key, v = tr[pcs[e]]
                if k == "w":
                    if sems.get(key, 0) >= v:
                        pcs[e] += 1
                        progress = True
                    else:
                        break
                else:
                    sems[key] = sems.get(key, 0) + v
                    pcs[e] += 1
                    progress = True
    stuck = {e: (pcs[e], len(tr), tr[pcs[e]] if pcs[e] < len(tr) else None) for e, tr in C.trace.items()}
    ok = all(pcs[e] == len(tr) for e, tr in C.trace.items())
    return ok, stuck, sems


_PROG = {}


def kernel(**inp):
    inp = {k: np.asarray(v) for k, v in inp.items()}
    xp = inp["x_prompt"][0]
    xsm = inp["x_sample"]
    ppm = inp["p_prompt"][0, 0]
    psm = inp["p_sample"][0]
    f32 = lambda a: np.ascontiguousarray(a, dtype=np.float32)
    shared = {
        "nab": _build_nab(inp["rpb"][0]),
        "msk": np.ascontiguousarray(np.concatenate([_masks()[k] for k in ("mXf", "mYf", "mXb", "mYb")], 1)),
        "w_in": f32(inp["w_in"][0]),
        "w_lora": f32(np.concatenate([np.concatenate([inp["w2_f"][0], inp["w2_b"][0]], 0),
                                      np.concatenate([inp["a2_f"][0], inp["a2_b"][0]], 0)], 1)),
        "g2": f32(inp["g2"][0]),
        "w_br_a": f32(inp["w_br_a"][0]), "w_br_n": f32(inp["w_br_n"][0]), "w_out": f32(inp["w_out"][0]),
        "w_gate": f32(inp["w_gate"][0]), "w_up": f32(inp["w_up"][0]), "w_down": f32(inp["w_down"][0]),
        "w_ple": f32(inp["w_ple"][0]), "w_pg": f32(inp["w_pg"][0]),
    }
    in_maps = []
    for c in range(NCORE):
        xs = np.zeros((3, EXT, D), np.float32)
        lo = c * SEQT - HALO
        hi = c * SEQT + SEQT + HALO
        a, b = max(lo, 0), min(hi, xp.shape[0])
        xs[0, a - lo:b - lo] = xp[a:b]
        xs[1, HALO:HALO + SEQT] = xsm[2 * c]
        xs[2, HALO:HALO + SEQT] = xsm[2 * c + 1]
        pp = np.stack([ppm[c * SEQT:(c + 1) * SEQT], psm[2 * c], psm[2 * c + 1]], 0)
        m = dict(shared)
        xo = np.zeros((7, SLOT_EXT, D), np.float32)
        slp = np.zeros((128, 7 * 40 + 8), np.float32)
        nb = NCORE - 1 - c
        for i in range(7):
            isb = i < nb
            g = (NCORE - 1 - i) if isb else (i - nb)
            lo2 = g * SEQT - 128
            hi2 = g * SEQT + SEQT + 128
            a2, b2 = max(lo2, 0), min(hi2, xp.shape[0])
            seg = np.zeros((SLOT_EXT, D), np.float32)
            seg[a2 - lo2:b2 - lo2] = xp[a2:b2]
            xo[i] = seg[::-1] if isb else seg
            o = i * 40
            slp[:, o:o + 4] = _pp(inp["w0_b" if isb else "w0_f"][0], 4)
            slp[:, o + 4:o + 8] = _pp(inp["a0_b" if isb else "a0_f"][0], 4)
            slp[:, o + 8:o + 23] = _pp(inp["mu_next" if isb else "mu_prev"][0], 15)
            slp[:, o + 23:o + 38] = _pp(inp["mu_prev" if isb else "mu_next"][0], 15)
            slp[64:128 if isb else 0:64, o + 38] = 0.0
            slp[(64 if isb else 0):(128 if isb else 64), o + 38] = 1.0
        slp[:, 280 + nb] = 1.0
        m["xo"] = xo.reshape(7 * SLOT_EXT, D)
        m["slp"] = slp
        m["xs"] = xs.reshape(3 * EXT, D)
        m["pp"] = f32(pp.reshape(3 * SEQT, 256))
        m["cst"] = _build_cst(inp, c)
        in_maps.append(m)
    if "nc" not in _PROG:
        _PROG["nc"] = build_program()
    res = run_bass_kernel_spmd(_PROG["nc"], in_maps, core_ids=list(range(NCORE)))
    ys = [np.asarray(r["y"]).reshape(3, SEQT, D) for r in res.results]
    y_prompt = np.concatenate([y[0] for y in ys], 0)[None]
    y_sample = np.stack([ys[c][1 + k] for c in range(NCORE) for k in range(2)], 0)
    return (y_prompt.astype(np.float32), y_sample.astype(np.float32))
```
